# Optimizing a Trainium2 kernel written in Bass

```python
import jax, jax.numpy as jnp
from jax import lax
import numpy as np

D_MODEL = 2048
BATCH = 4
SEQ = 2048
DEPTH = 2

CTX_LEN = 256
GRID_W = 64
EPS = 1e-6

POOL_WINDOWS = (2, 4, 8, 16)
POOL_W = D_MODEL // 2
POOL_GROUP = POOL_W // len(POOL_WINDOWS)
CONV_W = D_MODEL // 2
CONV_K = 3
EVEN_IN = 2 * POOL_W + 4 * CONV_W

HG_DK = 128
HG_HEADS = D_MODEL // HG_DK
HG_DV = D_MODEL // HG_HEADS
HG_K = HG_HEADS * HG_DK
HG_V = HG_HEADS * HG_DV
ODD_IN = 3 * HG_K + 2 * HG_V
CHUNK = 64

N_EVEN = (DEPTH + 1) // 2
N_ODD = DEPTH // 2

kernel_name = "hybrid_pool_conv_hgrn2_diffusion_block"


def rms_norm(x, g):
    xf = x.astype(jnp.float32)
    y = xf * lax.rsqrt(jnp.mean(xf * xf, axis=-1, keepdims=True) + EPS)
    return (y * g.astype(jnp.float32)).astype(x.dtype)


def box_mean(x, w, axis):
    n = x.shape[axis]
    cs = jnp.cumsum(x.astype(jnp.float32), axis=axis)
    pad = [(0, 0)] * x.ndim
    pad[axis] = (1, 0)
    cs = jnp.pad(cs, pad)
    t = jnp.arange(n)
    lo = jnp.maximum(t - w // 2, 0)
    hi = jnp.minimum(t + w // 2 - 1, n - 1)
    s = jnp.take(cs, hi + 1, axis=axis) - jnp.take(cs, lo, axis=axis)
    cnt = (hi - lo + 1).astype(jnp.float32).reshape((n,) + (1,) * (x.ndim - axis - 1))
    return (s / cnt).astype(x.dtype)


def multiscale_pool(v, pool_w, pool_scale, on_grid):
    b, n, _ = v.shape
    outs = []
    for gi, w in enumerate(POOL_WINDOWS):
        vg = v[..., gi * POOL_GROUP:(gi + 1) * POOL_GROUP]
        if on_grid:
            rows = n // GRID_W
            m = box_mean(box_mean(vg.reshape(b, rows, GRID_W, POOL_GROUP), w, 1), w, 2)
            m = m.reshape(b, n, POOL_GROUP)
        else:
            m = box_mean(vg, w, 1)
        outs.append(m - vg)
    pooled = jnp.stack(outs, axis=2)
    mixed = jnp.einsum('bngi,gio->bngo', pooled, pool_w).reshape(b, n, POOL_W)
    return mixed * pool_scale


def depthwise_conv3(u, w, bias):
    n = u.shape[1]
    up = jnp.pad(u, ((0, 0), (1, 1), (0, 0)))
    return up[:, :n] * w[0] + up[:, 1:n + 1] * w[1] + up[:, 2:] * w[2] + bias


def pool_conv_mixer(h, w_in, pool_w, pool_scale, conv_w, conv_b, w_out, on_grid):
    idx = [int(i) for i in np.cumsum([POOL_W, POOL_W, CONV_W, CONV_W, CONV_W])]
    a_v, a_g, b_x, b_b, b_c, b_g = jnp.split(h @ w_in, idx, axis=-1)
    a_out = multiscale_pool(a_v, pool_w, pool_scale, on_grid) * jax.nn.silu(a_g)
    b_out = b_b * depthwise_conv3(b_c * b_x, conv_w, conv_b) * jax.nn.silu(b_g)
    return jnp.concatenate([a_out, b_out], axis=-1) @ w_out


def _heads(t, d):
    return t.reshape(t.shape[0], t.shape[1], HG_HEADS, d)


def _chunks(t):
    b, n, hh, d = t.shape
    return t.reshape(b, n // CHUNK, CHUNK, hh, d)


def _rev(t, reverse):
    return jnp.flip(t, axis=1) if reverse else t


def hgrn2_gates(z, lb):
    zf = z.astype(jnp.float32)
    f = lb + (1.0 - lb) * jax.nn.sigmoid(zf)
    k = (1.0 - lb) * jax.nn.sigmoid(-zf)
    return _heads(k, HG_DK), _heads(jnp.log(f), HG_DK)


def chunk_state_scan(kc, vc, bc, s0, keep_starts):
    b_last = bc[:, :, -1]
    k_dec = kc * jnp.exp(b_last[:, :, None] - bc)
    ds = jnp.einsum('bnchk,bnchv->bnhkv', k_dec, vc)
    decay = jnp.exp(b_last)

    def step(s, inp):
        d, dsn = inp
        return d[..., None] * s + dsn, (s if keep_starts else None)

    s_fin, starts = lax.scan(step, s0, (jnp.moveaxis(decay, 1, 0), jnp.moveaxis(ds, 1, 0)))
    return s_fin, starts


def hgrn2_chunk_scan(q, k, v, logf, s0):
    qc, kc, vc = _chunks(q), _chunks(k), _chunks(v)
    bc = jnp.cumsum(_chunks(logf), axis=2)
    s_fin, starts = chunk_state_scan(kc, vc, bc, s0, True)
    starts = jnp.moveaxis(starts, 0, 1)
    q_dec = qc * jnp.exp(bc)
    k_inv = kc * jnp.exp(-bc)
    inter = jnp.einsum('bnchk,bnhkv->bnchv', q_dec, starts)
    scores = jnp.einsum('bnchk,bnshk->bnhcs', q_dec, k_inv)
    mask = jnp.tril(jnp.ones((CHUNK, CHUNK), dtype=bool))
    scores = jnp.where(mask, scores, 0.0)
    intra = jnp.einsum('bnhcs,bnshv->bnchv', scores, vc)
    o = (inter + intra).reshape(q.shape[0], q.shape[1], HG_HEADS, HG_DV)
    return o, s_fin


def hgrn2_final_state(k, v, logf, s0):
    bc = jnp.cumsum(_chunks(logf), axis=2)
    s_fin, _ = chunk_state_scan(_chunks(k), _chunks(v), bc, s0, False)
    return s_fin


def hgrn2_readout(o, g, onorm_g, w_out):
    b, n = o.shape[:2]
    o = o * lax.rsqrt(jnp.mean(o * o, axis=-1, keepdims=True) + EPS)
    o = o.reshape(b, n, HG_V) * onorm_g.astype(jnp.float32) * jax.nn.silu(g.astype(jnp.float32))
    return o.astype(g.dtype) @ w_out


def hgrn2_mixer(h_lat, h_ctx, w_in, onorm_g, w_out, lb_fwd, lb_bwd, ctx_out):
    idx = [int(i) for i in np.cumsum([HG_K, HG_K, HG_V, HG_K])]
    zf_l, zb_l, i_l, q_l, g_l = jnp.split(h_lat @ w_in, idx, axis=-1)
    if ctx_out:
        zf_c, zb_c, i_c, q_c, g_c = jnp.split(h_ctx @ w_in, idx, axis=-1)
        q_c = _heads(q_c.astype(jnp.float32), HG_DK)
    else:
        zf_c, zb_c, i_c = jnp.split(h_ctx @ w_in[:, :idx[2]], idx[:2], axis=-1)
    q_l = _heads(q_l.astype(jnp.float32), HG_DK)
    v_l = _heads(i_l.astype(jnp.float32), HG_DV)
    v_c = _heads(i_c.astype(jnp.float32), HG_DV)
    s0 = jnp.zeros((h_lat.shape[0], HG_HEADS, HG_DK, HG_DV), jnp.float32)
    outs_l, outs_c = [], []
    for lb, z_l, z_c, reverse in ((lb_fwd, zf_l, zf_c, False), (lb_bwd, zb_l, zb_c, True)):
        k_l, lf_l = hgrn2_gates(z_l, lb)
        k_c, lf_c = hgrn2_gates(z_c, lb)
        if ctx_out:
            o_c, s_ctx = hgrn2_chunk_scan(_rev(q_c, reverse), _rev(k_c, reverse),
                                          _rev(v_c, reverse), _rev(lf_c, reverse), s0)
            outs_c.append(_rev(o_c, reverse))
        else:
            s_ctx = hgrn2_final_state(_rev(k_c, reverse), _rev(v_c, reverse),
                                      _rev(lf_c, reverse), s0)
        o_l, _ = hgrn2_chunk_scan(_rev(q_l, reverse), _rev(k_l, reverse),
                                  _rev(v_l, reverse), _rev(lf_l, reverse), s_ctx)
        outs_l.append(_rev(o_l, reverse))
    y_l = hgrn2_readout(outs_l[0] + outs_l[1], g_l, onorm_g, w_out)
    y_c = hgrn2_readout(outs_c[0] + outs_c[1], g_c, onorm_g, w_out) if ctx_out else None
    return y_l, y_c


def setup_inputs(seed: int = 0) -> dict:
    key = jax.random.key(seed)
    ks = jax.random.split(key, 18)
    D = D_MODEL

    def nrm(k, shape, s):
        return jax.random.normal(k, shape, jnp.float32) * s

    return {
        "x": nrm(ks[0], (BATCH, SEQ, D), 1.0),
        "c": nrm(ks[1], (BATCH, D), 1.0),
        "ctx": nrm(ks[2], (BATCH, CTX_LEN, D), 1.0),
        "c_ctx": nrm(ks[3], (D,), 1.0),
        "ada_w": nrm(ks[4], (DEPTH, D, 3 * D), 0.5 * D ** -0.5),
        "ada_b": nrm(ks[5], (DEPTH, 3 * D), 0.01),
        "pre_g": 1.0 + nrm(ks[6], (DEPTH, D), 0.05),
        "post_g": 1.0 + nrm(ks[7], (DEPTH, D), 0.05),
        "ev_w_in": nrm(ks[8], (N_EVEN, D, EVEN_IN), D ** -0.5),
        "ev_pool_w": nrm(ks[9], (N_EVEN, len(POOL_WINDOWS), POOL_GROUP, POOL_GROUP), POOL_GROUP ** -0.5),
        "ev_pool_scale": 1.0 + nrm(ks[10], (N_EVEN, POOL_W), 0.1),
        "ev_conv_w": nrm(ks[11], (N_EVEN, CONV_K, CONV_W), CONV_K ** -0.5),
        "ev_conv_b": nrm(ks[12], (N_EVEN, CONV_W), 0.01),
        "ev_w_out": nrm(ks[13], (N_EVEN, POOL_W + CONV_W, D), (POOL_W + CONV_W) ** -0.5),
        "od_w_in": nrm(ks[14], (N_ODD, D, ODD_IN), D ** -0.5),
        "od_onorm_g": 1.0 + nrm(ks[15], (N_ODD, HG_V), 0.05),
        "od_w_out": nrm(ks[16], (N_ODD, HG_V, D), HG_V ** -0.5),
        "lb_logits": nrm(ks[17], (2, DEPTH + 1, HG_K), 0.1),
    }


def reference(x, c, ctx, c_ctx, ada_w, ada_b, pre_g, post_g, ev_w_in, ev_pool_w,
              ev_pool_scale, ev_conv_w, ev_conv_b, ev_w_out, od_w_in, od_onorm_g,
              od_w_out, lb_logits):
    lb_table = jnp.cumsum(jax.nn.softmax(lb_logits.astype(jnp.float32), axis=1), axis=1)
    s_lat = jax.nn.silu(c)
    s_ctx = jax.nn.silu(c_ctx)
    for layer in range(DEPTH):
        last = layer == DEPTH - 1
        shift, scale, gate = jnp.split((s_lat @ ada_w[layer] + ada_b[layer])[:, None, :], 3, axis=-1)
        h = rms_norm(x, pre_g[layer]) * (1.0 + scale) + shift
        need_ctx = (layer % 2 == 1) or not last
        if need_ctx:
            shift_c, scale_c, gate_c = jnp.split(s_ctx @ ada_w[layer] + ada_b[layer], 3)
            hc = rms_norm(ctx, pre_g[layer]) * (1.0 + scale_c) + shift_c
        if layer % 2 == 0:
            e = layer // 2
            y = pool_conv_mixer(h, ev_w_in[e], ev_pool_w[e], ev_pool_scale[e], ev_conv_w[e],
                                ev_conv_b[e], ev_w_out[e], True)
            if not last:
                yc = pool_conv_mixer(hc, ev_w_in[e], ev_pool_w[e], ev_pool_scale[e], ev_conv_w[e],
                                     ev_conv_b[e], ev_w_out[e], False)
        else:
            o = layer // 2
            y, yc = hgrn2_mixer(h, hc, od_w_in[o], od_onorm_g[o], od_w_out[o],
                                lb_table[0, layer], lb_table[1, layer], not last)
        x = x + gate * rms_norm(y, post_g[layer])
        if not last:
            ctx = ctx + gate_c * rms_norm(yc, post_g[layer])
    return x
```

```python
import numpy as np
from contextlib import ExitStack
import concourse.bass as bass
import concourse.mybir as mybir
from concourse.bass_utils import run_bass_kernel_spmd

F32 = mybir.dt.float32
BF16 = mybir.dt.bfloat16
AF = mybir.ActivationFunctionType
ALU = mybir.AluOpType
AX = mybir.AxisListType

P = 128
D = 2048
KC = 16
NOWN = 1024
NHALO = 512
NCTX = 256
EPS = 1e-6
WINS = (2, 4, 8, 16)
RAD = (1, 1, 2, 4)
NPMB = 21
SB_BASE = 16640
SB_END = 229376 - 64


DBG = {"nheads": 16, "hstage": 9}


class Buf:
    __slots__ = ("name", "w", "r", "dsem", "dcnt", "ps")

    def __init__(self, name, ps=False):
        self.name = name
        self.ps = ps
        self.w = None
        self.r = {}
        self.dsem = None
        self.dcnt = 0


class Sched:
    def __init__(self, nc, es):
        self.nc = nc
        self.es = es
        self.eng = {"pe": nc.tensor, "act": nc.scalar, "dve": nc.vector, "pool": nc.gpsimd, "sp": nc.sync}
        self.sem = {e: es.enter_context(nc.semaphore("s_" + e)) for e in self.eng}
        self.cnt = {e: 0 for e in self.eng}
        self.seen = {e: {} for e in self.eng}
        self.nsem = 0

    def _deps(self, reads, writes):
        deps = {}

        def add(t):
            if t is None:
                return
            k = id(t[0])
            if k not in deps or deps[k][1] < t[1]:
                deps[k] = t

        for b in reads:
            add(b.w)
            if b.ps:
                for t in b.r.values():
                    add(t)
        for b in writes:
            add(b.w)
            for t in b.r.values():
                add(t)
        return deps

    def _need(self, e, deps):
        for k, (sem, val) in deps.items():
            if self.seen[e].get(k, 0) >= val:
                continue
            self.eng[e].wait_ge(sem, val)
            self.seen[e][k] = val

    def _mark(self, t, reads, writes):
        k = id(t[0])
        for b in reads:
            if k not in b.r or b.r[k][1] < t[1]:
                b.r[k] = t
        for b in writes:
            b.w = t
            b.r = {}

    def op(self, e, fn, reads=(), writes=()):
        deps = self._deps(reads, writes)
        if e == "pe":
            deps.pop(id(self.sem["pe"]), None)
        self._need(e, deps)
        ins = fn(self.eng[e])
        self.cnt[e] += 1
        ins.then_inc(self.sem[e], 1)
        t = (self.sem[e], self.cnt[e])
        self._mark(t, reads, writes)
        return t

    def dma(self, q, out, in_, dbuf, reads=(), writes=()):
        if dbuf.dsem is None:
            dbuf.dsem = self.es.enter_context(self.nc.semaphore("d%d" % self.nsem))
            self.nsem += 1
        self._need(q, self._deps(reads, writes))
        ins = self.eng[q].dma_start(out=out, in_=in_)
        dbuf.dcnt += 16
        ins.then_inc(dbuf.dsem, 16)
        t = (dbuf.dsem, dbuf.dcnt)
        self._mark(t, reads, writes)
        return t

    def barrier(self):
        cur = {id(self.sem[e]): (self.sem[e], self.cnt[e]) for e in self.eng if self.cnt[e] > 0}
        for e in self.eng:
            d = dict(cur)
            d.pop(id(self.sem[e]), None)
            self._need(e, d)

    def wait_buf(self, e, b):
        self._need(e, self._deps([], [b]))


class Arena:
    def __init__(self, nc):
        self.nc = nc
        self.n = 0

    def at(self, off, shape, dt, name=None):
        nb = int(np.prod(shape[1:])) * (2 if dt == BF16 else 4)
        assert off % 32 == 0, off
        assert SB_BASE <= off and off + nb <= SB_END, (name, off, nb)
        self.n += 1
        return self.nc.alloc_sbuf_tensor_at("%s_%d" % (name or "t", self.n), list(shape), dt, offset=off)


def build_program(mode, stop=None, plan=None):
    nc = bass.Bass("TRN2", target_bir_lowering=False)
    full = mode in ("B", "F")

    def din(name, shape, dt=F32):
        return nc.dram_tensor(name, list(shape), dt, kind="ExternalInput").ap()

    x_d = din("x_loc", [NOWN + NHALO, D])
    ctx_d = din("ctx_loc", [NCTX, D])
    cs2_d = din("cs2", [P, KC, 2])
    cbc_d = din("cbc", [2, P, KC, P])
    adaw_d = din("adaw", [2, 12, D, 512])
    adabT_d = din("adabT", [2, P, 64])
    adabg_d = din("adabg", [2, P, D])
    postg_d = din("postg", [2, P, D])
    pregT_d = din("pregT", [2, P, KC])
    w0_d = din("w0blk", [12, D, 512])
    wo0_d = din("wo0", [D, D])
    w1_d = din("w1blk", [16, D, 640])
    wo1_d = din("wo1", [D, D])
    poolw_d = din("poolw", [4, 256, 256])
    pm_d = din("pm", [4, NPMB, P, P])
    invc_d = din("invc", [4, P, NOWN + NCTX])
    pscT_d = din("pscT", [P, 8])
    cwT_d = din("cwT", [P, 8, 3])
    cbT_d = din("cbT", [P, 8])
    lbl_d = din("lbl", [P, 2, 3, 16])
    ongT_d = din("ongT", [P, KC])
    idf_d = din("idf", [P, P])
    m1_d = din("mask1", [P, P])
    m2_d = din("mask2", [P, P])
    if mode == "B":
        sin_d = din("sin", [16, P, P])
    if mode == "F":
        psel_d = din("psel", [P, 2])
        agin = [nc.dram_tensor("agin%d" % i, [P, P], F32) for i in range(16)]
        agout = [nc.dram_tensor("agout%d" % i, [2 * P, P], F32) for i in range(16)]
    if full:
        out_d = nc.dram_tensor("out", [NOWN, D], F32, kind="ExternalOutput").ap()
    else:
        sout_d = nc.dram_tensor("sout", [16, P, P], F32, kind="ExternalOutput").ap()
    x1_d = nc.dram_tensor("x1_scr", [NOWN, D], F32, **({"kind": "ExternalOutput"} if stop else {})).ap()
    dbg = {}
    if stop:
        dbg["hT"] = nc.dram_tensor("dbg_hT", [P, KC, 1792], BF16, kind="ExternalOutput").ap()
        dbg["midT"] = nc.dram_tensor("dbg_midT", [P, KC, 1280], BF16, kind="ExternalOutput").ap()
        dbg["h1T"] = nc.dram_tensor("dbg_h1T", [P, KC, 1280], BF16, kind="ExternalOutput").ap()
        dbg["ogT"] = nc.dram_tensor("dbg_ogT", [P, KC, 1024], BF16, kind="ExternalOutput").ap()
        dbg["modv"] = nc.dram_tensor("dbg_modv", [8, P, KC], F32, kind="ExternalOutput").ap()
        dbg["lbv"] = nc.dram_tensor("dbg_lbv", [P, 16, 4], F32, kind="ExternalOutput").ap()
    DBGB = Buf("dbg")

    def finish():
        S.barrier()
        for b_ in (OUTB, X1B, DBGB):
            S.wait_buf("sp", b_)
        try:
            S._need("sp", dict(FIN_TICKS))
        except NameError:
            pass
        es.close()
        nc._ring_plan = ring.rec
        return nc


    es = ExitStack()
    S = Sched(nc, es)
    A = Arena(nc)
    KB = 1024

    pst = es.enter_context(nc.psum_tensor("pst", [P, 8, 512], F32))
    psflat = pst[:].rearrange("p a b -> p (a b)")
    psbf = psflat.bitcast(BF16)
    PB = [Buf("ps%d" % i, ps=True) for i in range(8)]

    def bank(i, n=512, off=0):
        return psflat[:, i * 512 + off: i * 512 + off + n]

    R_RING = SB_BASE
    R_H = R_RING + 64 * KB
    R_M = R_H + 56 * KB
    R_C = R_M + 40 * KB
    o = R_C

    def calloc(shape, dt, name):
        nonlocal o
        t = A.at(o, shape, dt, name)
        nb = int(np.prod(shape[1:])) * (2 if dt == BF16 else 4)
        o += (nb + 31) // 32 * 32
        return t

    idf = calloc([P, P], F32, "idf")
    idb = calloc([P, P], BF16, "idb")
    mk1 = calloc([P, P], BF16, "mk1")
    mk2 = calloc([P, P], BF16, "mk2")
    rmask = calloc([P, 1280], F32, "rmask")
    s2b = calloc([P, KC, 2], BF16, "s2b")
    lbv = calloc([P, 16, 4], F32, "lbv")
    nlbv = calloc([P, 16, 2], F32, "nlbv")
    ongT = calloc([P, KC], F32, "ongT")
    pscT = calloc([P, 8], F32, "pscT")
    cwT = calloc([P, 8, 3], F32, "cwT")
    cbT = calloc([P, 8], F32, "cbT")
    modv = [[calloc([P, KC], F32, "modv") for _ in range(4)] for _ in range(2)]
    poolw = calloc([P, 4, 2, 256], BF16, "poolw")
    epsc = calloc([P, 1], F32, "epsc")
    psel = calloc([P, 2], F32, "psel")
    R_W = (o + 63) // 64 * 64
    CONST = Buf("const")
    assert R_W - R_C <= 20 * KB, R_W - R_C

    hT = A.at(R_H, [P, KC, 1792], BF16, "hT")
    h1T = A.at(R_H, [P, KC, 1280], BF16, "h1T")
    midT = A.at(R_M, [P, KC, 1280], BF16, "midT")
    ogT = A.at(R_M, [P, KC, 1024], BF16, "ogT")
    HT = Buf("hT")
    MT = Buf("midT")

    def wsrc(key):
        if key[0] == "adaw":
            return adaw_d[key[1], key[2]]
        if key[0] == "w0":
            return w0_d[key[1]]
        if key[0] == "w1":
            return w1_d[key[1]]
        wo_ = wo0_d if key[1] == 0 else wo1_d
        return wo_[:, key[2] * 512:(key[2] + 1) * 512]

    class Ring:
        def __init__(self):
            self.slots = []
            self.i = 0
            self.epoch = 0
            self.n = 0
            self.issued = 0
            self.rec = []
            self.inflight = {}

        def config(self, n, cols):
            nb = KC * cols * 2
            assert n * nb <= 64 * KB
            self.slots = [(A.at(R_RING + k * nb, [P, KC, cols], BF16, "ring"), Buf("ring%d" % k)) for k in range(n)]
            self.i = 0
            self.epoch += 1

        def _issue(self, src2d, cols):
            t, b = self.slots[self.i % len(self.slots)]
            self.i += 1
            v = src2d.rearrange("(kc p) n -> p kc n", p=P)
            for q in range(4):
                S.dma("pool", t[:, q * 4:(q + 1) * 4, 0:cols], v[:, q * 4:(q + 1) * 4, :], b, writes=[b])
            return t, b

        def load(self, key, cols, hold=1):
            src2d = wsrc(key)
            self.rec.append((key, cols, self.epoch, hold))
            n = self.n
            self.n += 1
            if plan is None:
                return self._issue(src2d, cols)
            if n >= self.issued:
                self.inflight[n] = self._issue(src2d, cols)
                self.issued = n + 1
            res = self.inflight.pop(n)
            depth = len(self.slots) - max(hold, 1)
            while self.issued < len(plan) and self.issued <= n + depth and plan[self.issued][2] == self.epoch:
                key_n, cols_n, _, _ = plan[self.issued]
                self.inflight[self.issued] = self._issue(wsrc(key_n), cols_n)
                self.issued += 1
            return res

    ring = Ring()

    def cload(dst, src, q="pool"):
        S.dma(q, dst, src, CONST, writes=[CONST])

    cload(idf[:], idf_d)
    cload(idb[:], idf_d, "pool")
    cload(mk1[:], m1_d, "pool")
    cload(mk2[:], m2_d, "pool")
    cload(ongT[:], ongT_d)
    cload(pscT[:], pscT_d)
    cload(cwT[:], cwT_d)
    cload(cbT[:], cbT_d)
    if mode == "F":
        cload(psel[:], psel_d)
    cload(poolw[:], poolw_d.rearrange("g (c p) o -> p g c o", p=P), "pool")
    S.op("dve", lambda e: e.memset(rmask[:], 1.0), writes=[CONST])
    S.op("dve", lambda e: e.memset(rmask[:].rearrange("p (a b) -> p a b", b=P)[:, :, 0:1], 0.0), writes=[CONST])
    S.op("dve", lambda e: e.memset(epsc[:], EPS), writes=[CONST])

    if mode == "F" and DBG.get("cc_early"):
        tt = A.at(R_M, [P, P], F32, "cctest")
        TTB = Buf("cctest")
        S.dma("sp", tt[:], idf_d, TTB, writes=[TTB])
        AG0 = Buf("ag0")
        S.dma("sp", agin[15][:, :], tt[:], TTB, reads=[TTB], writes=[AG0])
        S._need("pool", S._deps([AG0], []))
        cc0 = es.enter_context(nc.semaphore("cc_early"))
        nc.gpsimd.collective_compute("AllGather", ALU.bypass, replica_groups=[[2 * i_, 2 * i_ + 1] for i_ in range(DBG.get("ncores", 8) // 2)],
                                     ins=[agin[15].ap().opt()], outs=[agout[15].ap().opt()]).then_inc(cc0, 1)
        nc.sync.wait_ge(cc0, 1)
    def wtile(off, shape, dt, name):
        return A.at(R_W + off, shape, dt, name)

    W_LIMIT = SB_END - R_W

    cs2f = wtile(0, [P, KC, 2], F32, "cs2f")
    WK = Buf("wk")
    S.dma("sp", cs2f[:], cs2_d, WK, writes=[WK])
    S.op("act", lambda e: e.activation(out=s2b[:], in_=cs2f[:], func=AF.Silu), reads=[WK], writes=[CONST])

    lbl = wtile(256, [P, 2, 3, 16], F32, "lbl")
    lbe = wtile(1024, [P, 2, 3, 16], F32, "lbe")
    lbs = wtile(2048, [P, 2, 16], F32, "lbs")
    lbn = wtile(2560, [P, 2, 16], F32, "lbn")
    WK2 = Buf("wk2")
    S.dma("sp", lbl[:], lbl_d, WK2, writes=[WK2])
    S.op("act", lambda e: e.activation(out=lbe[:], in_=lbl[:], func=AF.Exp), reads=[WK2], writes=[WK2])
    S.op("dve", lambda e: e.tensor_tensor(out=lbn[:], in0=lbe[:, :, 0, :], in1=lbe[:, :, 1, :], op=ALU.add), reads=[WK2], writes=[WK2])
    S.op("dve", lambda e: e.tensor_tensor(out=lbs[:], in0=lbn[:], in1=lbe[:, :, 2, :], op=ALU.add), reads=[WK2], writes=[WK2])
    S.op("dve", lambda e: e.reciprocal(out=lbs[:], in_=lbs[:]), reads=[WK2], writes=[WK2])
    for d_ in range(2):
        S.op("dve", lambda e, d_=d_: e.tensor_tensor(out=lbv[:, :, 2 * d_ + 1], in0=lbn[:, d_, :], in1=lbs[:, d_, :], op=ALU.mult), reads=[WK2], writes=[CONST])
        S.op("dve", lambda e, d_=d_: e.tensor_tensor(out=lbv[:, :, 2 * d_], in0=lbe[:, d_, 2, :], in1=lbs[:, d_, :], op=ALU.mult), reads=[WK2], writes=[CONST])
        S.op("dve", lambda e, d_=d_: e.tensor_scalar(out=nlbv[:, :, d_], in0=lbv[:, :, 2 * d_], scalar1=-1.0, scalar2=None, op0=ALU.mult), reads=[CONST], writes=[CONST])

    ring.config(4, 512)
    WADA = Buf("wada")

    def ada_part1(layer):
        pb = PB[7]
        adab = wtile(30 * KB, [P, 64], F32, "adab")
        pg = wtile(30 * KB + 512, [P, KC], F32, "pg")
        mt = wtile(30 * KB + 1024, [P, 32, 2], F32, "mt")
        WA = WADA
        S.dma("sp", adab[:], adabT_d[layer], WA, writes=[WA])
        S.dma("sp", pg[:], pregT_d[layer], WA, writes=[WA])
        for j in range(8):
            wt, wb = ring.load(("adaw", layer, j), 512)

            def f(e, j=j, wt=wt):
                ins = None
                for fb in range(4):
                    for kc in range(KC):
                        ins = e.matmul(bank(7, 2, (j * 4 + fb) * 2), lhsT=wt[:, kc, fb * P:(fb + 1) * P], rhs=s2b[:, kc, :],
                                       start=(kc == 0), stop=(kc == KC - 1))
                return ins
            S.op("pe", f, reads=[wb, CONST], writes=[pb])
        S.op("dve", lambda e: e.tensor_tensor(out=mt[:].rearrange("p a b -> p (a b)"), in0=bank(7, 64), in1=adab[:], op=ALU.add),
             reads=[pb, WA], writes=[WA])
        mv = modv[layer]
        for v in range(2):
            S.op("dve", lambda e, v=v: e.tensor_copy(out=mv[2 * v][:], in_=mt[:, 0:16, v]), reads=[WA], writes=[CONST])
            S.op("dve", lambda e, v=v: e.scalar_tensor_tensor(out=mv[2 * v + 1][:], in0=mt[:, 16:32, v], scalar=1.0, in1=pg[:],
                                                           op0=ALU.add, op1=ALU.mult), reads=[WA], writes=[CONST])

    def ada_gate2(layer, variants):
        WG = Buf("wg")
        cbf = wtile(24 * KB, [P, KC, P], F32, "cbf")
        pgt = wtile(16 * KB, [P, D], F32, "pgt")
        bp = wtile(24 * KB, [P, D], F32, "bp")
        cbbs = []
        for vi, (which, gp_, gpb_) in enumerate(variants):
            cbb = A.at(R_H + 48 * KB + vi * 4 * KB, [P, KC, P], BF16, "cbb")
            S.dma("sp", cbf[:], cbc_d[which], WG, writes=[WG])
            S.op("act", lambda e, cbb=cbb: e.activation(out=cbb[:], in_=cbf[:], func=AF.Silu), reads=[WG], writes=[WG])
            cbbs.append(cbb)
        S.dma("sp", pgt[:], postg_d[layer], WG, writes=[WG])
        S.dma("sp", bp[:], adabg_d[layer], WG, writes=[WG])
        S.op("dve", lambda e: e.tensor_tensor(out=bp[:], in0=bp[:], in1=pgt[:], op=ALU.mult), reads=[WG], writes=[WG])
        for j in range(4):
            wt, wb = ring.load(("adaw", layer, 8 + j), 512)
            for vi, (which, gp_, gpb_) in enumerate(variants):
                pbi = 2 * vi + (j % 2)

                def f(e, wt=wt, pbi=pbi, cbb=cbbs[vi]):
                    ins = None
                    for kc in range(KC):
                        ins = e.matmul(bank(pbi), lhsT=cbb[:, kc, :], rhs=wt[:, kc, :], start=(kc == 0), stop=(kc == KC - 1))
                    return ins
                S.op("pe", f, reads=[wb, WG], writes=[PB[pbi]])
                S.op("dve", lambda e, j=j, pbi=pbi, gp_=gp_: e.tensor_tensor(out=gp_[:, j * 512:(j + 1) * 512], in0=bank(pbi), in1=pgt[:, j * 512:(j + 1) * 512], op=ALU.mult),
                     reads=[PB[pbi], WG], writes=[gpb_])
                S.op("dve", lambda e, j=j, gp_=gp_: e.tensor_tensor(out=gp_[:, j * 512:(j + 1) * 512], in0=gp_[:, j * 512:(j + 1) * 512], in1=bp[:, j * 512:(j + 1) * 512], op=ALU.add),
                     reads=[WG, gpb_], writes=[gpb_])

    def norm_transpose(xt, xb, junk, jb, st, stb, dstT, dcol, dbuf, gsv, shv, pbanks):
        S.op("act", lambda e: e.activation(out=junk[:], in_=xt[:], func=AF.Square, accum_out=st[:, 0:1]), reads=[xb], writes=[jb, stb])
        S.op("dve", lambda e: e.tensor_scalar(out=st[:, 1:2], in0=st[:, 0:1], scalar1=1.0 / D, scalar2=EPS, op0=ALU.mult, op1=ALU.add), reads=[stb], writes=[stb])
        S.op("act", lambda e: e.activation(out=st[:, 2:3], in_=st[:, 1:2], func=AF.Ln), reads=[stb], writes=[stb])
        S.op("act", lambda e: e.activation(out=st[:, 3:4], in_=st[:, 2:3], func=AF.Exp, scale=-0.5), reads=[stb], writes=[stb])
        S.op("dve", lambda e: e.tensor_scalar(out=xt[:], in0=xt[:], scalar1=st[:, 3:4], scalar2=None, op0=ALU.mult), reads=[stb, xb], writes=[xb])
        for g in range(4):
            pb = PB[pbanks[g]]

            def f(e, g=g):
                ins = None
                for q in range(4):
                    kc = g * 4 + q
                    ins = e.transpose(out=bank(pbanks[g], P, q * P), in_=xt[:, kc * P:(kc + 1) * P], identity=idf[:])
                return ins
            S.op("pe", f, reads=[xb, CONST], writes=[pb])
            for q in range(4):
                kc = g * 4 + q
                S.op("act", lambda e, g=g, q=q, kc=kc: e.activation(out=dstT[:, kc, dcol:dcol + P], in_=bank(pbanks[g], P, q * P), func=AF.Identity,
                                                                 scale=gsv[:, kc:kc + 1], bias=shv[:, kc:kc + 1]),
                     reads=[pb, CONST], writes=[dbuf])

    ada_part1(0)
    XO = R_M - R_W
    xts = [wtile(XO + i * 8 * KB, [P, D], F32, "xt") for i in range(2)]
    xbs = [Buf("xt0"), Buf("xt1")]
    junk = wtile(XO + 16 * KB, [P, D], BF16, "junk")
    JB = Buf("junk")
    stt = [wtile(XO + 20 * KB + i * 64, [P, 4], F32, "st") for i in range(2)]
    stb = [Buf("st0"), Buf("st1")]
    for t in range(14):
        i = t % 2
        src = x_d[t * P:(t + 1) * P, :] if t < 12 else ctx_d[(t - 12) * P:(t - 11) * P, :]
        S.dma("sp", xts[i][:], src, xbs[i], writes=[xbs[i]])
        mv = modv[0]
        gsv, shv = (mv[1], mv[0]) if t < 12 else (mv[3], mv[2])
        norm_transpose(xts[i], xbs[i], junk, JB, stt[i], stb[i], hT, t * P, HT, gsv, shv, [0, 1, 2, 3] if i == 0 else [4, 5, 6, 3])
    S.barrier()
    X1B = Buf("x1")
    OUTB = Buf("out")
    if stop == "pro":
        S.dma("sp", dbg["hT"], hT[:], DBGB, reads=[HT], writes=[DBGB])
        for l_ in range(2):
            for v_ in range(4):
                S.dma("sp", dbg["modv"][l_ * 4 + v_], modv[l_][v_][:], DBGB, reads=[CONST], writes=[DBGB])
        S.dma("sp", dbg["lbv"], lbv[:], DBGB, reads=[CONST], writes=[DBGB])
        return finish()

    TBLK = [(0, 512, 0), (512, 512, 512), (1536, 256, 1024)]
    av = wtile(0, [P, 14, 256], BF16, "av")
    AVB = Buf("av")
    pooled = wtile(7 * KB, [P, 2, 1280], BF16, "pooled")
    PLB = Buf("pooled")
    pmt = wtile(12 * KB, [P, NPMB, P], BF16, "pmt")
    PMB = Buf("pm")
    invc = wtile(12 * KB + NPMB * 256, [P, 1280], F32, "invc")
    IVB = Buf("invc")
    sag_o = 12 * KB + NPMB * 256 + 5 * KB
    sag = [wtile(sag_o + i * 2 * KB, [P, 512], F32, "sag") for i in range(2)]
    SGB = [Buf("sag0"), Buf("sag1")]
    assert sag_o + 4 * KB <= W_LIMIT, (sag_o, W_LIMIT)
    for gi in range(4):
        r = RAD[gi]
        wt, wb = ring.load(("w0", gi), 512)
        for q3 in range(3):
            S.dma("pool", pmt[:, q3 * 7:(q3 + 1) * 7, :], pm_d[gi, q3 * 7:(q3 + 1) * 7].rearrange("b p q -> p b q"), PMB, writes=[PMB])
        S.dma("sp", invc[:], invc_d[gi], IVB, writes=[IVB])
        tiles = list(range(8 + r)) + [12, 13]
        for n_, t in enumerate(tiles):
            pbi = (n_ // 2) % 4
            half = n_ % 2

            def f(e, t=t, pbi=pbi, half=half, wt=wt):
                ins = None
                for kc in range(KC):
                    ins = e.matmul(bank(pbi, 256, half * 256), lhsT=hT[:, kc, t * P:(t + 1) * P], rhs=wt[:, kc, 0:256],
                                   start=(kc == 0), stop=(kc == KC - 1))
                return ins
            S.op("pe", f, reads=[wb, HT], writes=[PB[pbi]])
            eng = "act" if n_ % 2 == 0 else "dve"
            if eng == "act":
                S.op("act", lambda e, t=t, pbi=pbi, half=half: e.activation(out=av[:, t, :], in_=bank(pbi, 256, half * 256), func=AF.Copy),
                     reads=[PB[pbi]], writes=[AVB])
            else:
                S.op("dve", lambda e, t=t, pbi=pbi, half=half: e.tensor_copy(out=av[:, t, :], in_=bank(pbi, 256, half * 256)),
                     reads=[PB[pbi]], writes=[AVB])
        for ch in range(2):
            for jg in range(2):
                pbi = 4 + (ch * 2 + jg) % 2

                def f(e, ch=ch, jg=jg, pbi=pbi):
                    ins = None
                    for jj in range(4):
                        j = jg * 4 + jj
                        rels = [rel for rel in range(-r, r + 1) if 0 <= j + rel]
                        for n2, rel in enumerate(rels):
                            blk = (9 + j) if rel == 0 else (rel + 4)
                            ins = e.matmul(bank(pbi, P, jj * P), lhsT=av[:, j + rel, ch * P:(ch + 1) * P], rhs=pmt[:, blk, :],
                                           start=(n2 == 0), stop=(n2 == len(rels) - 1))
                    return ins
                S.op("pe", f, reads=[AVB, PMB], writes=[PB[pbi]])
                S.op("dve", lambda e, ch=ch, jg=jg, pbi=pbi: e.tensor_tensor(out=pooled[:, ch, jg * 512:(jg + 1) * 512], in0=bank(pbi),
                                                                          in1=invc[:, jg * 512:(jg + 1) * 512], op=ALU.mult),
                     reads=[PB[pbi], IVB], writes=[PLB])

            def fc(e, ch=ch):
                ins = None
                for jc in range(2):
                    for ic in range(2):
                        ins = e.matmul(bank(6, P, jc * P), lhsT=av[:, 12 + ic, ch * P:(ch + 1) * P], rhs=pmt[:, 17 + jc * 2 + ic, :],
                                       start=(ic == 0), stop=(ic == 1))
                return ins
            S.op("pe", fc, reads=[AVB, PMB], writes=[PB[6]])
            S.op("dve", lambda e, ch=ch: e.tensor_tensor(out=pooled[:, ch, 1024:1280], in0=bank(6, 256), in1=invc[:, 1024:1280], op=ALU.mult),
                 reads=[PB[6], IVB], writes=[PLB])
        n3 = 0
        for oc in range(2):
            for (hc0, n, mc0) in TBLK:
                pm_, pg_ = (0, 1) if n3 % 2 == 0 else (2, 3)
                si = n3 % 2
                n3 += 1

                def fm(e, oc=oc, mc0=mc0, n=n, pm_=pm_):
                    ins = None
                    for ic in range(2):
                        ins = e.matmul(bank(pm_, n), lhsT=poolw[:, gi, ic, oc * P:(oc + 1) * P], rhs=pooled[:, ic, mc0:mc0 + n],
                                       start=(ic == 0), stop=(ic == 1))
                    return ins
                S.op("pe", fm, reads=[PLB, CONST], writes=[PB[pm_]])

                def fg(e, oc=oc, hc0=hc0, n=n, pg_=pg_, wt=wt):
                    ins = None
                    for kc in range(KC):
                        ins = e.matmul(bank(pg_, n), lhsT=wt[:, kc, 256 + oc * P:256 + (oc + 1) * P], rhs=hT[:, kc, hc0:hc0 + n],
                                       start=(kc == 0), stop=(kc == KC - 1))
                    return ins
                S.op("pe", fg, reads=[wb, HT], writes=[PB[pg_]])
                S.op("act", lambda e, n=n, pg_=pg_, si=si: e.activation(out=sag[si][:, 0:n], in_=bank(pg_, n), func=AF.Silu),
                     reads=[PB[pg_]], writes=[SGB[si]])
                S.op("dve", lambda e, oc=oc, mc0=mc0, n=n, pm_=pm_, si=si: e.scalar_tensor_tensor(
                    out=midT[:, gi * 2 + oc, mc0:mc0 + n], in0=bank(pm_, n), scalar=pscT[:, gi * 2 + oc:gi * 2 + oc + 1], in1=sag[si][:, 0:n],
                    op0=ALU.mult, op1=ALU.mult), reads=[PB[pm_], SGB[si], CONST], writes=[MT])
    S.barrier()
    ada_part1(1)

    ub = wtile(0, [P, 1026], F32, "ub")
    ubc = wtile(4128, [P, 258], F32, "ubc")
    UB = Buf("ub")
    bx = [wtile(6 * KB + i * 2 * KB, [P, 512], F32, "bx") for i in range(2)]
    BXB = [Buf("bx0"), Buf("bx1")]
    cv = [wtile(10 * KB + i * 2 * KB, [P, 512], F32, "cv") for i in range(2)]
    CVB = [Buf("cv0"), Buf("cv1")]
    sg0 = [wtile(14 * KB + i * 2 * KB, [P, 512], F32, "sg") for i in range(2)]
    SG0 = [Buf("sg0"), Buf("sg1")]
    S.op("dve", lambda e: e.memset(ub[:, 0:1], 0.0), writes=[UB])
    S.op("dve", lambda e: e.memset(ubc[:, 0:1], 0.0), writes=[UB])
    S.op("dve", lambda e: e.memset(ubc[:, 257:258], 0.0), writes=[UB])
    UBLK = [(0, 512, 1, ub), (512, 512, 513, ub), (1024, 1, 1025, ub), (1536, 256, 1, ubc)]
    for cb in range(8):
        wt, wb = ring.load(("w0", 4 + cb), 512)
        for n_, (hc0, n, uc0, ut) in enumerate(UBLK):
            px, pc = (0, 1) if n_ % 2 == 0 else (2, 3)
            si = n_ % 2

            def f(e, hc0=hc0, n=n, px=px, pc=pc, wt=wt):
                ins = None
                for (pbk, c0) in ((px, 0), (pc, 256)):
                    for kc in range(KC):
                        ins = e.matmul(bank(pbk, n), lhsT=wt[:, kc, c0:c0 + P], rhs=hT[:, kc, hc0:hc0 + n], start=(kc == 0), stop=(kc == KC - 1))
                return ins
            S.op("pe", f, reads=[wb, HT], writes=[PB[px], PB[pc]])
            S.op("act", lambda e, n=n, px=px, si=si: e.activation(out=bx[si][:, 0:n], in_=bank(px, n), func=AF.Copy), reads=[PB[px]], writes=[BXB[si]])
            S.op("dve", lambda e, n=n, pc=pc, si=si, uc0=uc0, ut=ut: e.tensor_tensor(out=ut[:, uc0:uc0 + n], in0=bank(pc, n), in1=bx[si][:, 0:n], op=ALU.mult),
                 reads=[PB[pc], BXB[si]], writes=[UB])
        for n_, (hc0, n, mc0) in enumerate(TBLK):
            pbb, pgg = (4, 5) if n_ % 2 == 0 else (6, 7)
            si = n_ % 2
            ut, u0 = (ub, hc0) if n_ < 2 else (ubc, 0)

            def f(e, hc0=hc0, n=n, pbb=pbb, pgg=pgg, wt=wt):
                ins = None
                for (pbk, c0) in ((pbb, 128), (pgg, 384)):
                    for kc in range(KC):
                        ins = e.matmul(bank(pbk, n), lhsT=wt[:, kc, c0:c0 + P], rhs=hT[:, kc, hc0:hc0 + n], start=(kc == 0), stop=(kc == KC - 1))
                return ins
            S.op("pe", f, reads=[wb, HT], writes=[PB[pbb], PB[pgg]])
            S.op("act", lambda e, n=n, pgg=pgg, si=si: e.activation(out=sg0[si][:, 0:n], in_=bank(pgg, n), func=AF.Silu), reads=[PB[pgg]], writes=[SG0[si]])
            S.op("dve", lambda e, n=n, si=si, ut=ut, u0=u0: e.tensor_scalar(out=cv[si][:, 0:n], in0=ut[:, u0 + 1:u0 + 1 + n], scalar1=cwT[:, cb, 1:2], scalar2=cbT[:, cb:cb + 1],
                                                                      op0=ALU.mult, op1=ALU.add), reads=[UB, CONST], writes=[CVB[si]])
            S.op("dve", lambda e, n=n, si=si, ut=ut, u0=u0: e.scalar_tensor_tensor(out=cv[si][:, 0:n], in0=ut[:, u0:u0 + n], scalar=cwT[:, cb, 0:1], in1=cv[si][:, 0:n],
                                                                             op0=ALU.mult, op1=ALU.add), reads=[UB, CONST, CVB[si]], writes=[CVB[si]])
            S.op("dve", lambda e, n=n, si=si, ut=ut, u0=u0: e.scalar_tensor_tensor(out=cv[si][:, 0:n], in0=ut[:, u0 + 2:u0 + 2 + n], scalar=cwT[:, cb, 2:3], in1=cv[si][:, 0:n],
                                                                             op0=ALU.mult, op1=ALU.add), reads=[UB, CONST, CVB[si]], writes=[CVB[si]])
            S.op("dve", lambda e, n=n, si=si, pbb=pbb: e.tensor_tensor(out=cv[si][:, 0:n], in0=bank(pbb, n), in1=cv[si][:, 0:n], op=ALU.mult),
                 reads=[PB[pbb], CVB[si]], writes=[CVB[si]])
            S.op("dve", lambda e, n=n, si=si, mc0=mc0: e.tensor_tensor(out=midT[:, 8 + cb, mc0:mc0 + n], in0=cv[si][:, 0:n], in1=sg0[si][:, 0:n], op=ALU.mult),
                 reads=[CVB[si], SG0[si]], writes=[MT])
    S.barrier()

    gp = wtile(0, [P, D], F32, "gp")
    GPB = Buf("gp")
    xt = wtile(8 * KB, [P, D], F32, "xt")
    XB = Buf("xt")
    xt2 = wtile(24 * KB, [P, D], F32, "xt2")
    XB2 = Buf("xt2")
    tmp = wtile(16 * KB, [P, D], F32, "tmp")
    TB = Buf("tmp")
    junk2 = wtile(16 * KB, [P, D], BF16, "junk2")
    JB2 = Buf("junk2")
    st2 = wtile(32 * KB, [P, 8], F32, "st2")
    SB2 = Buf("st2")
    st3 = wtile(32 * KB + 64, [P, 8], F32, "st3")
    SB3 = Buf("st3")
    XTS = [(xt, XB, st2, SB2), (xt2, XB2, st3, SB3)]
    FIN_TICKS = {}
    assert 32 * KB + 128 <= W_LIMIT

    def epilogue_tiles(layer, srcT, SRCB, wo_d, tiles, gpwhich, final):
        wts = [ring.load(("wo", layer, j), 512, hold=4) for j in range(4)]
        pend = None
        for n_, (c0, xsrc, xdst, nxt, dcol, gp, GPB) in enumerate(tiles):
            xt_, XB_, st_, SB_ = XTS[n_ % 2]
            S.dma("sp", xt_[:], xsrc, XB_, writes=[XB_])
            for j in range(4):
                wt, wb = wts[j]

                def f(e, j=j, wt=wt, c0=c0):
                    ins = None
                    for kc in range(KC):
                        ins = e.matmul(bank(j), lhsT=srcT[:, kc, c0:c0 + P], rhs=wt[:, kc, :], start=(kc == 0), stop=(kc == KC - 1))
                    return ins
                S.op("pe", f, reads=[wb, SRCB], writes=[PB[j]])
            if pend is not None:
                pend()
                pend = None
            yps = psflat[:, 0:D]
            S.op("act", lambda e: e.activation(out=junk2[:], in_=yps, func=AF.Square, accum_out=st_[:, 0:1]), reads=PB[0:4], writes=[TB, SB_])
            S.op("dve", lambda e: e.tensor_scalar(out=st_[:, 1:2], in0=st_[:, 0:1], scalar1=1.0 / D, scalar2=EPS, op0=ALU.mult, op1=ALU.add), reads=[SB_], writes=[SB_])
            S.op("act", lambda e: e.activation(out=st_[:, 2:3], in_=st_[:, 1:2], func=AF.Ln), reads=[SB_], writes=[SB_])
            S.op("act", lambda e: e.activation(out=st_[:, 3:4], in_=st_[:, 2:3], func=AF.Exp, scale=-0.5), reads=[SB_], writes=[SB_])
            S.op("dve", lambda e: e.scalar_tensor_tensor(out=tmp[:], in0=yps, scalar=st_[:, 3:4], in1=gp[:], op0=ALU.mult, op1=ALU.mult),
                 reads=PB[0:4] + [SB_, GPB], writes=[TB])
            S.op("dve", lambda e: e.tensor_tensor(out=xt_[:], in0=xt_[:], in1=tmp[:], op=ALU.add), reads=[TB, XB_], writes=[XB_])
            if xdst is not None:
                tk = S.dma("sp", xdst, xt_[:], XB_, reads=[XB_], writes=[X1B if not final else OUTB])
                FIN_TICKS[id(tk[0])] = tk
            if nxt is not None:
                pend = (lambda xt_=xt_, XB_=XB_, st_=st_, SB_=SB_, nxt=nxt, dcol=dcol:
                        norm_transpose(xt_, XB_, junk2, TB, st_[:, 4:8], SB_, h1T, dcol, HT, nxt[0], nxt[1], [4, 5, 6, 7]))
        if pend is not None:
            pend()

    if stop == "mid":
        S.dma("sp", dbg["midT"], midT[:], DBGB, reads=[MT], writes=[DBGB])
        return finish()
    mv1 = modv[1]
    gpc = A.at(R_H + 40 * KB, [P, D], F32, "gpc")
    GPCB = Buf("gpc")
    ada_gate2(0, [(0, gp, GPB), (1, gpc, GPCB)])
    S.barrier()
    tl = [(t * P, x_d[t * P:(t + 1) * P, :], x1_d[t * P:(t + 1) * P, :], (mv1[1], mv1[0]), t * P, gp, GPB) for t in range(8)]
    tl += [(1024 + t * P, ctx_d[t * P:(t + 1) * P, :], None, (mv1[3], mv1[2]), 1024 + t * P, gpc, GPCB) for t in range(2)]
    epilogue_tiles(0, midT, MT, wo0_d, tl, 0, False)
    S.barrier()

    if stop == "l0":
        S.dma("sp", dbg["h1T"], h1T[:], DBGB, reads=[HT], writes=[DBGB])
        return finish()
    ring.config(2, 640)
    regions = [[R_W, SB_END], [R_RING + 40 * KB, R_RING + 64 * KB], [R_H + 40 * KB, R_H + 56 * KB], [R_M + 32 * KB, R_M + 40 * KB]]

    def wa(shape, dt, name):
        nb = int(np.prod(shape[1:])) * (2 if dt == BF16 else 4)
        nb = (nb + 63) // 64 * 64
        for rg in regions:
            if rg[0] + nb <= rg[1]:
                t = A.at(rg[0], shape, dt, name)
                rg[0] += nb
                return t
        raise AssertionError("heads work region full: " + name)

    T1 = wa([P, 1280], F32, "T1")
    T2 = wa([P, 1280], F32, "T2")
    T3 = wa([P, 1280], F32, "T3")
    TB1, TB2, TB3 = Buf("T1"), Buf("T2"), Buf("T3")
    kinvT = [wa([P, 1280], BF16, "kinvT1"), wa([P, 1024], BF16, "kinvT2")]
    KIB = [Buf("kinvT1"), Buf("kinvT2")]
    qdT = [wa([P, 1024], BF16, "qdT1"), wa([P, 1024], BF16, "qdT2")]
    QDB = [Buf("qdT1"), Buf("qdT2")]
    kinv = [wa([P, 10, P], BF16, "kinv1"), wa([P, 8, P], BF16, "kinv2")]
    KTB = [Buf("kinv1"), Buf("kinv2")]
    vt = wa([P, 10, P], BF16, "vt")
    VB = Buf("vt")
    sgt = wa([P, 8, P], F32, "sgt")
    SGT = Buf("sgt")
    dec = [wa([P, 10], F32, "dec1"), wa([P, 8], F32, "dec2")]
    DCB = [Buf("dec1"), Buf("dec2")]
    scm = [wa([P, 512], BF16, "scm0"), wa([P, 512], BF16, "scm1")]
    SCB = [Buf("scm0"), Buf("scm1")]
    Sall = wa([P, 11, P], F32, "Sall")
    dsd = wa([P, 10, P], F32, "dsd")
    Sball = wa([P, 10, P], BF16, "Sball")
    vTs = wa([P, 1280], BF16, "vTs")
    VTB = Buf("vTs")
    DSB = Buf("dsd")
    SBB = Buf("Sball")
    Sg = wa([P, 2, P], F32, "Sg")
    SGX = Buf("Sg")
    STB = Buf("S")
    o1 = wa([P, 8, P], F32, "o1")
    O1B = Buf("o1")
    ot = wa([P, 8, P], F32, "ot")
    OTB = Buf("ot")
    rs = wa([P, 8, 4], F32, "rs")
    RSB = Buf("rs")
    OGB = MT
    SETS = [(kinvT, KIB, qdT, QDB, kinv, KTB, vt, VB, sgt, SGT, dec, DCB),
            ([wa([P, 1280], BF16, "kinvT1b"), wa([P, 1024], BF16, "kinvT2b")], [Buf("kinvT1b"), Buf("kinvT2b")],
             [wa([P, 1024], BF16, "qdT1b"), wa([P, 1024], BF16, "qdT2b")], [Buf("qdT1b"), Buf("qdT2b")],
             [wa([P, 10, P], BF16, "kinv1b"), wa([P, 8, P], BF16, "kinv2b")], [Buf("kinv1b"), Buf("kinv2b")],
             wa([P, 10, P], BF16, "vtb"), Buf("vtb"), wa([P, 8, P], F32, "sgtb"), Buf("sgtb"),
             [wa([P, 10], F32, "dec1b"), wa([P, 8], F32, "dec2b")], [Buf("dec1b"), Buf("dec2b")])]

    PEND = [None, 0]
    CUR = [0]

    def use(par):
        nonlocal kinvT, KIB, qdT, QDB, kinv, KTB, vt, VB, sgt, SGT, dec, DCB
        (kinvT, KIB, qdT, QDB, kinv, KTB, vt, VB, sgt, SGT, dec, DCB) = SETS[par]

    def gates(hd, d_, zps_banks, ncol):
        zv = psflat[:, zps_banks[0] * 512: zps_banks[0] * 512 + ncol]
        zb = [PB[b] for b in zps_banks]
        oml = lbv[:, hd, 2 * d_:2 * d_ + 1]
        lb = lbv[:, hd, 2 * d_ + 1:2 * d_ + 2]
        noml = nlbv[:, hd, d_:d_ + 1]
        nch = ncol // P
        S.op("act", lambda e: e.activation(out=T1[:, 0:ncol], in_=zv, func=AF.Sigmoid), reads=zb, writes=[TB1])
        S.op("act", lambda e: e.activation(out=T2[:, 0:ncol], in_=T1[:, 0:ncol], func=AF.Ln, scale=oml, bias=lb), reads=[TB1, CONST], writes=[TB2])
        S.op("dve", lambda e: e.tensor_scalar(out=T3[:, 0:ncol], in0=T1[:, 0:ncol], scalar1=noml, scalar2=oml, op0=ALU.mult, op1=ALU.add),
             reads=[TB1, CONST], writes=[TB3])
        S.op("dve", lambda e: e.tensor_tensor_scan(out=T1[:, 0:ncol], data0=rmask[:, 0:ncol], data1=T2[:, 0:ncol], initial=0.0, op0=ALU.mult, op1=ALU.add),
             reads=[TB2, CONST, TB3], writes=[TB1])
        BC, BCB, EX, EXB = T1, TB1, T2, TB2
        if d_ == 1:
            S.op("dve", lambda e: e.tensor_tensor(out=T2[:, 0:ncol], in0=T2[:, 0:ncol], in1=T1[:, 0:ncol], op=ALU.subtract), reads=[TB1, TB2], writes=[TB2])
            t1v = T1[:, 0:ncol].rearrange("p (a b) -> p a b", b=P)
            t2v = T2[:, 0:ncol].rearrange("p (a b) -> p a b", b=P)
            S.op("dve", lambda e: e.tensor_tensor(out=t2v, in0=t2v, in1=t1v[:, :, P - 1:P].broadcast_to([P, nch, P]), op=ALU.add), reads=[TB1, TB2], writes=[TB2])
            BC, BCB, EX, EXB = T2, TB2, T1, TB1
        S.op("act", lambda e: e.activation(out=EX[:, 0:ncol], in_=BC[:, 0:ncol], func=AF.Exp, scale=-1.0), reads=[BCB], writes=[EXB])
        S.op("dve", lambda e: e.tensor_tensor(out=kinvT[d_][:, 0:ncol], in0=T3[:, 0:ncol], in1=EX[:, 0:ncol], op=ALU.mult), reads=[EXB, TB3], writes=[KIB[d_]])
        S.op("act", lambda e: e.activation(out=EX[:, 0:ncol], in_=BC[:, 0:ncol], func=AF.Exp), reads=[BCB, KIB[d_]], writes=[EXB])
        e2v = EX[:, 0:ncol].rearrange("p (a b) -> p a b", b=P)
        col = P - 1 if d_ == 0 else 0
        S.op("dve", lambda e: e.tensor_copy(out=dec[d_][:, 0:nch], in_=e2v[:, :, col]), reads=[EXB], writes=[DCB[d_]])
        return EX, EXB

    def head(hd):
        wt, wb = ring.load(("w1", hd), 640)

        def proj_fm(c0, banks_, cols):
            for (pbk, off, hc0, n) in banks_:
                def f(e, pbk=pbk, off=off, hc0=hc0, n=n):
                    ins = None
                    for kc in range(KC):
                        ins = e.matmul(bank(pbk, n, off), lhsT=wt[:, kc, c0:c0 + P], rhs=h1T[:, kc, hc0:hc0 + n], start=(kc == 0), stop=(kc == KC - 1))
                    return ins
                S.op("pe", f, reads=[wb, HT], writes=[PB[pbk]])

        proj_fm(0, [(0, 0, 0, 512), (1, 0, 512, 512), (2, 0, 1024, 256)], 1280)
        yield
        EX, EXB = gates(hd, 0, [0, 1, 2], 1280)
        yield
        proj_fm(256, [(3, 0, 0, 512), (4, 0, 512, 512)], 1024)
        qv = psflat[:, 3 * 512: 3 * 512 + 1024]
        S.op("dve", lambda e: e.tensor_tensor(out=qdT[0][:], in0=qv, in1=EX[:, 0:1024], op=ALU.mult), reads=[PB[3], PB[4], EXB], writes=[QDB[0]])
        if full:
            yield
            proj_fm(128, [(0, 0, 0, 512), (1, 0, 512, 512)], 1024)
            EX2, EXB2 = gates(hd, 1, [0, 1], 1024)
            S.op("dve", lambda e: e.tensor_tensor(out=qdT[1][:], in0=qv, in1=EX2[:, 0:1024], op=ALU.mult), reads=[PB[3], PB[4], EXB2], writes=[QDB[1]])
        if DBG["hstage"] < 0.25:
            return
        yield
        proj_fm(384, [(2, 0, 0, 512), (3, 0, 512, 512), (4, 0, 1024, 256)], 1280)
        vps = psflat[:, 2 * 512: 2 * 512 + 1280]
        S.op("act", lambda e: e.activation(out=vTs[:], in_=vps, func=AF.Copy), reads=[PB[2], PB[3], PB[4]], writes=[VTB])
        yield

        def fvt(e):
            ins = None
            for t in range(10):
                ins = e.transpose(out=psbf[:, t * P:(t + 1) * P], in_=vTs[:, t * P:(t + 1) * P], identity=idb[:])
            return ins
        S.op("pe", fvt, reads=[VTB, CONST], writes=[PB[0], PB[1]])
        S.op("dve", lambda e: e.tensor_copy(out=vt[:].rearrange("p a b -> p (a b)"), in_=psbf[:, 0:1280]), reads=[PB[0], PB[1]], writes=[VB])
        if full:
            yield
            proj_fm(512, [(2, 0, 0, 512), (3, 0, 512, 512)], 1024)
            S.op("act", lambda e: e.activation(out=sgt[:].rearrange("p a b -> p (a b)"), in_=psflat[:, 2 * 512: 2 * 512 + 1024], func=AF.Silu),
                 reads=[PB[2], PB[3]], writes=[SGT])
        if DBG["hstage"] < 0.35:
            return
        for d_ in range(2 if full else 1):
            nch = 10 if d_ == 0 else 8
            for g0 in range(0, nch, 8):
                gn = min(8, nch - g0)
                pbk = 3 + d_

                def f(e, d_=d_, g0=g0, gn=gn, pbk=pbk):
                    ins = None
                    for q in range(gn):
                        c = g0 + q
                        ins = e.transpose(out=psbf[:, pbk * 1024 + q * P: pbk * 1024 + (q + 1) * P], in_=kinvT[d_][:, c * P:(c + 1) * P], identity=idb[:])
                    return ins
                S.op("pe", f, reads=[KIB[d_], CONST], writes=[PB[pbk]])
                S.op("act", lambda e, d_=d_, g0=g0, gn=gn, pbk=pbk: e.activation(
                    out=kinv[d_][:, g0:g0 + gn, :].rearrange("p a b -> p (a b)"), in_=psbf[:, pbk * 1024: pbk * 1024 + gn * P], func=AF.Copy),
                    reads=[PB[pbk]], writes=[KTB[d_]])

    def head2(hd):
        CUR[0] = hd % 2
        use(hd % 2)

        def scan(d_, order, s_init):
            mk = mk1 if d_ == 0 else mk2
            n = len(order)
            if s_init is None:
                S.op("dve", lambda e: e.memset(Sall[:, 0, :], 0.0), writes=[STB])
            else:
                s_init()
            lat = [c for c in order if c < 8]
            for g0 in range(0, n, 4):
                cs = order[g0:g0 + 4]

                def fs(e, cs=cs):
                    ins = None
                    for q, c in enumerate(cs):
                        ins = e.matmul(bank(7, P, q * P), lhsT=kinv[d_][:, c, :], rhs=vt[:, c, :], start=True, stop=True)
                    return ins
                S.op("pe", fs, reads=[KTB[d_], VB], writes=[PB[7]])
                for q, c in enumerate(cs):
                    S.op("act", lambda e, q=q, c=c, i=g0 + q: e.activation(out=dsd[:, i, :], in_=bank(7, P, q * P), func=AF.Identity, scale=dec[d_][:, c:c + 1]),
                         reads=[PB[7], DCB[d_]], writes=[DSB])
                step()
            for g in range(2):
                cs = lat[g * 4:(g + 1) * 4]
                pbk = 5 + g

                def f(e, cs=cs, pbk=pbk):
                    ins = None
                    for q, c in enumerate(cs):
                        ins = e.matmul(bank(pbk, P, q * P), lhsT=kinvT[d_][:, c * P:(c + 1) * P], rhs=qdT[d_][:, c * P:(c + 1) * P], start=True, stop=True)
                    return ins
                S.op("pe", f, reads=[KIB[d_], QDB[d_]], writes=[PB[pbk]])
                S.op("dve", lambda e, g=g, pbk=pbk: e.tensor_tensor(out=scm[g][:].rearrange("p (a b) -> p a b", b=P), in0=bank(pbk).rearrange("p (a b) -> p a b", b=P),
                                                                in1=mk[:].unsqueeze(1).broadcast_to([P, 4, P]), op=ALU.mult),
                     reads=[PB[pbk], CONST], writes=[SCB[g]])
            for i, c in enumerate(order):
                S.op("dve", lambda e, i=i, c=c: e.scalar_tensor_tensor(out=Sall[:, i + 1, :], in0=Sall[:, i, :], scalar=dec[d_][:, c:c + 1], in1=dsd[:, i, :],
                                                                   op0=ALU.mult, op1=ALU.add), reads=[STB, DSB, DCB[d_]], writes=[STB])
            S.op("act", lambda e: e.activation(out=Sball[:, 0:n, :].rearrange("p a b -> p (a b)"), in_=Sall[:, 0:n, :].rearrange("p a b -> p (a b)"), func=AF.Copy),
                 reads=[STB], writes=[SBB])
            step()
            for n_, c in enumerate(lat):
                i = order.index(c)
                g, q = divmod(n_, 4)
                pbo = 5 + (n_ % 2)

                def fo(e, c=c, g=g, q=q, pbo=pbo, i=i):
                    e.matmul(bank(pbo, P), lhsT=scm[g][:, q * P:(q + 1) * P], rhs=vt[:, c, :], start=True, stop=False)
                    return e.matmul(bank(pbo, P), lhsT=qdT[d_][:, c * P:(c + 1) * P], rhs=Sball[:, i, :], start=False, stop=True)
                S.op("pe", fo, reads=[SCB[g], VB, QDB[d_], SBB], writes=[PB[pbo]])
                if d_ == 0:
                    S.op("act", lambda e, c=c, pbo=pbo: e.activation(out=o1[:, c, :], in_=bank(pbo, P), func=AF.Copy), reads=[PB[pbo]], writes=[O1B])
                else:
                    S.op("dve", lambda e, c=c, pbo=pbo: e.tensor_tensor(out=ot[:, c, :], in0=bank(pbo, P), in1=o1[:, c, :], op=ALU.add), reads=[PB[pbo], O1B], writes=[OTB])
                step()

        if DBG["hstage"] < 2:
            return
        scan(0, [8, 9, 0, 1, 2, 3, 4, 5, 6, 7], None)
        if DBG["hstage"] < 3:
            return
        if mode == "A":
            S.dma("sp", sout_d[hd], Sall[:, 10, :], STB, reads=[STB], writes=[OUTB])
            return
        if mode == "F":
            AGB = Buf("agin")
            S.dma("sp", agin[hd][:, :], Sall[:, 10, :], STB, reads=[STB], writes=[AGB])
            S._need("pool", S._deps([AGB], []))
            ccs = es.enter_context(nc.semaphore("cc%d" % hd))
            nc.gpsimd.collective_compute("AllGather", ALU.bypass, replica_groups=[[2 * i_, 2 * i_ + 1] for i_ in range(DBG.get("ncores", 8) // 2)],
                                         ins=[agin[hd].ap().opt()], outs=[agout[hd].ap().opt()]).then_inc(ccs, 1)

            def init2():
                nc.sync.wait_ge(ccs, 1)
                S.dma("sp", Sg[:], agout[hd].ap().rearrange("(r p) v -> p r v", p=P), SGX, reads=[], writes=[SGX])
                S.op("dve", lambda e: e.tensor_scalar(out=Sall[:, 0, :], in0=Sg[:, 0, :], scalar1=psel[:, 0:1], scalar2=None, op0=ALU.mult), reads=[SGX, CONST], writes=[STB])
                S.op("dve", lambda e: e.scalar_tensor_tensor(out=Sall[:, 0, :], in0=Sg[:, 1, :], scalar=psel[:, 1:2], in1=Sall[:, 0, :], op0=ALU.mult, op1=ALU.add),
                     reads=[SGX, CONST, STB], writes=[STB])
            if DBG["hstage"] < 3.5:
                init2()
                return
            scan(1, [7, 6, 5, 4, 3, 2, 1, 0], init2)
        else:
            scan(1, [7, 6, 5, 4, 3, 2, 1, 0], lambda: S.dma("sp", Sall[:, 0, :], sin_d[hd], STB, writes=[STB]))
        if DBG["hstage"] < 4:
            return
        S.op("dve", lambda e: e.tensor_tensor(out=o1[:], in0=ot[:], in1=ot[:], op=ALU.mult), reads=[OTB, O1B], writes=[O1B])
        S.op("dve", lambda e: e.tensor_reduce(out=rs[:, :, 0], in_=o1[:], axis=AX.X, op=ALU.add), reads=[O1B], writes=[RSB])
        S.op("dve", lambda e: e.tensor_scalar(out=rs[:, :, 1], in0=rs[:, :, 0], scalar1=1.0 / P, scalar2=EPS, op0=ALU.mult, op1=ALU.add), reads=[RSB], writes=[RSB])
        S.op("act", lambda e: e.activation(out=rs[:, :, 2], in_=rs[:, :, 1], func=AF.Ln), reads=[RSB], writes=[RSB])
        S.op("act", lambda e: e.activation(out=rs[:, :, 3], in_=rs[:, :, 2], func=AF.Exp, scale=-0.5), reads=[RSB], writes=[RSB])
        S.op("dve", lambda e: e.tensor_tensor(out=ot[:], in0=ot[:], in1=rs[:, :, 3:4].broadcast_to([P, 8, P]), op=ALU.mult), reads=[RSB, OTB], writes=[OTB])
        for g in range(2):
            pbk = 5 + g

            def f(e, g=g, pbk=pbk):
                ins = None
                for q in range(4):
                    ins = e.transpose(out=bank(pbk, P, q * P), in_=ot[:, g * 4 + q, :], identity=idf[:])
                return ins
            S.op("pe", f, reads=[OTB, CONST], writes=[PB[pbk]])
            S.op("dve", lambda e, g=g, pbk=pbk: e.scalar_tensor_tensor(out=ogT[:, hd, g * 512:(g + 1) * 512], in0=bank(pbk), scalar=ongT[:, hd:hd + 1],
                                                                   in1=sgt[:].rearrange("p a b -> p (a b)")[:, g * 512:(g + 1) * 512], op0=ALU.mult, op1=ALU.mult),
                 reads=[PB[pbk], CONST, SGT], writes=[OGB])

    def step():
        if PEND[0] is not None:
            use(PEND[1])
            try:
                next(PEND[0])
            except StopIteration:
                PEND[0] = None
            use(CUR[0])

    def exhaust():
        while PEND[0] is not None:
            step()

    PEND[0], PEND[1] = head(0), 0
    exhaust()
    for hd in range(DBG["nheads"]):
        if hd + 1 < DBG["nheads"]:
            PEND[0], PEND[1] = head(hd + 1), (hd + 1) % 2
            exhaust()
        head2(hd)
        exhaust()
    S.barrier()
    if stop == "heads":
        if not DBG.get("nodump"):
            S.dma("sp", dbg["ogT"], ogT[:], DBGB, reads=[OGB], writes=[DBGB])
        return finish()

    if full:
        ring.config(4, 512)
        ada_gate2(1, [(0, gp, GPB)])
        S.barrier()
        tl = [(t * P, x1_d[t * P:(t + 1) * P, :], out_d[t * P:(t + 1) * P, :], None, 0, gp, GPB) for t in range(8)]
        S._need("sp", dict(FIN_TICKS))
        epilogue_tiles(1, ogT, OGB, wo1_d, tl, 0, True)
    return finish()


def _pool_consts(s):
    pm = np.zeros((4, NPMB, P, P), np.float32)
    invc = np.zeros((4, P, NOWN + NCTX), np.float32)

    def g_rc(R, C):
        return (R, C) if s == 0 else (31 - R, 63 - C)

    for gi, w in enumerate(WINS):
        lo, hi = w // 2, w // 2 - 1
        r = RAD[gi]

        def inwin(go, gin):
            return go - lo <= gin <= go + hi

        def cnt1(g, n):
            return min(g + hi, n - 1) - max(g - lo, 0) + 1
        for rel in range(-r, r + 1):
            if rel == 0:
                continue
            B = np.zeros((P, P), np.float32)
            for oi in range(P):
                Ro, Co = 12 + oi // 64, oi % 64
                gro, gco = g_rc(Ro, Co)
                for ii in range(P):
                    Ri, Ci = 12 + 2 * rel + ii // 64, ii % 64
                    gri, gci = g_rc(Ri, Ci)
                    if inwin(gro, gri) and inwin(gco, gci):
                        B[ii, oi] = 1.0
            pm[gi, rel + 4] = B
        for j in range(8):
            Dm = np.zeros((P, P), np.float32)
            for oi in range(P):
                Ro, Co = 2 * j + oi // 64, oi % 64
                gro, gco = g_rc(Ro, Co)
                c = cnt1(gro, 32) * cnt1(gco, 64)
                invc[gi, :, j * P + oi] = 1.0 / c
                for ii in range(P):
                    Ri, Ci = 2 * j + ii // 64, ii % 64
                    gri, gci = g_rc(Ri, Ci)
                    if inwin(gro, gri) and inwin(gco, gci):
                        Dm[ii, oi] = 1.0
                Dm[oi, oi] -= c
            pm[gi, 9 + j] = Dm
        for jc in range(2):
            for ic in range(2):
                Cm = np.zeros((P, P), np.float32)
                for oi in range(P):
                    po = jc * P + oi
                    go = po if s == 0 else 255 - po
                    c = cnt1(go, 256)
                    invc[gi, :, NOWN + po] = 1.0 / c
                    for ii in range(P):
                        pi = ic * P + ii
                        gin = pi if s == 0 else 255 - pi
                        if inwin(go, gin):
                            Cm[ii, oi] = 1.0
                    if ic == jc:
                        Cm[oi, oi] -= c
                pm[gi, 17 + jc * 2 + ic] = Cm
    return pm, invc


def _pp(v):
    return np.ascontiguousarray(np.asarray(v, np.float32).reshape(KC, P).T)


_CACHE = {}


def _program():
    if "F" not in _CACHE:
        p1 = build_program("F")
        _CACHE["F"] = build_program("F", plan=p1._ring_plan)
    return _CACHE["F"]


def _host_inputs(x, c, ctx, c_ctx, ada_w, ada_b, pre_g, post_g, ev_w_in, ev_pool_w, ev_pool_scale, ev_conv_w,
                 ev_conv_b, ev_w_out, od_w_in, od_onorm_g, od_w_out, lb_logits):
    f = np.float32
    x = np.asarray(x, f); ctx = np.asarray(ctx, f); c = np.asarray(c, f); c_ctx = np.asarray(c_ctx, f)
    ada_w = np.asarray(ada_w, f); ada_b = np.asarray(ada_b, f)
    adaw = np.ascontiguousarray(ada_w.reshape(2, D, 12, 512).transpose(0, 2, 1, 3))
    adabT = np.zeros((2, P, 64), f)
    for l in range(2):
        a = ada_b[l, :4096].reshape(32, P).T
        adabT[l, :, 0::2] = a
        adabT[l, :, 1::2] = a
    adabg = np.ascontiguousarray(np.broadcast_to(ada_b[:, None, 4096:], (2, P, D)))
    postg = np.ascontiguousarray(np.broadcast_to(np.asarray(post_g, f)[:, None, :], (2, P, D)))
    pregT = np.stack([_pp(pre_g[0]), _pp(pre_g[1])])
    wi = np.asarray(ev_w_in, f)[0]
    blocks = []
    for gi in range(4):
        blocks.append(np.concatenate([wi[:, gi * 256:(gi + 1) * 256], wi[:, 1024 + gi * 256:1024 + (gi + 1) * 256]], 1))
    for cb in range(8):
        blocks.append(np.concatenate([wi[:, 2048 + k * 1024 + cb * P: 2048 + k * 1024 + (cb + 1) * P] for k in range(4)], 1))
    w0blk = np.ascontiguousarray(np.stack(blocks))
    w1 = np.asarray(od_w_in, f)[0]

    def w1blk(s):
        zf, zb = (0, 2048) if s == 0 else (2048, 0)
        bl = []
        for hd in range(16):
            sl = slice(hd * P, (hd + 1) * P)
            bl.append(np.concatenate([w1[:, zf:zf + 2048][:, sl], w1[:, zb:zb + 2048][:, sl], w1[:, 6144:8192][:, sl],
                                      w1[:, 4096:6144][:, sl], w1[:, 8192:10240][:, sl]], 1))
        return np.ascontiguousarray(np.stack(bl))
    w1b = [w1blk(0), w1blk(1)]
    pc = [_pool_consts(0), _pool_consts(1)]
    pscT = np.ascontiguousarray(np.asarray(ev_pool_scale, f)[0].reshape(8, P).T)
    cw = np.asarray(ev_conv_w, f)[0]
    cbT = np.ascontiguousarray(np.asarray(ev_conv_b, f)[0].reshape(8, P).T)
    ongT = _pp(np.asarray(od_onorm_g, f)[0])
    lbl = np.asarray(lb_logits, f)
    idf = np.eye(P, dtype=f)
    m1 = np.triu(np.ones((P, P), f))
    m2 = np.ascontiguousarray(m1.T)
    wo0 = np.ascontiguousarray(np.asarray(ev_w_out, f)[0])
    wo1 = np.ascontiguousarray(np.asarray(od_w_out, f)[0])
    poolw = np.ascontiguousarray(np.asarray(ev_pool_w, f)[0])
    maps = []
    for core in range(8):
        b, s = core // 2, core % 2
        xb = x[b] if s == 0 else x[b, ::-1]
        cxb = ctx[b] if s == 0 else ctx[b, ::-1]
        cs2 = np.stack([_pp(c[b]), _pp(c_ctx)], -1)
        cbc = np.stack([np.broadcast_to(_pp(c[b])[:, :, None], (P, KC, P)), np.broadcast_to(_pp(c_ctx)[:, :, None], (P, KC, P))])
        cwl = cw if s == 0 else cw[::-1]
        cwT = np.ascontiguousarray(cwl.reshape(3, 8, P).transpose(2, 1, 0))
        lb = lbl if s == 0 else lbl[::-1]
        lblT = np.ascontiguousarray(lb.reshape(2, 3, 16, P).transpose(3, 0, 1, 2))
        maps.append({
            "x_loc": np.ascontiguousarray(xb[:NOWN + NHALO]), "ctx_loc": np.ascontiguousarray(cxb),
            "cs2": np.ascontiguousarray(cs2), "cbc": np.ascontiguousarray(cbc), "adaw": adaw, "adabT": adabT, "adabg": adabg,
            "postg": postg, "pregT": pregT, "w0blk": w0blk, "wo0": wo0, "w1blk": w1b[s], "wo1": wo1, "poolw": poolw,
            "pm": pc[s][0], "invc": pc[s][1], "pscT": pscT, "cwT": cwT, "cbT": cbT, "lbl": lblT, "ongT": ongT,
            "idf": idf, "mask1": m1, "mask2": m2,
            "psel": np.ascontiguousarray(np.broadcast_to(np.array([1.0, 0.0] if s == 1 else [0.0, 1.0], f), (P, 2))),
        })
    return maps


def kernel(**inputs):
    ncF = _program()
    maps = _host_inputs(**inputs)
    resB = run_bass_kernel_spmd(ncF, maps, core_ids=list(range(8)))
    out = np.empty((4, 2048, D), np.float32)
    for core in range(8):
        b, s = core // 2, core % 2
        y = np.asarray(resB.results[core]["out"], np.float32)
        if s == 0:
            out[b, :NOWN] = y
        else:
            out[b, NOWN:] = y[::-1]
    return out
```

```python
import numpy as np
from contextlib import ExitStack
import concourse.bass as bass
import concourse.mybir as mybir
from concourse.bass_utils import run_bass_kernel_spmd

F32 = mybir.dt.float32
BF16 = mybir.dt.bfloat16
AF = mybir.ActivationFunctionType
ALU = mybir.AluOpType
AX = mybir.AxisListType

P = 128
D = 2048
KC = 16
NOWN = 1024
NHALO = 512
NCTX = 256
EPS = 1e-6
WINS = (2, 4, 8, 16)
RAD = (1, 1, 2, 4)
NPMB = 21
SB_BASE = 16640
SB_END = 229376 - 64


DBG = {"nheads": 16, "hstage": 9}


class Buf:
    __slots__ = ("name", "w", "r", "dsem", "dcnt", "ps")

    def __init__(self, name, ps=False):
        self.name = name
        self.ps = ps
        self.w = None
        self.r = {}
        self.dsem = None
        self.dcnt = 0


class Sched:
    def __init__(self, nc, es):
        self.nc = nc
        self.es = es
        self.eng = {"pe": nc.tensor, "act": nc.scalar, "dve": nc.vector, "pool": nc.gpsimd, "sp": nc.sync}
        self.sem = {e: es.enter_context(nc.semaphore("s_" + e)) for e in self.eng}
        self.cnt = {e: 0 for e in self.eng}
        self.seen = {e: {} for e in self.eng}
        self.nsem = 0

    def _deps(self, reads, writes):
        deps = {}

        def add(t):
            if t is None:
                return
            k = id(t[0])
            if k not in deps or deps[k][1] < t[1]:
                deps[k] = t

        for b in reads:
            add(b.w)
            if b.ps:
                for t in b.r.values():
                    add(t)
        for b in writes:
            add(b.w)
            for t in b.r.values():
                add(t)
        return deps

    def _need(self, e, deps):
        for k, (sem, val) in deps.items():
            if self.seen[e].get(k, 0) >= val:
                continue
            self.eng[e].wait_ge(sem, val)
            self.seen[e][k] = val

    def _mark(self, t, reads, writes):
        k = id(t[0])
        for b in reads:
            if k not in b.r or b.r[k][1] < t[1]:
                b.r[k] = t
        for b in writes:
            b.w = t
            b.r = {}

    def op(self, e, fn, reads=(), writes=()):
        deps = self._deps(reads, writes)
        if e == "pe":
            deps.pop(id(self.sem["pe"]), None)
        self._need(e, deps)
        ins = fn(self.eng[e])
        self.cnt[e] += 1
        ins.then_inc(self.sem[e], 1)
        t = (self.sem[e], self.cnt[e])
        self._mark(t, reads, writes)
        return t

    def dma(self, q, out, in_, dbuf, reads=(), writes=()):
        if dbuf.dsem is None:
            dbuf.dsem = self.es.enter_context(self.nc.semaphore("d%d" % self.nsem))
            self.nsem += 1
        self._need(q, self._deps(reads, writes))
        ins = self.eng[q].dma_start(out=out, in_=in_)
        dbuf.dcnt += 16
        ins.then_inc(dbuf.dsem, 16)
        t = (dbuf.dsem, dbuf.dcnt)
        self._mark(t, reads, writes)
        return t

    def barrier(self):
        cur = {id(self.sem[e]): (self.sem[e], self.cnt[e]) for e in self.eng if self.cnt[e] > 0}
        for e in self.eng:
            d = dict(cur)
            d.pop(id(self.sem[e]), None)
            self._need(e, d)

    def wait_buf(self, e, b):
        self._need(e, self._deps([], [b]))


class Arena:
    def __init__(self, nc):
        self.nc = nc
        self.n = 0

    def at(self, off, shape, dt, name=None):
        nb = int(np.prod(shape[1:])) * (2 if dt == BF16 else 4)
        assert off % 32 == 0, off
        assert SB_BASE <= off and off + nb <= SB_END, (name, off, nb)
        self.n += 1
        return self.nc.alloc_sbuf_tensor_at("%s_%d" % (name or "t", self.n), list(shape), dt, offset=off)


def build_program(mode, stop=None, plan=None):
    nc = bass.Bass("TRN2", target_bir_lowering=False)
    full = mode in ("B", "F")

    def din(name, shape, dt=F32):
        return nc.dram_tensor(name, list(shape), dt, kind="ExternalInput").ap()

    x_d = din("x_loc", [NOWN + NHALO, D])
    ctx_d = din("ctx_loc", [NCTX, D])
    cs2_d = din("cs2", [P, KC, 2])
    cbc_d = din("cbc", [2, P, KC, P])
    adaw_d = din("adaw", [2, 12, D, 512])
    adabT_d = din("adabT", [2, P, 64])
    adabg_d = din("adabg", [2, P, D])
    postg_d = din("postg", [2, P, D])
    pregT_d = din("pregT", [2, P, KC])
    w0_d = din("w0blk", [12, D, 512])
    wo0_d = din("wo0", [D, D])
    w1_d = din("w1blk", [16, D, 640])
    wo1_d = din("wo1", [D, D])
    poolw_d = din("poolw", [4, 256, 256])
    pm_d = din("pm", [4, NPMB, P, P])
    invc_d = din("invc", [4, P, NOWN + NCTX])
    pscT_d = din("pscT", [P, 8])
    cwT_d = din("cwT", [P, 8, 3])
    cbT_d = din("cbT", [P, 8])
    lbl_d = din("lbl", [P, 2, 3, 16])
    ongT_d = din("ongT", [P, KC])
    idf_d = din("idf", [P, P])
    m1_d = din("mask1", [P, P])
    m2_d = din("mask2", [P, P])
    if mode == "B":
        sin_d = din("sin", [16, P, P])
    if mode == "F":
        psel_d = din("psel", [P, 2])
        agin = [nc.dram_tensor("agin%d" % i, [P, P], F32) for i in range(16)]
        agout = [nc.dram_tensor("agout%d" % i, [2 * P, P], F32) for i in range(16)]
    if full:
        out_d = nc.dram_tensor("out", [NOWN, D], F32, kind="ExternalOutput").ap()
    else:
        sout_d = nc.dram_tensor("sout", [16, P, P], F32, kind="ExternalOutput").ap()
    x1_d = nc.dram_tensor("x1_scr", [NOWN, D], F32, **({"kind": "ExternalOutput"} if stop else {})).ap()
    dbg = {}
    if stop:
        dbg["hT"] = nc.dram_tensor("dbg_hT", [P, KC, 1792], BF16, kind="ExternalOutput").ap()
        dbg["midT"] = nc.dram_tensor("dbg_midT", [P, KC, 1280], BF16, kind="ExternalOutput").ap()
        dbg["h1T"] = nc.dram_tensor("dbg_h1T", [P, KC, 1280], BF16, kind="ExternalOutput").ap()
        dbg["ogT"] = nc.dram_tensor("dbg_ogT", [P, KC, 1024], BF16, kind="ExternalOutput").ap()
        dbg["modv"] = nc.dram_tensor("dbg_modv", [8, P, KC], F32, kind="ExternalOutput").ap()
        dbg["lbv"] = nc.dram_tensor("dbg_lbv", [P, 16, 4], F32, kind="ExternalOutput").ap()
    DBGB = Buf("dbg")

    def finish():
        S.barrier()
        for b_ in (OUTB, X1B, DBGB):
            S.wait_buf("sp", b_)
        try:
            S._need("sp", dict(FIN_TICKS))
        except NameError:
            pass
        es.close()
        nc._ring_plan = ring.rec
        return nc


    es = ExitStack()
    S = Sched(nc, es)
    A = Arena(nc)
    KB = 1024

    pst = es.enter_context(nc.psum_tensor("pst", [P, 8, 512], F32))
    psflat = pst[:].rearrange("p a b -> p (a b)")
    psbf = psflat.bitcast(BF16)
    PB = [Buf("ps%d" % i, ps=True) for i in range(8)]

    def bank(i, n=512, off=0):
        return psflat[:, i * 512 + off: i * 512 + off + n]

    R_RING = SB_BASE
    R_H = R_RING + 64 * KB
    R_M = R_H + 56 * KB
    R_C = R_M + 40 * KB
    o = R_C

    def calloc(shape, dt, name):
        nonlocal o
        t = A.at(o, shape, dt, name)
        nb = int(np.prod(shape[1:])) * (2 if dt == BF16 else 4)
        o += (nb + 31) // 32 * 32
        return t

    idf = calloc([P, P], F32, "idf")
    idb = calloc([P, P], BF16, "idb")
    mk1 = calloc([P, P], BF16, "mk1")
    mk2 = calloc([P, P], BF16, "mk2")
    rmask = calloc([P, 1280], F32, "rmask")
    s2b = calloc([P, KC, 2], BF16, "s2b")
    lbv = calloc([P, 16, 4], F32, "lbv")
    nlbv = calloc([P, 16, 2], F32, "nlbv")
    ongT = calloc([P, KC], F32, "ongT")
    pscT = calloc([P, 8], F32, "pscT")
    cwT = calloc([P, 8, 3], F32, "cwT")
    cbT = calloc([P, 8], F32, "cbT")
    modv = [[calloc([P, KC], F32, "modv") for _ in range(4)] for _ in range(2)]
    poolw = calloc([P, 4, 2, 256], BF16, "poolw")
    epsc = calloc([P, 1], F32, "epsc")
    psel = calloc([P, 2], F32, "psel")
    R_W = (o + 63) // 64 * 64
    CONST = Buf("const")
    assert R_W - R_C <= 20 * KB, R_W - R_C

    hT = A.at(R_H, [P, KC, 1792], BF16, "hT")
    h1T = A.at(R_H, [P, KC, 1280], BF16, "h1T")
    midT = A.at(R_M, [P, KC, 1280], BF16, "midT")
    ogT = A.at(R_M, [P, KC, 1024], BF16, "ogT")
    HT = Buf("hT")
    MT = Buf("midT")

    def wsrc(key):
        if key[0] == "adaw":
            return adaw_d[key[1], key[2]]
        if key[0] == "w0":
            return w0_d[key[1]]
        if key[0] == "w1":
            return w1_d[key[1]]
        wo_ = wo0_d if key[1] == 0 else wo1_d
        return wo_[:, key[2] * 512:(key[2] + 1) * 512]

    class Ring:
        def __init__(self):
            self.slots = []
            self.i = 0
            self.epoch = 0
            self.n = 0
            self.issued = 0
            self.rec = []
            self.inflight = {}

        def config(self, n, cols):
            nb = KC * cols * 2
            assert n * nb <= 64 * KB
            self.slots = [(A.at(R_RING + k * nb, [P, KC, cols], BF16, "ring"), Buf("ring%d" % k)) for k in range(n)]
            self.i = 0
            self.epoch += 1

        def _issue(self, src2d, cols):
            t, b = self.slots[self.i % len(self.slots)]
            self.i += 1
            v = src2d.rearrange("(kc p) n -> p kc n", p=P)
            for q in range(2):
                S.dma("pool", t[:, q * 8:(q + 1) * 8, 0:cols], v[:, q * 8:(q + 1) * 8, :], b, writes=[b])
            return t, b

        def load(self, key, cols, hold=1):
            src2d = wsrc(key)
            self.rec.append((key, cols, self.epoch, hold))
            n = self.n
            self.n += 1
            if plan is None:
                return self._issue(src2d, cols)
            if n >= self.issued:
                self.inflight[n] = self._issue(src2d, cols)
                self.issued = n + 1
            res = self.inflight.pop(n)
            depth = len(self.slots) - max(hold, 1)
            while self.issued < len(plan) and self.issued <= n + depth and plan[self.issued][2] == self.epoch:
                key_n, cols_n, _, _ = plan[self.issued]
                self.inflight[self.issued] = self._issue(wsrc(key_n), cols_n)
                self.issued += 1
            return res

    ring = Ring()

    def cload(dst, src, q="pool"):
        S.dma(q, dst, src, CONST, writes=[CONST])

    cload(idf[:], idf_d)
    cload(idb[:], idf_d, "pool")
    cload(mk1[:], m1_d, "pool")
    cload(mk2[:], m2_d, "pool")
    cload(ongT[:], ongT_d)
    cload(pscT[:], pscT_d)
    cload(cwT[:], cwT_d)
    cload(cbT[:], cbT_d)
    if mode == "F":
        cload(psel[:], psel_d)
    cload(poolw[:], poolw_d.rearrange("g (c p) o -> p g c o", p=P), "pool")
    S.op("dve", lambda e: e.memset(rmask[:], 1.0), writes=[CONST])
    S.op("dve", lambda e: e.memset(rmask[:].rearrange("p (a b) -> p a b", b=P)[:, :, 0:1], 0.0), writes=[CONST])
    S.op("dve", lambda e: e.memset(epsc[:], EPS), writes=[CONST])

    if mode == "F" and DBG.get("cc_early"):
        tt = A.at(R_M, [P, P], F32, "cctest")
        TTB = Buf("cctest")
        S.dma("sp", tt[:], idf_d, TTB, writes=[TTB])
        AG0 = Buf("ag0")
        S.dma("sp", agin[15][:, :], tt[:], TTB, reads=[TTB], writes=[AG0])
        S._need("pool", S._deps([AG0], []))
        cc0 = es.enter_context(nc.semaphore("cc_early"))
        nc.gpsimd.collective_compute("AllGather", ALU.bypass, replica_groups=[[2 * i_, 2 * i_ + 1] for i_ in range(DBG.get("ncores", 8) // 2)],
                                     ins=[agin[15].ap().opt()], outs=[agout[15].ap().opt()]).then_inc(cc0, 1)
        nc.sync.wait_ge(cc0, 1)
    def wtile(off, shape, dt, name):
        return A.at(R_W + off, shape, dt, name)

    W_LIMIT = SB_END - R_W

    cs2f = wtile(0, [P, KC, 2], F32, "cs2f")
    WK = Buf("wk")
    S.dma("sp", cs2f[:], cs2_d, WK, writes=[WK])
    S.op("act", lambda e: e.activation(out=s2b[:], in_=cs2f[:], func=AF.Silu), reads=[WK], writes=[CONST])

    lbl = wtile(256, [P, 2, 3, 16], F32, "lbl")
    lbe = wtile(1024, [P, 2, 3, 16], F32, "lbe")
    lbs = wtile(2048, [P, 2, 16], F32, "lbs")
    lbn = wtile(2560, [P, 2, 16], F32, "lbn")
    WK2 = Buf("wk2")
    S.dma("sp", lbl[:], lbl_d, WK2, writes=[WK2])
    S.op("act", lambda e: e.activation(out=lbe[:], in_=lbl[:], func=AF.Exp), reads=[WK2], writes=[WK2])
    S.op("dve", lambda e: e.tensor_tensor(out=lbn[:], in0=lbe[:, :, 0, :], in1=lbe[:, :, 1, :], op=ALU.add), reads=[WK2], writes=[WK2])
    S.op("dve", lambda e: e.tensor_tensor(out=lbs[:], in0=lbn[:], in1=lbe[:, :, 2, :], op=ALU.add), reads=[WK2], writes=[WK2])
    S.op("dve", lambda e: e.reciprocal(out=lbs[:], in_=lbs[:]), reads=[WK2], writes=[WK2])
    for d_ in range(2):
        S.op("dve", lambda e, d_=d_: e.tensor_tensor(out=lbv[:, :, 2 * d_ + 1], in0=lbn[:, d_, :], in1=lbs[:, d_, :], op=ALU.mult), reads=[WK2], writes=[CONST])
        S.op("dve", lambda e, d_=d_: e.tensor_tensor(out=lbv[:, :, 2 * d_], in0=lbe[:, d_, 2, :], in1=lbs[:, d_, :], op=ALU.mult), reads=[WK2], writes=[CONST])
        S.op("dve", lambda e, d_=d_: e.tensor_scalar(out=nlbv[:, :, d_], in0=lbv[:, :, 2 * d_], scalar1=-1.0, scalar2=None, op0=ALU.mult), reads=[CONST], writes=[CONST])

    ring.config(4, 512)
    WADA = Buf("wada")

    def ada_part1(layer):
        pb = PB[7]
        adab = wtile(30 * KB, [P, 64], F32, "adab")
        pg = wtile(30 * KB + 512, [P, KC], F32, "pg")
        mt = wtile(30 * KB + 1024, [P, 32, 2], F32, "mt")
        WA = WADA
        S.dma("sp", adab[:], adabT_d[layer], WA, writes=[WA])
        S.dma("sp", pg[:], pregT_d[layer], WA, writes=[WA])
        for j in range(8):
            wt, wb = ring.load(("adaw", layer, j), 512)

            def f(e, j=j, wt=wt):
                ins = None
                for fb in range(4):
                    for kc in range(KC):
                        ins = e.matmul(bank(7, 2, (j * 4 + fb) * 2), lhsT=wt[:, kc, fb * P:(fb + 1) * P], rhs=s2b[:, kc, :],
                                       start=(kc == 0), stop=(kc == KC - 1))
                return ins
            S.op("pe", f, reads=[wb, CONST], writes=[pb])
        S.op("dve", lambda e: e.tensor_tensor(out=mt[:].rearrange("p a b -> p (a b)"), in0=bank(7, 64), in1=adab[:], op=ALU.add),
             reads=[pb, WA], writes=[WA])
        mv = modv[layer]
        for v in range(2):
            S.op("dve", lambda e, v=v: e.tensor_copy(out=mv[2 * v][:], in_=mt[:, 0:16, v]), reads=[WA], writes=[CONST])
            S.op("dve", lambda e, v=v: e.scalar_tensor_tensor(out=mv[2 * v + 1][:], in0=mt[:, 16:32, v], scalar=1.0, in1=pg[:],
                                                           op0=ALU.add, op1=ALU.mult), reads=[WA], writes=[CONST])

    def ada_gate2(layer, variants):
        WG = Buf("wg")
        cbf = wtile(24 * KB, [P, KC, P], F32, "cbf")
        pgt = wtile(16 * KB, [P, D], F32, "pgt")
        bp = wtile(24 * KB, [P, D], F32, "bp")
        cbbs = []
        for vi, (which, gp_, gpb_) in enumerate(variants):
            cbb = A.at(R_H + 48 * KB + vi * 4 * KB, [P, KC, P], BF16, "cbb")
            S.dma("sp", cbf[:], cbc_d[which], WG, writes=[WG])
            S.op("act", lambda e, cbb=cbb: e.activation(out=cbb[:], in_=cbf[:], func=AF.Silu), reads=[WG], writes=[WG])
            cbbs.append(cbb)
        S.dma("sp", pgt[:], postg_d[layer], WG, writes=[WG])
        S.dma("sp", bp[:], adabg_d[layer], WG, writes=[WG])
        S.op("dve", lambda e: e.tensor_tensor(out=bp[:], in0=bp[:], in1=pgt[:], op=ALU.mult), reads=[WG], writes=[WG])
        for j in range(4):
            wt, wb = ring.load(("adaw", layer, 8 + j), 512)
            for vi, (which, gp_, gpb_) in enumerate(variants):
                pbi = 2 * vi + (j % 2)

                def f(e, wt=wt, pbi=pbi, cbb=cbbs[vi]):
                    ins = None
                    for kc in range(KC):
                        ins = e.matmul(bank(pbi), lhsT=cbb[:, kc, :], rhs=wt[:, kc, :], start=(kc == 0), stop=(kc == KC - 1))
                    return ins
                S.op("pe", f, reads=[wb, WG], writes=[PB[pbi]])
                S.op("dve", lambda e, j=j, pbi=pbi, gp_=gp_: e.tensor_tensor(out=gp_[:, j * 512:(j + 1) * 512], in0=bank(pbi), in1=pgt[:, j * 512:(j + 1) * 512], op=ALU.mult),
                     reads=[PB[pbi], WG], writes=[gpb_])
                S.op("dve", lambda e, j=j, gp_=gp_: e.tensor_tensor(out=gp_[:, j * 512:(j + 1) * 512], in0=gp_[:, j * 512:(j + 1) * 512], in1=bp[:, j * 512:(j + 1) * 512], op=ALU.add),
                     reads=[WG, gpb_], writes=[gpb_])

    def norm_transpose(xt, xb, junk, jb, st, stb, dstT, dcol, dbuf, gsv, shv, pbanks):
        S.op("act", lambda e: e.activation(out=junk[:], in_=xt[:], func=AF.Square, accum_out=st[:, 0:1]), reads=[xb], writes=[jb, stb])
        S.op("dve", lambda e: e.tensor_scalar(out=st[:, 1:2], in0=st[:, 0:1], scalar1=1.0 / D, scalar2=EPS, op0=ALU.mult, op1=ALU.add), reads=[stb], writes=[stb])
        S.op("act", lambda e: e.activation(out=st[:, 2:3], in_=st[:, 1:2], func=AF.Ln), reads=[stb], writes=[stb])
        S.op("act", lambda e: e.activation(out=st[:, 3:4], in_=st[:, 2:3], func=AF.Exp, scale=-0.5), reads=[stb], writes=[stb])
        S.op("dve", lambda e: e.tensor_scalar(out=xt[:], in0=xt[:], scalar1=st[:, 3:4], scalar2=None, op0=ALU.mult), reads=[stb, xb], writes=[xb])
        for g in range(4):
            pb = PB[pbanks[g]]

            def f(e, g=g):
                ins = None
                for q in range(4):
                    kc = g * 4 + q
                    ins = e.transpose(out=bank(pbanks[g], P, q * P), in_=xt[:, kc * P:(kc + 1) * P], identity=idf[:])
                return ins
            S.op("pe", f, reads=[xb, CONST], writes=[pb])
            for q in range(4):
                kc = g * 4 + q
                S.op("act", lambda e, g=g, q=q, kc=kc: e.activation(out=dstT[:, kc, dcol:dcol + P], in_=bank(pbanks[g], P, q * P), func=AF.Identity,
                                                                 scale=gsv[:, kc:kc + 1], bias=shv[:, kc:kc + 1]),
                     reads=[pb, CONST], writes=[dbuf])

    ada_part1(0)
    XO = R_M - R_W
    xts = [wtile(XO + i * 8 * KB, [P, D], F32, "xt") for i in range(2)]
    xbs = [Buf("xt0"), Buf("xt1")]
    junk = wtile(XO + 16 * KB, [P, D], BF16, "junk")
    JB = Buf("junk")
    stt = [wtile(XO + 20 * KB + i * 64, [P, 4], F32, "st") for i in range(2)]
    stb = [Buf("st0"), Buf("st1")]
    for t in range(14):
        i = t % 2
        src = x_d[t * P:(t + 1) * P, :] if t < 12 else ctx_d[(t - 12) * P:(t - 11) * P, :]
        S.dma("sp", xts[i][:], src, xbs[i], writes=[xbs[i]])
        mv = modv[0]
        gsv, shv = (mv[1], mv[0]) if t < 12 else (mv[3], mv[2])
        norm_transpose(xts[i], xbs[i], junk, JB, stt[i], stb[i], hT, t * P, HT, gsv, shv, [0, 1, 2, 3] if i == 0 else [4, 5, 6, 3])
    S.barrier()
    X1B = Buf("x1")
    OUTB = Buf("out")
    if stop == "pro":
        S.dma("sp", dbg["hT"], hT[:], DBGB, reads=[HT], writes=[DBGB])
        for l_ in range(2):
            for v_ in range(4):
                S.dma("sp", dbg["modv"][l_ * 4 + v_], modv[l_][v_][:], DBGB, reads=[CONST], writes=[DBGB])
        S.dma("sp", dbg["lbv"], lbv[:], DBGB, reads=[CONST], writes=[DBGB])
        return finish()

    TBLK = [(0, 512, 0), (512, 512, 512), (1536, 256, 1024)]
    av = wtile(0, [P, 14, 256], BF16, "av")
    AVB = Buf("av")
    pooled = wtile(7 * KB, [P, 2, 1280], BF16, "pooled")
    PLB = Buf("pooled")
    pmt = wtile(12 * KB, [P, NPMB, P], BF16, "pmt")
    PMB = Buf("pm")
    invc = wtile(12 * KB + NPMB * 256, [P, 1280], F32, "invc")
    IVB = Buf("invc")
    sag_o = 12 * KB + NPMB * 256 + 5 * KB
    sag = [wtile(sag_o + i * 2 * KB, [P, 512], F32, "sag") for i in range(2)]
    SGB = [Buf("sag0"), Buf("sag1")]
    assert sag_o + 4 * KB <= W_LIMIT, (sag_o, W_LIMIT)
    for gi in range(4):
        r = RAD[gi]
        wt, wb = ring.load(("w0", gi), 512)
        for q3 in range(3):
            S.dma("pool", pmt[:, q3 * 7:(q3 + 1) * 7, :], pm_d[gi, q3 * 7:(q3 + 1) * 7].rearrange("b p q -> p b q"), PMB, writes=[PMB])
        S.dma("sp", invc[:], invc_d[gi], IVB, writes=[IVB])
        tiles = list(range(8 + r)) + [12, 13]
        for n_, t in enumerate(tiles):
            pbi = (n_ // 2) % 4
            half = n_ % 2

            def f(e, t=t, pbi=pbi, half=half, wt=wt):
                ins = None
                for kc in range(KC):
                    ins = e.matmul(bank(pbi, 256, half * 256), lhsT=hT[:, kc, t * P:(t + 1) * P], rhs=wt[:, kc, 0:256],
                                   start=(kc == 0), stop=(kc == KC - 1))
                return ins
            S.op("pe", f, reads=[wb, HT], writes=[PB[pbi]])
            eng = "act" if n_ % 2 == 0 else "dve"
            if eng == "act":
                S.op("act", lambda e, t=t, pbi=pbi, half=half: e.activation(out=av[:, t, :], in_=bank(pbi, 256, half * 256), func=AF.Copy),
                     reads=[PB[pbi]], writes=[AVB])
            else:
                S.op("dve", lambda e, t=t, pbi=pbi, half=half: e.tensor_copy(out=av[:, t, :], in_=bank(pbi, 256, half * 256)),
                     reads=[PB[pbi]], writes=[AVB])
        for ch in range(2):
            for jg in range(2):
                pbi = 4 + (ch * 2 + jg) % 2

                def f(e, ch=ch, jg=jg, pbi=pbi):
                    ins = None
                    for jj in range(4):
                        j = jg * 4 + jj
                        rels = [rel for rel in range(-r, r + 1) if 0 <= j + rel]
                        for n2, rel in enumerate(rels):
                            blk = (9 + j) if rel == 0 else (rel + 4)
                            ins = e.matmul(bank(pbi, P, jj * P), lhsT=av[:, j + rel, ch * P:(ch + 1) * P], rhs=pmt[:, blk, :],
                                           start=(n2 == 0), stop=(n2 == len(rels) - 1))
                    return ins
                S.op("pe", f, reads=[AVB, PMB], writes=[PB[pbi]])
                S.op("dve", lambda e, ch=ch, jg=jg, pbi=pbi: e.tensor_tensor(out=pooled[:, ch, jg * 512:(jg + 1) * 512], in0=bank(pbi),
                                                                          in1=invc[:, jg * 512:(jg + 1) * 512], op=ALU.mult),
                     reads=[PB[pbi], IVB], writes=[PLB])

            def fc(e, ch=ch):
                ins = None
                for jc in range(2):
                    for ic in range(2):
                        ins = e.matmul(bank(6, P, jc * P), lhsT=av[:, 12 + ic, ch * P:(ch + 1) * P], rhs=pmt[:, 17 + jc * 2 + ic, :],
                                       start=(ic == 0), stop=(ic == 1))
                return ins
            S.op("pe", fc, reads=[AVB, PMB], writes=[PB[6]])
            S.op("dve", lambda e, ch=ch: e.tensor_tensor(out=pooled[:, ch, 1024:1280], in0=bank(6, 256), in1=invc[:, 1024:1280], op=ALU.mult),
                 reads=[PB[6], IVB], writes=[PLB])
        n3 = 0
        for oc in range(2):
            for (hc0, n, mc0) in TBLK:
                pm_, pg_ = (0, 1) if n3 % 2 == 0 else (2, 3)
                si = n3 % 2
                n3 += 1

                def fm(e, oc=oc, mc0=mc0, n=n, pm_=pm_):
                    ins = None
                    for ic in range(2):
                        ins = e.matmul(bank(pm_, n), lhsT=poolw[:, gi, ic, oc * P:(oc + 1) * P], rhs=pooled[:, ic, mc0:mc0 + n],
                                       start=(ic == 0), stop=(ic == 1))
                    return ins
                S.op("pe", fm, reads=[PLB, CONST], writes=[PB[pm_]])

                def fg(e, oc=oc, hc0=hc0, n=n, pg_=pg_, wt=wt):
                    ins = None
                    for kc in range(KC):
                        ins = e.matmul(bank(pg_, n), lhsT=wt[:, kc, 256 + oc * P:256 + (oc + 1) * P], rhs=hT[:, kc, hc0:hc0 + n],
                                       start=(kc == 0), stop=(kc == KC - 1))
                    return ins
                S.op("pe", fg, reads=[wb, HT], writes=[PB[pg_]])
                S.op("act", lambda e, n=n, pg_=pg_, si=si: e.activation(out=sag[si][:, 0:n], in_=bank(pg_, n), func=AF.Silu),
                     reads=[PB[pg_]], writes=[SGB[si]])
                S.op("dve", lambda e, oc=oc, mc0=mc0, n=n, pm_=pm_, si=si: e.scalar_tensor_tensor(
                    out=midT[:, gi * 2 + oc, mc0:mc0 + n], in0=bank(pm_, n), scalar=pscT[:, gi * 2 + oc:gi * 2 + oc + 1], in1=sag[si][:, 0:n],
                    op0=ALU.mult, op1=ALU.mult), reads=[PB[pm_], SGB[si], CONST], writes=[MT])
    S.barrier()
    ada_part1(1)

    ub = wtile(0, [P, 1026], F32, "ub")
    ubc = wtile(4128, [P, 258], F32, "ubc")
    UB = Buf("ub")
    bx = [wtile(6 * KB + i * 2 * KB, [P, 512], F32, "bx") for i in range(2)]
    BXB = [Buf("bx0"), Buf("bx1")]
    cv = [wtile(10 * KB + i * 2 * KB, [P, 512], F32, "cv") for i in range(2)]
    CVB = [Buf("cv0"), Buf("cv1")]
    sg0 = [wtile(14 * KB + i * 2 * KB, [P, 512], F32, "sg") for i in range(2)]
    SG0 = [Buf("sg0"), Buf("sg1")]
    S.op("dve", lambda e: e.memset(ub[:, 0:1], 0.0), writes=[UB])
    S.op("dve", lambda e: e.memset(ubc[:, 0:1], 0.0), writes=[UB])
    S.op("dve", lambda e: e.memset(ubc[:, 257:258], 0.0), writes=[UB])
    UBLK = [(0, 512, 1, ub), (512, 512, 513, ub), (1024, 1, 1025, ub), (1536, 256, 1, ubc)]
    for cb in range(8):
        wt, wb = ring.load(("w0", 4 + cb), 512)
        for n_, (hc0, n, uc0, ut) in enumerate(UBLK):
            px, pc = (0, 1) if n_ % 2 == 0 else (2, 3)
            si = n_ % 2

            def f(e, hc0=hc0, n=n, px=px, pc=pc, wt=wt):
                ins = None
                for (pbk, c0) in ((px, 0), (pc, 256)):
                    for kc in range(KC):
                        ins = e.matmul(bank(pbk, n), lhsT=wt[:, kc, c0:c0 + P], rhs=hT[:, kc, hc0:hc0 + n], start=(kc == 0), stop=(kc == KC - 1))
                return ins
            S.op("pe", f, reads=[wb, HT], writes=[PB[px], PB[pc]])
            S.op("act", lambda e, n=n, px=px, si=si: e.activation(out=bx[si][:, 0:n], in_=bank(px, n), func=AF.Copy), reads=[PB[px]], writes=[BXB[si]])
            S.op("dve", lambda e, n=n, pc=pc, si=si, uc0=uc0, ut=ut: e.tensor_tensor(out=ut[:, uc0:uc0 + n], in0=bank(pc, n), in1=bx[si][:, 0:n], op=ALU.mult),
                 reads=[PB[pc], BXB[si]], writes=[UB])
        for n_, (hc0, n, mc0) in enumerate(TBLK):
            pbb, pgg = (4, 5) if n_ % 2 == 0 else (6, 7)
            si = n_ % 2
            ut, u0 = (ub, hc0) if n_ < 2 else (ubc, 0)

            def f(e, hc0=hc0, n=n, pbb=pbb, pgg=pgg, wt=wt):
                ins = None
                for (pbk, c0) in ((pbb, 128), (pgg, 384)):
                    for kc in range(KC):
                        ins = e.matmul(bank(pbk, n), lhsT=wt[:, kc, c0:c0 + P], rhs=hT[:, kc, hc0:hc0 + n], start=(kc == 0), stop=(kc == KC - 1))
                return ins
            S.op("pe", f, reads=[wb, HT], writes=[PB[pbb], PB[pgg]])
            S.op("act", lambda e, n=n, pgg=pgg, si=si: e.activation(out=sg0[si][:, 0:n], in_=bank(pgg, n), func=AF.Silu), reads=[PB[pgg]], writes=[SG0[si]])
            S.op("dve", lambda e, n=n, si=si, ut=ut, u0=u0: e.tensor_scalar(out=cv[si][:, 0:n], in0=ut[:, u0 + 1:u0 + 1 + n], scalar1=cwT[:, cb, 1:2], scalar2=cbT[:, cb:cb + 1],
                                                                      op0=ALU.mult, op1=ALU.add), reads=[UB, CONST], writes=[CVB[si]])
            S.op("dve", lambda e, n=n, si=si, ut=ut, u0=u0: e.scalar_tensor_tensor(out=cv[si][:, 0:n], in0=ut[:, u0:u0 + n], scalar=cwT[:, cb, 0:1], in1=cv[si][:, 0:n],
                                                                             op0=ALU.mult, op1=ALU.add), reads=[UB, CONST, CVB[si]], writes=[CVB[si]])
            S.op("dve", lambda e, n=n, si=si, ut=ut, u0=u0: e.scalar_tensor_tensor(out=cv[si][:, 0:n], in0=ut[:, u0 + 2:u0 + 2 + n], scalar=cwT[:, cb, 2:3], in1=cv[si][:, 0:n],
                                                                             op0=ALU.mult, op1=ALU.add), reads=[UB, CONST, CVB[si]], writes=[CVB[si]])
            S.op("dve", lambda e, n=n, si=si, pbb=pbb: e.tensor_tensor(out=cv[si][:, 0:n], in0=bank(pbb, n), in1=cv[si][:, 0:n], op=ALU.mult),
                 reads=[PB[pbb], CVB[si]], writes=[CVB[si]])
            S.op("dve", lambda e, n=n, si=si, mc0=mc0: e.tensor_tensor(out=midT[:, 8 + cb, mc0:mc0 + n], in0=cv[si][:, 0:n], in1=sg0[si][:, 0:n], op=ALU.mult),
                 reads=[CVB[si], SG0[si]], writes=[MT])
    S.barrier()

    gp = wtile(0, [P, D], F32, "gp")
    GPB = Buf("gp")
    xt = wtile(8 * KB, [P, D], F32, "xt")
    XB = Buf("xt")
    xt2 = wtile(24 * KB, [P, D], F32, "xt2")
    XB2 = Buf("xt2")
    tmp = wtile(16 * KB, [P, D], F32, "tmp")
    TB = Buf("tmp")
    junk2 = wtile(16 * KB, [P, D], BF16, "junk2")
    JB2 = Buf("junk2")
    st2 = wtile(32 * KB, [P, 8], F32, "st2")
    SB2 = Buf("st2")
    st3 = wtile(32 * KB + 64, [P, 8], F32, "st3")
    SB3 = Buf("st3")
    XTS = [(xt, XB, st2, SB2), (xt2, XB2, st3, SB3)]
    FIN_TICKS = {}
    assert 32 * KB + 128 <= W_LIMIT

    def epilogue_tiles(layer, srcT, SRCB, wo_d, tiles, gpwhich, final):
        wts = [ring.load(("wo", layer, j), 512, hold=4) for j in range(4)]
        pend = None
        for n_, (c0, xsrc, xdst, nxt, dcol, gp, GPB) in enumerate(tiles):
            xt_, XB_, st_, SB_ = XTS[n_ % 2]
            S.dma("sp", xt_[:], xsrc, XB_, writes=[XB_])
            for j in range(4):
                wt, wb = wts[j]

                def f(e, j=j, wt=wt, c0=c0):
                    ins = None
                    for kc in range(KC):
                        ins = e.matmul(bank(j), lhsT=srcT[:, kc, c0:c0 + P], rhs=wt[:, kc, :], start=(kc == 0), stop=(kc == KC - 1))
                    return ins
                S.op("pe", f, reads=[wb, SRCB], writes=[PB[j]])
            if pend is not None:
                pend()
                pend = None
            yps = psflat[:, 0:D]
            S.op("act", lambda e: e.activation(out=junk2[:], in_=yps, func=AF.Square, accum_out=st_[:, 0:1]), reads=PB[0:4], writes=[TB, SB_])
            S.op("dve", lambda e: e.tensor_scalar(out=st_[:, 1:2], in0=st_[:, 0:1], scalar1=1.0 / D, scalar2=EPS, op0=ALU.mult, op1=ALU.add), reads=[SB_], writes=[SB_])
            S.op("act", lambda e: e.activation(out=st_[:, 2:3], in_=st_[:, 1:2], func=AF.Ln), reads=[SB_], writes=[SB_])
            S.op("act", lambda e: e.activation(out=st_[:, 3:4], in_=st_[:, 2:3], func=AF.Exp, scale=-0.5), reads=[SB_], writes=[SB_])
            S.op("dve", lambda e: e.scalar_tensor_tensor(out=tmp[:], in0=yps, scalar=st_[:, 3:4], in1=gp[:], op0=ALU.mult, op1=ALU.mult),
                 reads=PB[0:4] + [SB_, GPB], writes=[TB])
            S.op("dve", lambda e: e.tensor_tensor(out=xt_[:], in0=xt_[:], in1=tmp[:], op=ALU.add), reads=[TB, XB_], writes=[XB_])
            if xdst is not None:
                tk = S.dma("sp", xdst, xt_[:], XB_, reads=[XB_], writes=[X1B if not final else OUTB])
                FIN_TICKS[id(tk[0])] = tk
            if nxt is not None:
                pend = (lambda xt_=xt_, XB_=XB_, st_=st_, SB_=SB_, nxt=nxt, dcol=dcol:
                        norm_transpose(xt_, XB_, junk2, TB, st_[:, 4:8], SB_, h1T, dcol, HT, nxt[0], nxt[1], [4, 5, 6, 7]))
        if pend is not None:
            pend()

    if stop == "mid":
        S.dma("sp", dbg["midT"], midT[:], DBGB, reads=[MT], writes=[DBGB])
        return finish()
    mv1 = modv[1]
    gpc = A.at(R_H + 40 * KB, [P, D], F32, "gpc")
    GPCB = Buf("gpc")
    ada_gate2(0, [(0, gp, GPB), (1, gpc, GPCB)])
    S.barrier()
    tl = [(t * P, x_d[t * P:(t + 1) * P, :], x1_d[t * P:(t + 1) * P, :], (mv1[1], mv1[0]), t * P, gp, GPB) for t in range(8)]
    tl += [(1024 + t * P, ctx_d[t * P:(t + 1) * P, :], None, (mv1[3], mv1[2]), 1024 + t * P, gpc, GPCB) for t in range(2)]
    epilogue_tiles(0, midT, MT, wo0_d, tl, 0, False)
    S.barrier()

    if stop == "l0":
        S.dma("sp", dbg["h1T"], h1T[:], DBGB, reads=[HT], writes=[DBGB])
        return finish()
    ring.config(2, 640)
    regions = [[R_W, SB_END], [R_RING + 40 * KB, R_RING + 64 * KB], [R_H + 40 * KB, R_H + 56 * KB], [R_M + 32 * KB, R_M + 40 * KB]]

    def wa(shape, dt, name):
        nb = int(np.prod(shape[1:])) * (2 if dt == BF16 else 4)
        nb = (nb + 63) // 64 * 64
        for rg in regions:
            if rg[0] + nb <= rg[1]:
                t = A.at(rg[0], shape, dt, name)
                rg[0] += nb
                return t
        raise AssertionError("heads work region full: " + name)

    T1 = wa([P, 1280], F32, "T1")
    T2 = wa([P, 1280], F32, "T2")
    T3 = wa([P, 1280], F32, "T3")
    TB1, TB2, TB3 = Buf("T1"), Buf("T2"), Buf("T3")
    kinvT = [wa([P, 1280], BF16, "kinvT1"), wa([P, 1024], BF16, "kinvT2")]
    KIB = [Buf("kinvT1"), Buf("kinvT2")]
    qdT = [wa([P, 1024], BF16, "qdT1"), wa([P, 1024], BF16, "qdT2")]
    QDB = [Buf("qdT1"), Buf("qdT2")]
    kinv = [wa([P, 10, P], BF16, "kinv1"), wa([P, 8, P], BF16, "kinv2")]
    KTB = [Buf("kinv1"), Buf("kinv2")]
    vt = wa([P, 10, P], BF16, "vt")
    VB = Buf("vt")
    sgt = wa([P, 8, P], F32, "sgt")
    SGT = Buf("sgt")
    dec = [wa([P, 10], F32, "dec1"), wa([P, 8], F32, "dec2")]
    DCB = [Buf("dec1"), Buf("dec2")]
    scm = [wa([P, 512], BF16, "scm0"), wa([P, 512], BF16, "scm1")]
    SCB = [Buf("scm0"), Buf("scm1")]
    Sall = wa([P, 11, P], F32, "Sall")
    dsd = wa([P, 10, P], F32, "dsd")
    Sball = wa([P, 10, P], BF16, "Sball")
    vTs = wa([P, 1280], BF16, "vTs")
    VTB = Buf("vTs")
    DSB = Buf("dsd")
    SBB = Buf("Sball")
    Sg = wa([P, 2, P], F32, "Sg")
    SGX = Buf("Sg")
    STB = Buf("S")
    o1 = wa([P, 8, P], F32, "o1")
    O1B = Buf("o1")
    ot = wa([P, 8, P], F32, "ot")
    OTB = Buf("ot")
    rs = wa([P, 8, 4], F32, "rs")
    RSB = Buf("rs")
    OGB = MT
    SETS = [(kinvT, KIB, qdT, QDB, kinv, KTB, vt, VB, sgt, SGT, dec, DCB),
            ([wa([P, 1280], BF16, "kinvT1b"), wa([P, 1024], BF16, "kinvT2b")], [Buf("kinvT1b"), Buf("kinvT2b")],
             [wa([P, 1024], BF16, "qdT1b"), wa([P, 1024], BF16, "qdT2b")], [Buf("qdT1b"), Buf("qdT2b")],
             [wa([P, 10, P], BF16, "kinv1b"), wa([P, 8, P], BF16, "kinv2b")], [Buf("kinv1b"), Buf("kinv2b")],
             wa([P, 10, P], BF16, "vtb"), Buf("vtb"), wa([P, 8, P], F32, "sgtb"), Buf("sgtb"),
             [wa([P, 10], F32, "dec1b"), wa([P, 8], F32, "dec2b")], [Buf("dec1b"), Buf("dec2b")])]

    PEND = [None, 0]
    CUR = [0]

    def use(par):
        nonlocal kinvT, KIB, qdT, QDB, kinv, KTB, vt, VB, sgt, SGT, dec, DCB
        (kinvT, KIB, qdT, QDB, kinv, KTB, vt, VB, sgt, SGT, dec, DCB) = SETS[par]

    def gates(hd, d_, zps_banks, ncol):
        zv = psflat[:, zps_banks[0] * 512: zps_banks[0] * 512 + ncol]
        zb = [PB[b] for b in zps_banks]
        oml = lbv[:, hd, 2 * d_:2 * d_ + 1]
        lb = lbv[:, hd, 2 * d_ + 1:2 * d_ + 2]
        noml = nlbv[:, hd, d_:d_ + 1]
        nch = ncol // P
        S.op("act", lambda e: e.activation(out=T1[:, 0:ncol], in_=zv, func=AF.Sigmoid), reads=zb, writes=[TB1])
        S.op("act", lambda e: e.activation(out=T2[:, 0:ncol], in_=T1[:, 0:ncol], func=AF.Ln, scale=oml, bias=lb), reads=[TB1, CONST], writes=[TB2])
        S.op("dve", lambda e: e.tensor_scalar(out=T3[:, 0:ncol], in0=T1[:, 0:ncol], scalar1=noml, scalar2=oml, op0=ALU.mult, op1=ALU.add),
             reads=[TB1, CONST], writes=[TB3])
        S.op("dve", lambda e: e.tensor_tensor_scan(out=T1[:, 0:ncol], data0=rmask[:, 0:ncol], data1=T2[:, 0:ncol], initial=0.0, op0=ALU.mult, op1=ALU.add),
             reads=[TB2, CONST, TB3], writes=[TB1])
        BC, BCB, EX, EXB = T1, TB1, T2, TB2
        if d_ == 1:
            S.op("dve", lambda e: e.tensor_tensor(out=T2[:, 0:ncol], in0=T2[:, 0:ncol], in1=T1[:, 0:ncol], op=ALU.subtract), reads=[TB1, TB2], writes=[TB2])
            t1v = T1[:, 0:ncol].rearrange("p (a b) -> p a b", b=P)
            t2v = T2[:, 0:ncol].rearrange("p (a b) -> p a b", b=P)
            S.op("dve", lambda e: e.tensor_tensor(out=t2v, in0=t2v, in1=t1v[:, :, P - 1:P].broadcast_to([P, nch, P]), op=ALU.add), reads=[TB1, TB2], writes=[TB2])
            BC, BCB, EX, EXB = T2, TB2, T1, TB1
        S.op("act", lambda e: e.activation(out=EX[:, 0:ncol], in_=BC[:, 0:ncol], func=AF.Exp, scale=-1.0), reads=[BCB], writes=[EXB])
        S.op("dve", lambda e: e.tensor_tensor(out=kinvT[d_][:, 0:ncol], in0=T3[:, 0:ncol], in1=EX[:, 0:ncol], op=ALU.mult), reads=[EXB, TB3], writes=[KIB[d_]])
        S.op("act", lambda e: e.activation(out=EX[:, 0:ncol], in_=BC[:, 0:ncol], func=AF.Exp), reads=[BCB, KIB[d_]], writes=[EXB])
        e2v = EX[:, 0:ncol].rearrange("p (a b) -> p a b", b=P)
        col = P - 1 if d_ == 0 else 0
        S.op("dve", lambda e: e.tensor_copy(out=dec[d_][:, 0:nch], in_=e2v[:, :, col]), reads=[EXB], writes=[DCB[d_]])
        return EX, EXB

    def head(hd):
        wt, wb = ring.load(("w1", hd), 640)

        def proj_fm(c0, banks_, cols):
            for (pbk, off, hc0, n) in banks_:
                def f(e, pbk=pbk, off=off, hc0=hc0, n=n):
                    ins = None
                    for kc in range(KC):
                        ins = e.matmul(bank(pbk, n, off), lhsT=wt[:, kc, c0:c0 + P], rhs=h1T[:, kc, hc0:hc0 + n], start=(kc == 0), stop=(kc == KC - 1))
                    return ins
                S.op("pe", f, reads=[wb, HT], writes=[PB[pbk]])

        proj_fm(0, [(0, 0, 0, 512), (1, 0, 512, 512), (2, 0, 1024, 256)], 1280)
        yield
        EX, EXB = gates(hd, 0, [0, 1, 2], 1280)
        yield
        proj_fm(256, [(3, 0, 0, 512), (4, 0, 512, 512)], 1024)
        qv = psflat[:, 3 * 512: 3 * 512 + 1024]
        S.op("dve", lambda e: e.tensor_tensor(out=qdT[0][:], in0=qv, in1=EX[:, 0:1024], op=ALU.mult), reads=[PB[3], PB[4], EXB], writes=[QDB[0]])
        if full:
            yield
            proj_fm(128, [(0, 0, 0, 512), (1, 0, 512, 512)], 1024)
            EX2, EXB2 = gates(hd, 1, [0, 1], 1024)
            S.op("dve", lambda e: e.tensor_tensor(out=qdT[1][:], in0=qv, in1=EX2[:, 0:1024], op=ALU.mult), reads=[PB[3], PB[4], EXB2], writes=[QDB[1]])
        if DBG["hstage"] < 0.25:
            return
        yield
        proj_fm(384, [(2, 0, 0, 512), (3, 0, 512, 512), (4, 0, 1024, 256)], 1280)
        vps = psflat[:, 2 * 512: 2 * 512 + 1280]
        S.op("act", lambda e: e.activation(out=vTs[:], in_=vps, func=AF.Copy), reads=[PB[2], PB[3], PB[4]], writes=[VTB])
        yield

        def fvt(e):
            ins = None
            for t in range(10):
                ins = e.transpose(out=psbf[:, t * P:(t + 1) * P], in_=vTs[:, t * P:(t + 1) * P], identity=idb[:])
            return ins
        S.op("pe", fvt, reads=[VTB, CONST], writes=[PB[0], PB[1]])
        S.op("dve", lambda e: e.tensor_copy(out=vt[:].rearrange("p a b -> p (a b)"), in_=psbf[:, 0:1280]), reads=[PB[0], PB[1]], writes=[VB])
        if full:
            yield
            proj_fm(512, [(2, 0, 0, 512), (3, 0, 512, 512)], 1024)
            S.op("act", lambda e: e.activation(out=sgt[:].rearrange("p a b -> p (a b)"), in_=psflat[:, 2 * 512: 2 * 512 + 1024], func=AF.Silu),
                 reads=[PB[2], PB[3]], writes=[SGT])
        if DBG["hstage"] < 0.35:
            return
        for d_ in range(2 if full else 1):
            nch = 10 if d_ == 0 else 8
            for g0 in range(0, nch, 8):
                gn = min(8, nch - g0)
                pbk = 3 + d_

                def f(e, d_=d_, g0=g0, gn=gn, pbk=pbk):
                    ins = None
                    for q in range(gn):
                        c = g0 + q
                        ins = e.transpose(out=psbf[:, pbk * 1024 + q * P: pbk * 1024 + (q + 1) * P], in_=kinvT[d_][:, c * P:(c + 1) * P], identity=idb[:])
                    return ins
                S.op("pe", f, reads=[KIB[d_], CONST], writes=[PB[pbk]])
                S.op("act", lambda e, d_=d_, g0=g0, gn=gn, pbk=pbk: e.activation(
                    out=kinv[d_][:, g0:g0 + gn, :].rearrange("p a b -> p (a b)"), in_=psbf[:, pbk * 1024: pbk * 1024 + gn * P], func=AF.Copy),
                    reads=[PB[pbk]], writes=[KTB[d_]])

    def head2(hd):
        CUR[0] = hd % 2
        use(hd % 2)

        def scan(d_, order, s_init):
            mk = mk1 if d_ == 0 else mk2
            n = len(order)
            if s_init is None:
                S.op("dve", lambda e: e.memset(Sall[:, 0, :], 0.0), writes=[STB])
            else:
                s_init()
            lat = [c for c in order if c < 8]
            for g0 in range(0, n, 4):
                cs = order[g0:g0 + 4]

                def fs(e, cs=cs):
                    ins = None
                    for q, c in enumerate(cs):
                        ins = e.matmul(bank(7, P, q * P), lhsT=kinv[d_][:, c, :], rhs=vt[:, c, :], start=True, stop=True)
                    return ins
                S.op("pe", fs, reads=[KTB[d_], VB], writes=[PB[7]])
                for q, c in enumerate(cs):
                    S.op("act", lambda e, q=q, c=c, i=g0 + q: e.activation(out=dsd[:, i, :], in_=bank(7, P, q * P), func=AF.Identity, scale=dec[d_][:, c:c + 1]),
                         reads=[PB[7], DCB[d_]], writes=[DSB])
                step()
            for g in range(2):
                cs = lat[g * 4:(g + 1) * 4]
                pbk = 5 + g

                def f(e, cs=cs, pbk=pbk):
                    ins = None
                    for q, c in enumerate(cs):
                        ins = e.matmul(bank(pbk, P, q * P), lhsT=kinvT[d_][:, c * P:(c + 1) * P], rhs=qdT[d_][:, c * P:(c + 1) * P], start=True, stop=True)
                    return ins
                S.op("pe", f, reads=[KIB[d_], QDB[d_]], writes=[PB[pbk]])
                S.op("dve", lambda e, g=g, pbk=pbk: e.tensor_tensor(out=scm[g][:].rearrange("p (a b) -> p a b", b=P), in0=bank(pbk).rearrange("p (a b) -> p a b", b=P),
                                                                in1=mk[:].unsqueeze(1).broadcast_to([P, 4, P]), op=ALU.mult),
                     reads=[PB[pbk], CONST], writes=[SCB[g]])
            for i, c in enumerate(order):
                S.op("dve", lambda e, i=i, c=c: e.scalar_tensor_tensor(out=Sall[:, i + 1, :], in0=Sall[:, i, :], scalar=dec[d_][:, c:c + 1], in1=dsd[:, i, :],
                                                                   op0=ALU.mult, op1=ALU.add), reads=[STB, DSB, DCB[d_]], writes=[STB])
            S.op("act", lambda e: e.activation(out=Sball[:, 0:n, :].rearrange("p a b -> p (a b)"), in_=Sall[:, 0:n, :].rearrange("p a b -> p (a b)"), func=AF.Copy),
                 reads=[STB], writes=[SBB])
            step()
            for n_, c in enumerate(lat):
                i = order.index(c)
                g, q = divmod(n_, 4)
                pbo = 5 + (n_ % 2)

                def fo(e, c=c, g=g, q=q, pbo=pbo, i=i):
                    e.matmul(bank(pbo, P), lhsT=scm[g][:, q * P:(q + 1) * P], rhs=vt[:, c, :], start=True, stop=False)
                    return e.matmul(bank(pbo, P), lhsT=qdT[d_][:, c * P:(c + 1) * P], rhs=Sball[:, i, :], start=False, stop=True)
                S.op("pe", fo, reads=[SCB[g], VB, QDB[d_], SBB], writes=[PB[pbo]])
                if d_ == 0:
                    S.op("act", lambda e, c=c, pbo=pbo: e.activation(out=o1[:, c, :], in_=bank(pbo, P), func=AF.Copy), reads=[PB[pbo]], writes=[O1B])
                else:
                    S.op("dve", lambda e, c=c, pbo=pbo: e.tensor_tensor(out=ot[:, c, :], in0=bank(pbo, P), in1=o1[:, c, :], op=ALU.add), reads=[PB[pbo], O1B], writes=[OTB])
                step()

        if DBG["hstage"] < 2:
            return
        scan(0, [8, 9, 0, 1, 2, 3, 4, 5, 6, 7], None)
        if DBG["hstage"] < 3:
            return
        if mode == "A":
            S.dma("sp", sout_d[hd], Sall[:, 10, :], STB, reads=[STB], writes=[OUTB])
            return
        if mode == "F":
            AGB = Buf("agin")
            S.dma("sp", agin[hd][:, :], Sall[:, 10, :], STB, reads=[STB], writes=[AGB])
            S._need("pool", S._deps([AGB], []))
            ccs = es.enter_context(nc.semaphore("cc%d" % hd))
            nc.gpsimd.collective_compute("AllGather", ALU.bypass, replica_groups=[[2 * i_, 2 * i_ + 1] for i_ in range(DBG.get("ncores", 8) // 2)],
                                         ins=[agin[hd].ap().opt()], outs=[agout[hd].ap().opt()]).then_inc(ccs, 1)

            def init2():
                nc.sync.wait_ge(ccs, 1)
                S.dma("sp", Sg[:], agout[hd].ap().rearrange("(r p) v -> p r v", p=P), SGX, reads=[], writes=[SGX])
                S.op("dve", lambda e: e.tensor_scalar(out=Sall[:, 0, :], in0=Sg[:, 0, :], scalar1=psel[:, 0:1], scalar2=None, op0=ALU.mult), reads=[SGX, CONST], writes=[STB])
                S.op("dve", lambda e: e.scalar_tensor_tensor(out=Sall[:, 0, :], in0=Sg[:, 1, :], scalar=psel[:, 1:2], in1=Sall[:, 0, :], op0=ALU.mult, op1=ALU.add),
                     reads=[SGX, CONST, STB], writes=[STB])
            if DBG["hstage"] < 3.5:
                init2()
                return
            scan(1, [7, 6, 5, 4, 3, 2, 1, 0], init2)
        else:
            scan(1, [7, 6, 5, 4, 3, 2, 1, 0], lambda: S.dma("sp", Sall[:, 0, :], sin_d[hd], STB, writes=[STB]))
        if DBG["hstage"] < 4:
            return
        S.op("dve", lambda e: e.tensor_tensor(out=o1[:], in0=ot[:], in1=ot[:], op=ALU.mult), reads=[OTB, O1B], writes=[O1B])
        S.op("dve", lambda e: e.tensor_reduce(out=rs[:, :, 0], in_=o1[:], axis=AX.X, op=ALU.add), reads=[O1B], writes=[RSB])
        S.op("dve", lambda e: e.tensor_scalar(out=rs[:, :, 1], in0=rs[:, :, 0], scalar1=1.0 / P, scalar2=EPS, op0=ALU.mult, op1=ALU.add), reads=[RSB], writes=[RSB])
        S.op("act", lambda e: e.activation(out=rs[:, :, 2], in_=rs[:, :, 1], func=AF.Ln), reads=[RSB], writes=[RSB])
        S.op("act", lambda e: e.activation(out=rs[:, :, 3], in_=rs[:, :, 2], func=AF.Exp, scale=-0.5), reads=[RSB], writes=[RSB])
        S.op("dve", lambda e: e.tensor_tensor(out=ot[:], in0=ot[:], in1=rs[:, :, 3:4].broadcast_to([P, 8, P]), op=ALU.mult), reads=[RSB, OTB], writes=[OTB])
        for g in range(2):
            pbk = 5 + g

            def f(e, g=g, pbk=pbk):
                ins = None
                for q in range(4):
                    ins = e.transpose(out=bank(pbk, P, q * P), in_=ot[:, g * 4 + q, :], identity=idf[:])
                return ins
            S.op("pe", f, reads=[OTB, CONST], writes=[PB[pbk]])
            S.op("dve", lambda e, g=g, pbk=pbk: e.scalar_tensor_tensor(out=ogT[:, hd, g * 512:(g + 1) * 512], in0=bank(pbk), scalar=ongT[:, hd:hd + 1],
                                                                   in1=sgt[:].rearrange("p a b -> p (a b)")[:, g * 512:(g + 1) * 512], op0=ALU.mult, op1=ALU.mult),
                 reads=[PB[pbk], CONST, SGT], writes=[OGB])

    def step():
        if PEND[0] is not None:
            use(PEND[1])
            try:
                next(PEND[0])
            except StopIteration:
                PEND[0] = None
            use(CUR[0])

    def exhaust():
        while PEND[0] is not None:
            step()

    PEND[0], PEND[1] = head(0), 0
    exhaust()
    for hd in range(DBG["nheads"]):
        if hd + 1 < DBG["nheads"]:
            PEND[0], PEND[1] = head(hd + 1), (hd + 1) % 2
        head2(hd)
        exhaust()
    S.barrier()
    if stop == "heads":
        if not DBG.get("nodump"):
            S.dma("sp", dbg["ogT"], ogT[:], DBGB, reads=[OGB], writes=[DBGB])
        return finish()

    if full:
        ring.config(4, 512)
        ada_gate2(1, [(0, gp, GPB)])
        S.barrier()
        tl = [(t * P, x1_d[t * P:(t + 1) * P, :], out_d[t * P:(t + 1) * P, :], None, 0, gp, GPB) for t in range(8)]
        S._need("sp", dict(FIN_TICKS))
        epilogue_tiles(1, ogT, OGB, wo1_d, tl, 0, True)
    return finish()


def _pool_consts(s):
    pm = np.zeros((4, NPMB, P, P), np.float32)
    invc = np.zeros((4, P, NOWN + NCTX), np.float32)

    def g_rc(R, C):
        return (R, C) if s == 0 else (31 - R, 63 - C)

    for gi, w in enumerate(WINS):
        lo, hi = w // 2, w // 2 - 1
        r = RAD[gi]

        def inwin(go, gin):
            return go - lo <= gin <= go + hi

        def cnt1(g, n):
            return min(g + hi, n - 1) - max(g - lo, 0) + 1
        for rel in range(-r, r + 1):
            if rel == 0:
                continue
            B = np.zeros((P, P), np.float32)
            for oi in range(P):
                Ro, Co = 12 + oi // 64, oi % 64
                gro, gco = g_rc(Ro, Co)
                for ii in range(P):
                    Ri, Ci = 12 + 2 * rel + ii // 64, ii % 64
                    gri, gci = g_rc(Ri, Ci)
                    if inwin(gro, gri) and inwin(gco, gci):
                        B[ii, oi] = 1.0
            pm[gi, rel + 4] = B
        for j in range(8):
            Dm = np.zeros((P, P), np.float32)
            for oi in range(P):
                Ro, Co = 2 * j + oi // 64, oi % 64
                gro, gco = g_rc(Ro, Co)
                c = cnt1(gro, 32) * cnt1(gco, 64)
                invc[gi, :, j * P + oi] = 1.0 / c
                for ii in range(P):
                    Ri, Ci = 2 * j + ii // 64, ii % 64
                    gri, gci = g_rc(Ri, Ci)
                    if inwin(gro, gri) and inwin(gco, gci):
                        Dm[ii, oi] = 1.0
                Dm[oi, oi] -= c
            pm[gi, 9 + j] = Dm
        for jc in range(2):
            for ic in range(2):
                Cm = np.zeros((P, P), np.float32)
                for oi in range(P):
                    po = jc * P + oi
                    go = po if s == 0 else 255 - po
                    c = cnt1(go, 256)
                    invc[gi, :, NOWN + po] = 1.0 / c
                    for ii in range(P):
                        pi = ic * P + ii
                        gin = pi if s == 0 else 255 - pi
                        if inwin(go, gin):
                            Cm[ii, oi] = 1.0
                    if ic == jc:
                        Cm[oi, oi] -= c
                pm[gi, 17 + jc * 2 + ic] = Cm
    return pm, invc


def _pp(v):
    return np.ascontiguousarray(np.asarray(v, np.float32).reshape(KC, P).T)


_CACHE = {}


def _program():
    if "F" not in _CACHE:
        p1 = build_program("F")
        _CACHE["F"] = build_program("F", plan=p1._ring_plan)
    return _CACHE["F"]


def _host_inputs(x, c, ctx, c_ctx, ada_w, ada_b, pre_g, post_g, ev_w_in, ev_pool_w, ev_pool_scale, ev_conv_w,
                 ev_conv_b, ev_w_out, od_w_in, od_onorm_g, od_w_out, lb_logits):
    f = np.float32
    x = np.asarray(x, f); ctx = np.asarray(ctx, f); c = np.asarray(c, f); c_ctx = np.asarray(c_ctx, f)
    ada_w = np.asarray(ada_w, f); ada_b = np.asarray(ada_b, f)
    adaw = np.ascontiguousarray(ada_w.reshape(2, D, 12, 512).transpose(0, 2, 1, 3))
    adabT = np.zeros((2, P, 64), f)
    for l in range(2):
        a = ada_b[l, :4096].reshape(32, P).T
        adabT[l, :, 0::2] = a
        adabT[l, :, 1::2] = a
    adabg = np.ascontiguousarray(np.broadcast_to(ada_b[:, None, 4096:], (2, P, D)))
    postg = np.ascontiguousarray(np.broadcast_to(np.asarray(post_g, f)[:, None, :], (2, P, D)))
    pregT = np.stack([_pp(pre_g[0]), _pp(pre_g[1])])
    wi = np.asarray(ev_w_in, f)[0]
    blocks = []
    for gi in range(4):
        blocks.append(np.concatenate([wi[:, gi * 256:(gi + 1) * 256], wi[:, 1024 + gi * 256:1024 + (gi + 1) * 256]], 1))
    for cb in range(8):
        blocks.append(np.concatenate([wi[:, 2048 + k * 1024 + cb * P: 2048 + k * 1024 + (cb + 1) * P] for k in range(4)], 1))
    w0blk = np.ascontiguousarray(np.stack(blocks))
    w1 = np.asarray(od_w_in, f)[0]

    def w1blk(s):
        zf, zb = (0, 2048) if s == 0 else (2048, 0)
        bl = []
        for hd in range(16):
            sl = slice(hd * P, (hd + 1) * P)
            bl.append(np.concatenate([w1[:, zf:zf + 2048][:, sl], w1[:, zb:zb + 2048][:, sl], w1[:, 6144:8192][:, sl],
                                      w1[:, 4096:6144][:, sl], w1[:, 8192:10240][:, sl]], 1))
        return np.ascontiguousarray(np.stack(bl))
    w1b = [w1blk(0), w1blk(1)]
    pc = [_pool_consts(0), _pool_consts(1)]
    pscT = np.ascontiguousarray(np.asarray(ev_pool_scale, f)[0].reshape(8, P).T)
    cw = np.asarray(ev_conv_w, f)[0]
    cbT = np.ascontiguousarray(np.asarray(ev_conv_b, f)[0].reshape(8, P).T)
    ongT = _pp(np.asarray(od_onorm_g, f)[0])
    lbl = np.asarray(lb_logits, f)
    idf = np.eye(P, dtype=f)
    m1 = np.triu(np.ones((P, P), f))
    m2 = np.ascontiguousarray(m1.T)
    wo0 = np.ascontiguousarray(np.asarray(ev_w_out, f)[0])
    wo1 = np.ascontiguousarray(np.asarray(od_w_out, f)[0])
    poolw = np.ascontiguousarray(np.asarray(ev_pool_w, f)[0])
    maps = []
    for core in range(8):
        b, s = core // 2, core % 2
        xb = x[b] if s == 0 else x[b, ::-1]
        cxb = ctx[b] if s == 0 else ctx[b, ::-1]
        cs2 = np.stack([_pp(c[b]), _pp(c_ctx)], -1)
        cbc = np.stack([np.broadcast_to(_pp(c[b])[:, :, None], (P, KC, P)), np.broadcast_to(_pp(c_ctx)[:, :, None], (P, KC, P))])
        cwl = cw if s == 0 else cw[::-1]
        cwT = np.ascontiguousarray(cwl.reshape(3, 8, P).transpose(2, 1, 0))
        lb = lbl if s == 0 else lbl[::-1]
        lblT = np.ascontiguousarray(lb.reshape(2, 3, 16, P).transpose(3, 0, 1, 2))
        maps.append({
            "x_loc": np.ascontiguousarray(xb[:NOWN + NHALO]), "ctx_loc": np.ascontiguousarray(cxb),
            "cs2": np.ascontiguousarray(cs2), "cbc": np.ascontiguousarray(cbc), "adaw": adaw, "adabT": adabT, "adabg": adabg,
            "postg": postg, "pregT": pregT, "w0blk": w0blk, "wo0": wo0, "w1blk": w1b[s], "wo1": wo1, "poolw": poolw,
            "pm": pc[s][0], "invc": pc[s][1], "pscT": pscT, "cwT": cwT, "cbT": cbT, "lbl": lblT, "ongT": ongT,
            "idf": idf, "mask1": m1, "mask2": m2,
            "psel": np.ascontiguousarray(np.broadcast_to(np.array([1.0, 0.0] if s == 1 else [0.0, 1.0], f), (P, 2))),
        })
    return maps


def kernel(**inputs):
    ncF = _program()
    maps = _host_inputs(**inputs)
    resB = run_bass_kernel_spmd(ncF, maps, core_ids=list(range(8)))
    out = np.empty((4, 2048, D), np.float32)
    for core in range(8):
        b, s = core // 2, core % 2
        y = np.asarray(resB.results[core]["out"], np.float32)
        if s == 0:
            out[b, :NOWN] = y
        else:
            out[b, NOWN:] = y[::-1]
    return out
```

```python
import numpy as np
from contextlib import ExitStack
import concourse.bass as bass
import concourse.mybir as mybir
from concourse.bass_utils import run_bass_kernel_spmd

F32 = mybir.dt.float32
BF16 = mybir.dt.bfloat16
AF = mybir.ActivationFunctionType
ALU = mybir.AluOpType
AX = mybir.AxisListType

P = 128
D = 2048
KC = 16
NOWN = 1024
NHALO = 512
NCTX = 256
EPS = 1e-6
WINS = (2, 4, 8, 16)
RAD = (1, 1, 2, 4)
NPMB = 21
SB_BASE = 16640
SB_END = 229376 - 64


DBG = {"nheads": 16, "hstage": 9}


class Buf:
    __slots__ = ("name", "w", "r", "dsem", "dcnt", "ps")

    def __init__(self, name, ps=False):
        self.name = name
        self.ps = ps
        self.w = None
        self.r = {}
        self.dsem = None
        self.dcnt = 0


class Sched:
    def __init__(self, nc, es):
        self.nc = nc
        self.es = es
        self.eng = {"pe": nc.tensor, "act": nc.scalar, "dve": nc.vector, "pool": nc.gpsimd, "sp": nc.sync}
        self.sem = {e: es.enter_context(nc.semaphore("s_" + e)) for e in self.eng}
        self.cnt = {e: 0 for e in self.eng}
        self.seen = {e: {} for e in self.eng}
        self.nsem = 0

    def _deps(self, reads, writes):
        deps = {}

        def add(t):
            if t is None:
                return
            k = id(t[0])
            if k not in deps or deps[k][1] < t[1]:
                deps[k] = t

        for b in reads:
            add(b.w)
            if b.ps:
                for t in b.r.values():
                    add(t)
        for b in writes:
            add(b.w)
            for t in b.r.values():
                add(t)
        return deps

    def _need(self, e, deps):
        for k, (sem, val) in deps.items():
            if self.seen[e].get(k, 0) >= val:
                continue
            self.eng[e].wait_ge(sem, val)
            self.seen[e][k] = val

    def _mark(self, t, reads, writes):
        k = id(t[0])
        for b in reads:
            if k not in b.r or b.r[k][1] < t[1]:
                b.r[k] = t
        for b in writes:
            b.w = t
            b.r = {}

    def op(self, e, fn, reads=(), writes=()):
        deps = self._deps(reads, writes)
        if e == "pe":
            deps.pop(id(self.sem["pe"]), None)
        self._need(e, deps)
        ins = fn(self.eng[e])
        self.cnt[e] += 1
        ins.then_inc(self.sem[e], 1)
        t = (self.sem[e], self.cnt[e])
        self._mark(t, reads, writes)
        return t

    def dma(self, q, out, in_, dbuf, reads=(), writes=()):
        if dbuf.dsem is None:
            dbuf.dsem = self.es.enter_context(self.nc.semaphore("d%d" % self.nsem))
            self.nsem += 1
        self._need(q, self._deps(reads, writes))
        ins = self.eng[q].dma_start(out=out, in_=in_)
        dbuf.dcnt += 16
        ins.then_inc(dbuf.dsem, 16)
        t = (dbuf.dsem, dbuf.dcnt)
        self._mark(t, reads, writes)
        return t

    def barrier(self):
        cur = {id(self.sem[e]): (self.sem[e], self.cnt[e]) for e in self.eng if self.cnt[e] > 0}
        for e in self.eng:
            d = dict(cur)
            d.pop(id(self.sem[e]), None)
            self._need(e, d)

    def wait_buf(self, e, b):
        self._need(e, self._deps([], [b]))


class Arena:
    def __init__(self, nc):
        self.nc = nc
        self.n = 0

    def at(self, off, shape, dt, name=None):
        nb = int(np.prod(shape[1:])) * (2 if dt == BF16 else 4)
        assert off % 32 == 0, off
        assert SB_BASE <= off and off + nb <= SB_END, (name, off, nb)
        self.n += 1
        return self.nc.alloc_sbuf_tensor_at("%s_%d" % (name or "t", self.n), list(shape), dt, offset=off)


def build_program(mode, stop=None, plan=None):
    nc = bass.Bass("TRN2", target_bir_lowering=False)
    full = mode in ("B", "F")

    def din(name, shape, dt=F32):
        return nc.dram_tensor(name, list(shape), dt, kind="ExternalInput").ap()

    x_d = din("x_loc", [NOWN + NHALO, D])
    ctx_d = din("ctx_loc", [NCTX, D])
    cs2_d = din("cs2", [P, KC, 2])
    cbc_d = din("cbc", [2, P, KC, P])
    adaw_d = din("adaw", [2, 12, D, 512])
    adabT_d = din("adabT", [2, P, 64])
    adabg_d = din("adabg", [2, P, D])
    postg_d = din("postg", [2, P, D])
    pregT_d = din("pregT", [2, P, KC])
    w0_d = din("w0blk", [12, D, 512])
    wo0_d = din("wo0", [D, D])
    w1_d = din("w1blk", [16, D, 640])
    wo1_d = din("wo1", [D, D])
    poolw_d = din("poolw", [4, 256, 256])
    pm_d = din("pm", [4, NPMB, P, P])
    invc_d = din("invc", [4, P, NOWN + NCTX])
    pscT_d = din("pscT", [P, 8])
    cwT_d = din("cwT", [P, 8, 3])
    cbT_d = din("cbT", [P, 8])
    lbl_d = din("lbl", [P, 2, 3, 16])
    ongT_d = din("ongT", [P, KC])
    idf_d = din("idf", [P, P])
    m1_d = din("mask1", [P, P])
    m2_d = din("mask2", [P, P])
    if mode == "B":
        sin_d = din("sin", [16, P, P])
    if mode == "F":
        psel_d = din("psel", [P, 2])
        agin = [nc.dram_tensor("agin%d" % i, [P, P], F32) for i in range(16)]
        agout = [nc.dram_tensor("agout%d" % i, [2 * P, P], F32) for i in range(16)]
    if full:
        out_d = nc.dram_tensor("out", [NOWN, D], F32, kind="ExternalOutput").ap()
    else:
        sout_d = nc.dram_tensor("sout", [16, P, P], F32, kind="ExternalOutput").ap()
    x1_d = nc.dram_tensor("x1_scr", [NOWN, D], F32, **({"kind": "ExternalOutput"} if stop else {})).ap()
    dbg = {}
    if stop:
        dbg["hT"] = nc.dram_tensor("dbg_hT", [P, KC, 1792], BF16, kind="ExternalOutput").ap()
        dbg["midT"] = nc.dram_tensor("dbg_midT", [P, KC, 1280], BF16, kind="ExternalOutput").ap()
        dbg["h1T"] = nc.dram_tensor("dbg_h1T", [P, KC, 1280], BF16, kind="ExternalOutput").ap()
        dbg["ogT"] = nc.dram_tensor("dbg_ogT", [P, KC, 1024], BF16, kind="ExternalOutput").ap()
        dbg["modv"] = nc.dram_tensor("dbg_modv", [8, P, KC], F32, kind="ExternalOutput").ap()
        dbg["lbv"] = nc.dram_tensor("dbg_lbv", [P, 16, 4], F32, kind="ExternalOutput").ap()
    DBGB = Buf("dbg")

    def finish():
        S.barrier()
        for b_ in (OUTB, X1B, DBGB):
            S.wait_buf("sp", b_)
        try:
            S._need("sp", dict(FIN_TICKS))
        except NameError:
            pass
        es.close()
        nc._ring_plan = ring.rec
        return nc


    es = ExitStack()
    S = Sched(nc, es)
    A = Arena(nc)
    KB = 1024

    pst = es.enter_context(nc.psum_tensor("pst", [P, 8, 512], F32))
    psflat = pst[:].rearrange("p a b -> p (a b)")
    psbf = psflat.bitcast(BF16)
    PB = [Buf("ps%d" % i, ps=True) for i in range(8)]

    def bank(i, n=512, off=0):
        return psflat[:, i * 512 + off: i * 512 + off + n]

    R_RING = SB_BASE
    R_H = R_RING + 64 * KB
    R_M = R_H + 56 * KB
    R_C = R_M + 40 * KB
    o = R_C

    def calloc(shape, dt, name):
        nonlocal o
        t = A.at(o, shape, dt, name)
        nb = int(np.prod(shape[1:])) * (2 if dt == BF16 else 4)
        o += (nb + 31) // 32 * 32
        return t

    idf = calloc([P, P], F32, "idf")
    idb = calloc([P, P], BF16, "idb")
    mk1 = calloc([P, P], BF16, "mk1")
    mk2 = calloc([P, P], BF16, "mk2")
    rmask = calloc([P, 1280], F32, "rmask")
    s2b = calloc([P, KC, 2], BF16, "s2b")
    lbv = calloc([P, 16, 4], F32, "lbv")
    nlbv = calloc([P, 16, 2], F32, "nlbv")
    ongT = calloc([P, KC], F32, "ongT")
    pscT = calloc([P, 8], F32, "pscT")
    cwT = calloc([P, 8, 3], F32, "cwT")
    cbT = calloc([P, 8], F32, "cbT")
    modv = [[calloc([P, KC], F32, "modv") for _ in range(4)] for _ in range(2)]
    poolw = calloc([P, 4, 2, 256], BF16, "poolw")
    epsc = calloc([P, 1], F32, "epsc")
    psel = calloc([P, 2], F32, "psel")
    R_W = (o + 63) // 64 * 64
    CONST = Buf("const")
    assert R_W - R_C <= 20 * KB, R_W - R_C

    hT = A.at(R_H, [P, KC, 1792], BF16, "hT")
    h1T = A.at(R_H, [P, KC, 1280], BF16, "h1T")
    midT = A.at(R_M, [P, KC, 1280], BF16, "midT")
    ogT = A.at(R_M, [P, KC, 1024], BF16, "ogT")
    HT = Buf("hT")
    MT = Buf("midT")

    def wsrc(key):
        if key[0] == "adaw":
            return adaw_d[key[1], key[2]]
        if key[0] == "w0":
            return w0_d[key[1]]
        if key[0] == "w1":
            return w1_d[key[1]]
        wo_ = wo0_d if key[1] == 0 else wo1_d
        return wo_[:, key[2] * 512:(key[2] + 1) * 512]

    class Ring:
        def __init__(self):
            self.slots = []
            self.i = 0
            self.epoch = 0
            self.n = 0
            self.issued = 0
            self.rec = []
            self.inflight = {}

        def config(self, n, cols):
            nb = KC * cols * 2
            assert n * nb <= 64 * KB
            self.slots = [(A.at(R_RING + k * nb, [P, KC, cols], BF16, "ring"), Buf("ring%d" % k)) for k in range(n)]
            self.i = 0
            self.epoch += 1

        def _issue(self, src2d, cols):
            t, b = self.slots[self.i % len(self.slots)]
            self.i += 1
            v = src2d.rearrange("(kc p) n -> p kc n", p=P)
            S.dma("pool", t[:, :, 0:cols], v, b, writes=[b])
            return t, b

        def load(self, key, cols, hold=1):
            src2d = wsrc(key)
            self.rec.append((key, cols, self.epoch, hold))
            n = self.n
            self.n += 1
            if plan is None:
                return self._issue(src2d, cols)
            if n >= self.issued:
                self.inflight[n] = self._issue(src2d, cols)
                self.issued = n + 1
            res = self.inflight.pop(n)
            depth = len(self.slots) - max(hold, 1)
            while self.issued < len(plan) and self.issued <= n + depth and plan[self.issued][2] == self.epoch:
                key_n, cols_n, _, _ = plan[self.issued]
                self.inflight[self.issued] = self._issue(wsrc(key_n), cols_n)
                self.issued += 1
            return res

    ring = Ring()

    def cload(dst, src, q="pool"):
        S.dma(q, dst, src, CONST, writes=[CONST])

    cload(idf[:], idf_d)
    cload(idb[:], idf_d, "pool")
    cload(mk1[:], m1_d, "pool")
    cload(mk2[:], m2_d, "pool")
    cload(ongT[:], ongT_d)
    cload(pscT[:], pscT_d)
    cload(cwT[:], cwT_d)
    cload(cbT[:], cbT_d)
    if mode == "F":
        cload(psel[:], psel_d)
    cload(poolw[:], poolw_d.rearrange("g (c p) o -> p g c o", p=P), "pool")
    S.op("dve", lambda e: e.memset(rmask[:], 1.0), writes=[CONST])
    S.op("dve", lambda e: e.memset(rmask[:].rearrange("p (a b) -> p a b", b=P)[:, :, 0:1], 0.0), writes=[CONST])
    S.op("dve", lambda e: e.memset(epsc[:], EPS), writes=[CONST])

    if mode == "F" and DBG.get("cc_early"):
        tt = A.at(R_M, [P, P], F32, "cctest")
        TTB = Buf("cctest")
        S.dma("sp", tt[:], idf_d, TTB, writes=[TTB])
        AG0 = Buf("ag0")
        S.dma("sp", agin[15][:, :], tt[:], TTB, reads=[TTB], writes=[AG0])
        S._need("pool", S._deps([AG0], []))
        cc0 = es.enter_context(nc.semaphore("cc_early"))
        nc.gpsimd.collective_compute("AllGather", ALU.bypass, replica_groups=[[2 * i_, 2 * i_ + 1] for i_ in range(DBG.get("ncores", 8) // 2)],
                                     ins=[agin[15].ap().opt()], outs=[agout[15].ap().opt()]).then_inc(cc0, 1)
        nc.sync.wait_ge(cc0, 1)
    def wtile(off, shape, dt, name):
        return A.at(R_W + off, shape, dt, name)

    W_LIMIT = SB_END - R_W

    cs2f = wtile(0, [P, KC, 2], F32, "cs2f")
    WK = Buf("wk")
    S.dma("sp", cs2f[:], cs2_d, WK, writes=[WK])
    S.op("act", lambda e: e.activation(out=s2b[:], in_=cs2f[:], func=AF.Silu), reads=[WK], writes=[CONST])

    lbl = wtile(256, [P, 2, 3, 16], F32, "lbl")
    lbe = wtile(1024, [P, 2, 3, 16], F32, "lbe")
    lbs = wtile(2048, [P, 2, 16], F32, "lbs")
    lbn = wtile(2560, [P, 2, 16], F32, "lbn")
    WK2 = Buf("wk2")
    S.dma("sp", lbl[:], lbl_d, WK2, writes=[WK2])
    S.op("act", lambda e: e.activation(out=lbe[:], in_=lbl[:], func=AF.Exp), reads=[WK2], writes=[WK2])
    S.op("dve", lambda e: e.tensor_tensor(out=lbn[:], in0=lbe[:, :, 0, :], in1=lbe[:, :, 1, :], op=ALU.add), reads=[WK2], writes=[WK2])
    S.op("dve", lambda e: e.tensor_tensor(out=lbs[:], in0=lbn[:], in1=lbe[:, :, 2, :], op=ALU.add), reads=[WK2], writes=[WK2])
    S.op("dve", lambda e: e.reciprocal(out=lbs[:], in_=lbs[:]), reads=[WK2], writes=[WK2])
    for d_ in range(2):
        S.op("dve", lambda e, d_=d_: e.tensor_tensor(out=lbv[:, :, 2 * d_ + 1], in0=lbn[:, d_, :], in1=lbs[:, d_, :], op=ALU.mult), reads=[WK2], writes=[CONST])
        S.op("dve", lambda e, d_=d_: e.tensor_tensor(out=lbv[:, :, 2 * d_], in0=lbe[:, d_, 2, :], in1=lbs[:, d_, :], op=ALU.mult), reads=[WK2], writes=[CONST])
        S.op("dve", lambda e, d_=d_: e.tensor_scalar(out=nlbv[:, :, d_], in0=lbv[:, :, 2 * d_], scalar1=-1.0, scalar2=None, op0=ALU.mult), reads=[CONST], writes=[CONST])

    ring.config(4, 512)
    WADA = Buf("wada")

    def ada_part1(layer):
        pb = PB[7]
        adab = wtile(30 * KB, [P, 64], F32, "adab")
        pg = wtile(30 * KB + 512, [P, KC], F32, "pg")
        mt = wtile(30 * KB + 1024, [P, 32, 2], F32, "mt")
        WA = WADA
        S.dma("sp", adab[:], adabT_d[layer], WA, writes=[WA])
        S.dma("sp", pg[:], pregT_d[layer], WA, writes=[WA])
        for j in range(8):
            wt, wb = ring.load(("adaw", layer, j), 512)

            def f(e, j=j, wt=wt):
                ins = None
                for fb in range(4):
                    for kc in range(KC):
                        ins = e.matmul(bank(7, 2, (j * 4 + fb) * 2), lhsT=wt[:, kc, fb * P:(fb + 1) * P], rhs=s2b[:, kc, :],
                                       start=(kc == 0), stop=(kc == KC - 1))
                return ins
            S.op("pe", f, reads=[wb, CONST], writes=[pb])
        S.op("dve", lambda e: e.tensor_tensor(out=mt[:].rearrange("p a b -> p (a b)"), in0=bank(7, 64), in1=adab[:], op=ALU.add),
             reads=[pb, WA], writes=[WA])
        mv = modv[layer]
        for v in range(2):
            S.op("dve", lambda e, v=v: e.tensor_copy(out=mv[2 * v][:], in_=mt[:, 0:16, v]), reads=[WA], writes=[CONST])
            S.op("dve", lambda e, v=v: e.scalar_tensor_tensor(out=mv[2 * v + 1][:], in0=mt[:, 16:32, v], scalar=1.0, in1=pg[:],
                                                           op0=ALU.add, op1=ALU.mult), reads=[WA], writes=[CONST])

    def ada_gate2(layer, variants):
        WG = Buf("wg")
        cbf = wtile(24 * KB, [P, KC, P], F32, "cbf")
        pgt = wtile(16 * KB, [P, D], F32, "pgt")
        bp = wtile(24 * KB, [P, D], F32, "bp")
        cbbs = []
        for vi, (which, gp_, gpb_) in enumerate(variants):
            cbb = A.at(R_H + 48 * KB + vi * 4 * KB, [P, KC, P], BF16, "cbb")
            S.dma("sp", cbf[:], cbc_d[which], WG, writes=[WG])
            S.op("act", lambda e, cbb=cbb: e.activation(out=cbb[:], in_=cbf[:], func=AF.Silu), reads=[WG], writes=[WG])
            cbbs.append(cbb)
        S.dma("sp", pgt[:], postg_d[layer], WG, writes=[WG])
        S.dma("sp", bp[:], adabg_d[layer], WG, writes=[WG])
        S.op("dve", lambda e: e.tensor_tensor(out=bp[:], in0=bp[:], in1=pgt[:], op=ALU.mult), reads=[WG], writes=[WG])
        for j in range(4):
            wt, wb = ring.load(("adaw", layer, 8 + j), 512)
            for vi, (which, gp_, gpb_) in enumerate(variants):
                pbi = 2 * vi + (j % 2)

                def f(e, wt=wt, pbi=pbi, cbb=cbbs[vi]):
                    ins = None
                    for kc in range(KC):
                        ins = e.matmul(bank(pbi), lhsT=cbb[:, kc, :], rhs=wt[:, kc, :], start=(kc == 0), stop=(kc == KC - 1))
                    return ins
                S.op("pe", f, reads=[wb, WG], writes=[PB[pbi]])
                S.op("dve", lambda e, j=j, pbi=pbi, gp_=gp_: e.tensor_tensor(out=gp_[:, j * 512:(j + 1) * 512], in0=bank(pbi), in1=pgt[:, j * 512:(j + 1) * 512], op=ALU.mult),
                     reads=[PB[pbi], WG], writes=[gpb_])
                S.op("dve", lambda e, j=j, gp_=gp_: e.tensor_tensor(out=gp_[:, j * 512:(j + 1) * 512], in0=gp_[:, j * 512:(j + 1) * 512], in1=bp[:, j * 512:(j + 1) * 512], op=ALU.add),
                     reads=[WG, gpb_], writes=[gpb_])

    def norm_transpose(xt, xb, junk, jb, st, stb, dstT, dcol, dbuf, gsv, shv, pbanks):
        S.op("act", lambda e: e.activation(out=junk[:], in_=xt[:], func=AF.Square, accum_out=st[:, 0:1]), reads=[xb], writes=[jb, stb])
        S.op("dve", lambda e: e.tensor_scalar(out=st[:, 1:2], in0=st[:, 0:1], scalar1=1.0 / D, scalar2=EPS, op0=ALU.mult, op1=ALU.add), reads=[stb], writes=[stb])
        S.op("act", lambda e: e.activation(out=st[:, 2:3], in_=st[:, 1:2], func=AF.Ln), reads=[stb], writes=[stb])
        S.op("act", lambda e: e.activation(out=st[:, 3:4], in_=st[:, 2:3], func=AF.Exp, scale=-0.5), reads=[stb], writes=[stb])
        S.op("dve", lambda e: e.tensor_scalar(out=xt[:], in0=xt[:], scalar1=st[:, 3:4], scalar2=None, op0=ALU.mult), reads=[stb, xb], writes=[xb])
        for g in range(4):
            pb = PB[pbanks[g]]

            def f(e, g=g):
                ins = None
                for q in range(4):
                    kc = g * 4 + q
                    ins = e.transpose(out=bank(pbanks[g], P, q * P), in_=xt[:, kc * P:(kc + 1) * P], identity=idf[:])
                return ins
            S.op("pe", f, reads=[xb, CONST], writes=[pb])
            for q in range(4):
                kc = g * 4 + q
                S.op("act", lambda e, g=g, q=q, kc=kc: e.activation(out=dstT[:, kc, dcol:dcol + P], in_=bank(pbanks[g], P, q * P), func=AF.Identity,
                                                                 scale=gsv[:, kc:kc + 1], bias=shv[:, kc:kc + 1]),
                     reads=[pb, CONST], writes=[dbuf])

    ada_part1(0)
    XO = R_M - R_W
    xts = [wtile(XO + i * 8 * KB, [P, D], F32, "xt") for i in range(2)]
    xbs = [Buf("xt0"), Buf("xt1")]
    junk = wtile(XO + 16 * KB, [P, D], BF16, "junk")
    JB = Buf("junk")
    stt = [wtile(XO + 20 * KB + i * 64, [P, 4], F32, "st") for i in range(2)]
    stb = [Buf("st0"), Buf("st1")]
    for t in range(14):
        i = t % 2
        src = x_d[t * P:(t + 1) * P, :] if t < 12 else ctx_d[(t - 12) * P:(t - 11) * P, :]
        S.dma("sp", xts[i][:], src, xbs[i], writes=[xbs[i]])
        mv = modv[0]
        gsv, shv = (mv[1], mv[0]) if t < 12 else (mv[3], mv[2])
        norm_transpose(xts[i], xbs[i], junk, JB, stt[i], stb[i], hT, t * P, HT, gsv, shv, [0, 1, 2, 3] if i == 0 else [4, 5, 6, 3])
    S.barrier()
    X1B = Buf("x1")
    OUTB = Buf("out")
    if stop == "pro":
        S.dma("sp", dbg["hT"], hT[:], DBGB, reads=[HT], writes=[DBGB])
        for l_ in range(2):
            for v_ in range(4):
                S.dma("sp", dbg["modv"][l_ * 4 + v_], modv[l_][v_][:], DBGB, reads=[CONST], writes=[DBGB])
        S.dma("sp", dbg["lbv"], lbv[:], DBGB, reads=[CONST], writes=[DBGB])
        return finish()

    TBLK = [(0, 512, 0), (512, 512, 512), (1536, 256, 1024)]
    av = wtile(0, [P, 14, 256], BF16, "av")
    AVB = Buf("av")
    pooled = wtile(7 * KB, [P, 2, 1280], BF16, "pooled")
    PLB = Buf("pooled")
    pmt = wtile(12 * KB, [P, NPMB, P], BF16, "pmt")
    PMB = Buf("pm")
    invc = wtile(12 * KB + NPMB * 256, [P, 1280], F32, "invc")
    IVB = Buf("invc")
    sag_o = 12 * KB + NPMB * 256 + 5 * KB
    sag = [wtile(sag_o + i * 2 * KB, [P, 512], F32, "sag") for i in range(2)]
    SGB = [Buf("sag0"), Buf("sag1")]
    assert sag_o + 4 * KB <= W_LIMIT, (sag_o, W_LIMIT)
    for gi in range(4):
        r = RAD[gi]
        wt, wb = ring.load(("w0", gi), 512)
        for q3 in range(3):
            S.dma("pool", pmt[:, q3 * 7:(q3 + 1) * 7, :], pm_d[gi, q3 * 7:(q3 + 1) * 7].rearrange("b p q -> p b q"), PMB, writes=[PMB])
        S.dma("sp", invc[:], invc_d[gi], IVB, writes=[IVB])
        tiles = list(range(8 + r)) + [12, 13]
        for n_, t in enumerate(tiles):
            pbi = (n_ // 2) % 4
            half = n_ % 2

            def f(e, t=t, pbi=pbi, half=half, wt=wt):
                ins = None
                for kc in range(KC):
                    ins = e.matmul(bank(pbi, 256, half * 256), lhsT=hT[:, kc, t * P:(t + 1) * P], rhs=wt[:, kc, 0:256],
                                   start=(kc == 0), stop=(kc == KC - 1))
                return ins
            S.op("pe", f, reads=[wb, HT], writes=[PB[pbi]])
            eng = "act" if n_ % 2 == 0 else "dve"
            if eng == "act":
                S.op("act", lambda e, t=t, pbi=pbi, half=half: e.activation(out=av[:, t, :], in_=bank(pbi, 256, half * 256), func=AF.Copy),
                     reads=[PB[pbi]], writes=[AVB])
            else:
                S.op("dve", lambda e, t=t, pbi=pbi, half=half: e.tensor_copy(out=av[:, t, :], in_=bank(pbi, 256, half * 256)),
                     reads=[PB[pbi]], writes=[AVB])
        for ch in range(2):
            for jg in range(2):
                pbi = 4 + (ch * 2 + jg) % 2

                def f(e, ch=ch, jg=jg, pbi=pbi):
                    ins = None
                    for jj in range(4):
                        j = jg * 4 + jj
                        rels = [rel for rel in range(-r, r + 1) if 0 <= j + rel]
                        for n2, rel in enumerate(rels):
                            blk = (9 + j) if rel == 0 else (rel + 4)
                            ins = e.matmul(bank(pbi, P, jj * P), lhsT=av[:, j + rel, ch * P:(ch + 1) * P], rhs=pmt[:, blk, :],
                                           start=(n2 == 0), stop=(n2 == len(rels) - 1))
                    return ins
                S.op("pe", f, reads=[AVB, PMB], writes=[PB[pbi]])
                S.op("dve", lambda e, ch=ch, jg=jg, pbi=pbi: e.tensor_tensor(out=pooled[:, ch, jg * 512:(jg + 1) * 512], in0=bank(pbi),
                                                                          in1=invc[:, jg * 512:(jg + 1) * 512], op=ALU.mult),
                     reads=[PB[pbi], IVB], writes=[PLB])

            def fc(e, ch=ch):
                ins = None
                for jc in range(2):
                    for ic in range(2):
                        ins = e.matmul(bank(6, P, jc * P), lhsT=av[:, 12 + ic, ch * P:(ch + 1) * P], rhs=pmt[:, 17 + jc * 2 + ic, :],
                                       start=(ic == 0), stop=(ic == 1))
                return ins
            S.op("pe", fc, reads=[AVB, PMB], writes=[PB[6]])
            S.op("dve", lambda e, ch=ch: e.tensor_tensor(out=pooled[:, ch, 1024:1280], in0=bank(6, 256), in1=invc[:, 1024:1280], op=ALU.mult),
                 reads=[PB[6], IVB], writes=[PLB])
        n3 = 0
        for oc in range(2):
            for (hc0, n, mc0) in TBLK:
                pm_, pg_ = (0, 1) if n3 % 2 == 0 else (2, 3)
                si = n3 % 2
                n3 += 1

                def fm(e, oc=oc, mc0=mc0, n=n, pm_=pm_):
                    ins = None
                    for ic in range(2):
                        ins = e.matmul(bank(pm_, n), lhsT=poolw[:, gi, ic, oc * P:(oc + 1) * P], rhs=pooled[:, ic, mc0:mc0 + n],
                                       start=(ic == 0), stop=(ic == 1))
                    return ins
                S.op("pe", fm, reads=[PLB, CONST], writes=[PB[pm_]])

                def fg(e, oc=oc, hc0=hc0, n=n, pg_=pg_, wt=wt):
                    ins = None
                    for kc in range(KC):
                        ins = e.matmul(bank(pg_, n), lhsT=wt[:, kc, 256 + oc * P:256 + (oc + 1) * P], rhs=hT[:, kc, hc0:hc0 + n],
                                       start=(kc == 0), stop=(kc == KC - 1))
                    return ins
                S.op("pe", fg, reads=[wb, HT], writes=[PB[pg_]])
                S.op("act", lambda e, n=n, pg_=pg_, si=si: e.activation(out=sag[si][:, 0:n], in_=bank(pg_, n), func=AF.Silu),
                     reads=[PB[pg_]], writes=[SGB[si]])
                S.op("dve", lambda e, oc=oc, mc0=mc0, n=n, pm_=pm_, si=si: e.scalar_tensor_tensor(
                    out=midT[:, gi * 2 + oc, mc0:mc0 + n], in0=bank(pm_, n), scalar=pscT[:, gi * 2 + oc:gi * 2 + oc + 1], in1=sag[si][:, 0:n],
                    op0=ALU.mult, op1=ALU.mult), reads=[PB[pm_], SGB[si], CONST], writes=[MT])
    S.barrier()
    ada_part1(1)

    ub = wtile(0, [P, 1026], F32, "ub")
    ubc = wtile(4128, [P, 258], F32, "ubc")
    UB = Buf("ub")
    bx = [wtile(6 * KB + i * 2 * KB, [P, 512], F32, "bx") for i in range(2)]
    BXB = [Buf("bx0"), Buf("bx1")]
    cv = [wtile(10 * KB + i * 2 * KB, [P, 512], F32, "cv") for i in range(2)]
    CVB = [Buf("cv0"), Buf("cv1")]
    sg0 = [wtile(14 * KB + i * 2 * KB, [P, 512], F32, "sg") for i in range(2)]
    SG0 = [Buf("sg0"), Buf("sg1")]
    S.op("dve", lambda e: e.memset(ub[:, 0:1], 0.0), writes=[UB])
    S.op("dve", lambda e: e.memset(ubc[:, 0:1], 0.0), writes=[UB])
    S.op("dve", lambda e: e.memset(ubc[:, 257:258], 0.0), writes=[UB])
    UBLK = [(0, 512, 1, ub), (512, 512, 513, ub), (1024, 1, 1025, ub), (1536, 256, 1, ubc)]
    for cb in range(8):
        wt, wb = ring.load(("w0", 4 + cb), 512)
        for n_, (hc0, n, uc0, ut) in enumerate(UBLK):
            px, pc = (0, 1) if n_ % 2 == 0 else (2, 3)
            si = n_ % 2

            def f(e, hc0=hc0, n=n, px=px, pc=pc, wt=wt):
                ins = None
                for (pbk, c0) in ((px, 0), (pc, 256)):
                    for kc in range(KC):
                        ins = e.matmul(bank(pbk, n), lhsT=wt[:, kc, c0:c0 + P], rhs=hT[:, kc, hc0:hc0 + n], start=(kc == 0), stop=(kc == KC - 1))
                return ins
            S.op("pe", f, reads=[wb, HT], writes=[PB[px], PB[pc]])
            S.op("act", lambda e, n=n, px=px, si=si: e.activation(out=bx[si][:, 0:n], in_=bank(px, n), func=AF.Copy), reads=[PB[px]], writes=[BXB[si]])
            S.op("dve", lambda e, n=n, pc=pc, si=si, uc0=uc0, ut=ut: e.tensor_tensor(out=ut[:, uc0:uc0 + n], in0=bank(pc, n), in1=bx[si][:, 0:n], op=ALU.mult),
                 reads=[PB[pc], BXB[si]], writes=[UB])
        for n_, (hc0, n, mc0) in enumerate(TBLK):
            pbb, pgg = (4, 5) if n_ % 2 == 0 else (6, 7)
            si = n_ % 2
            ut, u0 = (ub, hc0) if n_ < 2 else (ubc, 0)

            def f(e, hc0=hc0, n=n, pbb=pbb, pgg=pgg, wt=wt):
                ins = None
                for (pbk, c0) in ((pbb, 128), (pgg, 384)):
                    for kc in range(KC):
                        ins = e.matmul(bank(pbk, n), lhsT=wt[:, kc, c0:c0 + P], rhs=hT[:, kc, hc0:hc0 + n], start=(kc == 0), stop=(kc == KC - 1))
                return ins
            S.op("pe", f, reads=[wb, HT], writes=[PB[pbb], PB[pgg]])
            S.op("act", lambda e, n=n, pgg=pgg, si=si: e.activation(out=sg0[si][:, 0:n], in_=bank(pgg, n), func=AF.Silu), reads=[PB[pgg]], writes=[SG0[si]])
            S.op("dve", lambda e, n=n, si=si, ut=ut, u0=u0: e.tensor_scalar(out=cv[si][:, 0:n], in0=ut[:, u0 + 1:u0 + 1 + n], scalar1=cwT[:, cb, 1:2], scalar2=cbT[:, cb:cb + 1],
                                                                      op0=ALU.mult, op1=ALU.add), reads=[UB, CONST], writes=[CVB[si]])
            S.op("dve", lambda e, n=n, si=si, ut=ut, u0=u0: e.scalar_tensor_tensor(out=cv[si][:, 0:n], in0=ut[:, u0:u0 + n], scalar=cwT[:, cb, 0:1], in1=cv[si][:, 0:n],
                                                                             op0=ALU.mult, op1=ALU.add), reads=[UB, CONST, CVB[si]], writes=[CVB[si]])
            S.op("dve", lambda e, n=n, si=si, ut=ut, u0=u0: e.scalar_tensor_tensor(out=cv[si][:, 0:n], in0=ut[:, u0 + 2:u0 + 2 + n], scalar=cwT[:, cb, 2:3], in1=cv[si][:, 0:n],
                                                                             op0=ALU.mult, op1=ALU.add), reads=[UB, CONST, CVB[si]], writes=[CVB[si]])
            S.op("dve", lambda e, n=n, si=si, pbb=pbb: e.tensor_tensor(out=cv[si][:, 0:n], in0=bank(pbb, n), in1=cv[si][:, 0:n], op=ALU.mult),
                 reads=[PB[pbb], CVB[si]], writes=[CVB[si]])
            S.op("dve", lambda e, n=n, si=si, mc0=mc0: e.tensor_tensor(out=midT[:, 8 + cb, mc0:mc0 + n], in0=cv[si][:, 0:n], in1=sg0[si][:, 0:n], op=ALU.mult),
                 reads=[CVB[si], SG0[si]], writes=[MT])
    S.barrier()

    gp = wtile(0, [P, D], F32, "gp")
    GPB = Buf("gp")
    xt = wtile(8 * KB, [P, D], F32, "xt")
    XB = Buf("xt")
    xt2 = wtile(24 * KB, [P, D], F32, "xt2")
    XB2 = Buf("xt2")
    tmp = wtile(16 * KB, [P, D], F32, "tmp")
    TB = Buf("tmp")
    junk2 = wtile(16 * KB, [P, D], BF16, "junk2")
    JB2 = Buf("junk2")
    st2 = wtile(32 * KB, [P, 8], F32, "st2")
    SB2 = Buf("st2")
    st3 = wtile(32 * KB + 64, [P, 8], F32, "st3")
    SB3 = Buf("st3")
    XTS = [(xt, XB, st2, SB2), (xt2, XB2, st3, SB3)]
    FIN_TICKS = {}
    assert 32 * KB + 128 <= W_LIMIT

    def epilogue_tiles(layer, srcT, SRCB, wo_d, tiles, gpwhich, final):
        wts = [ring.load(("wo", layer, j), 512, hold=4) for j in range(4)]
        pend = None
        for n_, (c0, xsrc, xdst, nxt, dcol, gp, GPB) in enumerate(tiles):
            xt_, XB_, st_, SB_ = XTS[n_ % 2]
            S.dma("sp", xt_[:], xsrc, XB_, writes=[XB_])
            for j in range(4):
                wt, wb = wts[j]

                def f(e, j=j, wt=wt, c0=c0):
                    ins = None
                    for kc in range(KC):
                        ins = e.matmul(bank(j), lhsT=srcT[:, kc, c0:c0 + P], rhs=wt[:, kc, :], start=(kc == 0), stop=(kc == KC - 1))
                    return ins
                S.op("pe", f, reads=[wb, SRCB], writes=[PB[j]])
            if pend is not None:
                pend()
                pend = None
            yps = psflat[:, 0:D]
            S.op("act", lambda e: e.activation(out=junk2[:], in_=yps, func=AF.Square, accum_out=st_[:, 0:1]), reads=PB[0:4], writes=[TB, SB_])
            S.op("dve", lambda e: e.tensor_scalar(out=st_[:, 1:2], in0=st_[:, 0:1], scalar1=1.0 / D, scalar2=EPS, op0=ALU.mult, op1=ALU.add), reads=[SB_], writes=[SB_])
            S.op("act", lambda e: e.activation(out=st_[:, 2:3], in_=st_[:, 1:2], func=AF.Ln), reads=[SB_], writes=[SB_])
            S.op("act", lambda e: e.activation(out=st_[:, 3:4], in_=st_[:, 2:3], func=AF.Exp, scale=-0.5), reads=[SB_], writes=[SB_])
            S.op("dve", lambda e: e.scalar_tensor_tensor(out=tmp[:], in0=yps, scalar=st_[:, 3:4], in1=gp[:], op0=ALU.mult, op1=ALU.mult),
                 reads=PB[0:4] + [SB_, GPB], writes=[TB])
            S.op("dve", lambda e: e.tensor_tensor(out=xt_[:], in0=xt_[:], in1=tmp[:], op=ALU.add), reads=[TB, XB_], writes=[XB_])
            if xdst is not None:
                tk = S.dma("sp", xdst, xt_[:], XB_, reads=[XB_], writes=[X1B if not final else OUTB])
                FIN_TICKS[id(tk[0])] = tk
            if nxt is not None:
                pend = (lambda xt_=xt_, XB_=XB_, st_=st_, SB_=SB_, nxt=nxt, dcol=dcol:
                        norm_transpose(xt_, XB_, junk2, TB, st_[:, 4:8], SB_, h1T, dcol, HT, nxt[0], nxt[1], [4, 5, 6, 7]))
        if pend is not None:
            pend()

    if stop == "mid":
        S.dma("sp", dbg["midT"], midT[:], DBGB, reads=[MT], writes=[DBGB])
        return finish()
    mv1 = modv[1]
    gpc = A.at(R_H + 40 * KB, [P, D], F32, "gpc")
    GPCB = Buf("gpc")
    ada_gate2(0, [(0, gp, GPB), (1, gpc, GPCB)])
    S.barrier()
    tl = [(t * P, x_d[t * P:(t + 1) * P, :], x1_d[t * P:(t + 1) * P, :], (mv1[1], mv1[0]), t * P, gp, GPB) for t in range(8)]
    tl += [(1024 + t * P, ctx_d[t * P:(t + 1) * P, :], None, (mv1[3], mv1[2]), 1024 + t * P, gpc, GPCB) for t in range(2)]
    epilogue_tiles(0, midT, MT, wo0_d, tl, 0, False)
    S.barrier()

    if stop == "l0":
        S.dma("sp", dbg["h1T"], h1T[:], DBGB, reads=[HT], writes=[DBGB])
        return finish()
    ring.config(2, 640)
    regions = [[R_W, SB_END], [R_RING + 40 * KB, R_RING + 64 * KB], [R_H + 40 * KB, R_H + 56 * KB], [R_M + 32 * KB, R_M + 40 * KB]]

    def wa(shape, dt, name):
        nb = int(np.prod(shape[1:])) * (2 if dt == BF16 else 4)
        nb = (nb + 63) // 64 * 64
        for rg in regions:
            if rg[0] + nb <= rg[1]:
                t = A.at(rg[0], shape, dt, name)
                rg[0] += nb
                return t
        raise AssertionError("heads work region full: " + name)

    T1 = wa([P, 1280], F32, "T1")
    T2 = wa([P, 1280], F32, "T2")
    T3 = wa([P, 1280], F32, "T3")
    TB1, TB2, TB3 = Buf("T1"), Buf("T2"), Buf("T3")
    kinvT = [wa([P, 1280], BF16, "kinvT1"), wa([P, 1024], BF16, "kinvT2")]
    KIB = [Buf("kinvT1"), Buf("kinvT2")]
    qdT = [wa([P, 1024], BF16, "qdT1"), wa([P, 1024], BF16, "qdT2")]
    QDB = [Buf("qdT1"), Buf("qdT2")]
    kinv = [wa([P, 10, P], BF16, "kinv1"), wa([P, 8, P], BF16, "kinv2")]
    KTB = [Buf("kinv1"), Buf("kinv2")]
    vt = wa([P, 10, P], BF16, "vt")
    VB = Buf("vt")
    sgt = wa([P, 8, P], F32, "sgt")
    SGT = Buf("sgt")
    dec = [wa([P, 10], F32, "dec1"), wa([P, 8], F32, "dec2")]
    DCB = [Buf("dec1"), Buf("dec2")]
    scm = [wa([P, 512], BF16, "scm0"), wa([P, 512], BF16, "scm1")]
    SCB = [Buf("scm0"), Buf("scm1")]
    Sall = wa([P, 11, P], F32, "Sall")
    dsd = wa([P, 10, P], F32, "dsd")
    Sball = wa([P, 10, P], BF16, "Sball")
    vTs = wa([P, 1280], BF16, "vTs")
    VTB = Buf("vTs")
    DSB = Buf("dsd")
    SBB = Buf("Sball")
    Sg = wa([P, 2, P], F32, "Sg")
    SGX = Buf("Sg")
    STB = Buf("S")
    o1 = wa([P, 8, P], F32, "o1")
    O1B = Buf("o1")
    ot = wa([P, 8, P], F32, "ot")
    OTB = Buf("ot")
    rs = wa([P, 8, 4], F32, "rs")
    RSB = Buf("rs")
    OGB = MT
    SETS = [(kinvT, KIB, qdT, QDB, kinv, KTB, vt, VB, sgt, SGT, dec, DCB),
            ([wa([P, 1280], BF16, "kinvT1b"), wa([P, 1024], BF16, "kinvT2b")], [Buf("kinvT1b"), Buf("kinvT2b")],
             [wa([P, 1024], BF16, "qdT1b"), wa([P, 1024], BF16, "qdT2b")], [Buf("qdT1b"), Buf("qdT2b")],
             [wa([P, 10, P], BF16, "kinv1b"), wa([P, 8, P], BF16, "kinv2b")], [Buf("kinv1b"), Buf("kinv2b")],
             wa([P, 10, P], BF16, "vtb"), Buf("vtb"), wa([P, 8, P], F32, "sgtb"), Buf("sgtb"),
             [wa([P, 10], F32, "dec1b"), wa([P, 8], F32, "dec2b")], [Buf("dec1b"), Buf("dec2b")])]

    PEND = [None, 0]
    CUR = [0]

    def use(par):
        nonlocal kinvT, KIB, qdT, QDB, kinv, KTB, vt, VB, sgt, SGT, dec, DCB
        (kinvT, KIB, qdT, QDB, kinv, KTB, vt, VB, sgt, SGT, dec, DCB) = SETS[par]

    def gates(hd, d_, zps_banks, ncol):
        zv = psflat[:, zps_banks[0] * 512: zps_banks[0] * 512 + ncol]
        zb = [PB[b] for b in zps_banks]
        oml = lbv[:, hd, 2 * d_:2 * d_ + 1]
        lb = lbv[:, hd, 2 * d_ + 1:2 * d_ + 2]
        noml = nlbv[:, hd, d_:d_ + 1]
        nch = ncol // P
        S.op("act", lambda e: e.activation(out=T1[:, 0:ncol], in_=zv, func=AF.Sigmoid), reads=zb, writes=[TB1])
        S.op("act", lambda e: e.activation(out=T2[:, 0:ncol], in_=T1[:, 0:ncol], func=AF.Ln, scale=oml, bias=lb), reads=[TB1, CONST], writes=[TB2])
        S.op("dve", lambda e: e.tensor_scalar(out=T3[:, 0:ncol], in0=T1[:, 0:ncol], scalar1=noml, scalar2=oml, op0=ALU.mult, op1=ALU.add),
             reads=[TB1, CONST], writes=[TB3])
        S.op("dve", lambda e: e.tensor_tensor_scan(out=T1[:, 0:ncol], data0=rmask[:, 0:ncol], data1=T2[:, 0:ncol], initial=0.0, op0=ALU.mult, op1=ALU.add),
             reads=[TB2, CONST, TB3], writes=[TB1])
        BC, BCB, EX, EXB = T1, TB1, T2, TB2
        if d_ == 1:
            S.op("dve", lambda e: e.tensor_tensor(out=T2[:, 0:ncol], in0=T2[:, 0:ncol], in1=T1[:, 0:ncol], op=ALU.subtract), reads=[TB1, TB2], writes=[TB2])
            t1v = T1[:, 0:ncol].rearrange("p (a b) -> p a b", b=P)
            t2v = T2[:, 0:ncol].rearrange("p (a b) -> p a b", b=P)
            S.op("dve", lambda e: e.tensor_tensor(out=t2v, in0=t2v, in1=t1v[:, :, P - 1:P].broadcast_to([P, nch, P]), op=ALU.add), reads=[TB1, TB2], writes=[TB2])
            BC, BCB, EX, EXB = T2, TB2, T1, TB1
        S.op("act", lambda e: e.activation(out=EX[:, 0:ncol], in_=BC[:, 0:ncol], func=AF.Exp, scale=-1.0), reads=[BCB], writes=[EXB])
        S.op("dve", lambda e: e.tensor_tensor(out=kinvT[d_][:, 0:ncol], in0=T3[:, 0:ncol], in1=EX[:, 0:ncol], op=ALU.mult), reads=[EXB, TB3], writes=[KIB[d_]])
        S.op("act", lambda e: e.activation(out=EX[:, 0:ncol], in_=BC[:, 0:ncol], func=AF.Exp), reads=[BCB, KIB[d_]], writes=[EXB])
        e2v = EX[:, 0:ncol].rearrange("p (a b) -> p a b", b=P)
        col = P - 1 if d_ == 0 else 0
        S.op("dve", lambda e: e.tensor_copy(out=dec[d_][:, 0:nch], in_=e2v[:, :, col]), reads=[EXB], writes=[DCB[d_]])
        return EX, EXB

    def head(hd):
        wt, wb = ring.load(("w1", hd), 640)

        def proj_fm(c0, banks_, cols):
            for (pbk, off, hc0, n) in banks_:
                def f(e, pbk=pbk, off=off, hc0=hc0, n=n):
                    ins = None
                    for kc in range(KC):
                        ins = e.matmul(bank(pbk, n, off), lhsT=wt[:, kc, c0:c0 + P], rhs=h1T[:, kc, hc0:hc0 + n], start=(kc == 0), stop=(kc == KC - 1))
                    return ins
                S.op("pe", f, reads=[wb, HT], writes=[PB[pbk]])

        proj_fm(0, [(0, 0, 0, 512), (1, 0, 512, 512), (2, 0, 1024, 256)], 1280)
        yield
        EX, EXB = gates(hd, 0, [0, 1, 2], 1280)
        yield
        proj_fm(256, [(3, 0, 0, 512), (4, 0, 512, 512)], 1024)
        qv = psflat[:, 3 * 512: 3 * 512 + 1024]
        S.op("dve", lambda e: e.tensor_tensor(out=qdT[0][:], in0=qv, in1=EX[:, 0:1024], op=ALU.mult), reads=[PB[3], PB[4], EXB], writes=[QDB[0]])
        if full:
            yield
            proj_fm(128, [(0, 0, 0, 512), (1, 0, 512, 512)], 1024)
            EX2, EXB2 = gates(hd, 1, [0, 1], 1024)
            S.op("dve", lambda e: e.tensor_tensor(out=qdT[1][:], in0=qv, in1=EX2[:, 0:1024], op=ALU.mult), reads=[PB[3], PB[4], EXB2], writes=[QDB[1]])
        if DBG["hstage"] < 0.25:
            return
        yield
        proj_fm(384, [(2, 0, 0, 512), (3, 0, 512, 512), (4, 0, 1024, 256)], 1280)
        vps = psflat[:, 2 * 512: 2 * 512 + 1280]
        S.op("act", lambda e: e.activation(out=vTs[:], in_=vps, func=AF.Copy), reads=[PB[2], PB[3], PB[4]], writes=[VTB])
        yield

        def fvt(e):
            ins = None
            for t in range(10):
                ins = e.transpose(out=psbf[:, t * P:(t + 1) * P], in_=vTs[:, t * P:(t + 1) * P], identity=idb[:])
            return ins
        S.op("pe", fvt, reads=[VTB, CONST], writes=[PB[0], PB[1]])
        S.op("dve", lambda e: e.tensor_copy(out=vt[:].rearrange("p a b -> p (a b)"), in_=psbf[:, 0:1280]), reads=[PB[0], PB[1]], writes=[VB])
        if full:
            yield
            proj_fm(512, [(2, 0, 0, 512), (3, 0, 512, 512)], 1024)
            S.op("act", lambda e: e.activation(out=sgt[:].rearrange("p a b -> p (a b)"), in_=psflat[:, 2 * 512: 2 * 512 + 1024], func=AF.Silu),
                 reads=[PB[2], PB[3]], writes=[SGT])
        if DBG["hstage"] < 0.35:
            return
        for d_ in range(2 if full else 1):
            nch = 10 if d_ == 0 else 8
            for g0 in range(0, nch, 8):
                gn = min(8, nch - g0)
                pbk = 3 + d_

                def f(e, d_=d_, g0=g0, gn=gn, pbk=pbk):
                    ins = None
                    for q in range(gn):
                        c = g0 + q
                        ins = e.transpose(out=psbf[:, pbk * 1024 + q * P: pbk * 1024 + (q + 1) * P], in_=kinvT[d_][:, c * P:(c + 1) * P], identity=idb[:])
                    return ins
                S.op("pe", f, reads=[KIB[d_], CONST], writes=[PB[pbk]])
                S.op("act", lambda e, d_=d_, g0=g0, gn=gn, pbk=pbk: e.activation(
                    out=kinv[d_][:, g0:g0 + gn, :].rearrange("p a b -> p (a b)"), in_=psbf[:, pbk * 1024: pbk * 1024 + gn * P], func=AF.Copy),
                    reads=[PB[pbk]], writes=[KTB[d_]])

    def head2(hd):
        CUR[0] = hd % 2
        use(hd % 2)

        def scan(d_, order, s_init):
            mk = mk1 if d_ == 0 else mk2
            n = len(order)
            if s_init is None:
                S.op("dve", lambda e: e.memset(Sall[:, 0, :], 0.0), writes=[STB])
            else:
                s_init()
            lat = [c for c in order if c < 8]
            for g0 in range(0, n, 4):
                cs = order[g0:g0 + 4]

                def fs(e, cs=cs):
                    ins = None
                    for q, c in enumerate(cs):
                        ins = e.matmul(bank(7, P, q * P), lhsT=kinv[d_][:, c, :], rhs=vt[:, c, :], start=True, stop=True)
                    return ins
                S.op("pe", fs, reads=[KTB[d_], VB], writes=[PB[7]])
                for q, c in enumerate(cs):
                    S.op("act", lambda e, q=q, c=c, i=g0 + q: e.activation(out=dsd[:, i, :], in_=bank(7, P, q * P), func=AF.Identity, scale=dec[d_][:, c:c + 1]),
                         reads=[PB[7], DCB[d_]], writes=[DSB])
                step()
            for g in range(2):
                cs = lat[g * 4:(g + 1) * 4]
                pbk = 5 + g

                def f(e, cs=cs, pbk=pbk):
                    ins = None
                    for q, c in enumerate(cs):
                        ins = e.matmul(bank(pbk, P, q * P), lhsT=kinvT[d_][:, c * P:(c + 1) * P], rhs=qdT[d_][:, c * P:(c + 1) * P], start=True, stop=True)
                    return ins
                S.op("pe", f, reads=[KIB[d_], QDB[d_]], writes=[PB[pbk]])
                S.op("dve", lambda e, g=g, pbk=pbk: e.tensor_tensor(out=scm[g][:].rearrange("p (a b) -> p a b", b=P), in0=bank(pbk).rearrange("p (a b) -> p a b", b=P),
                                                                in1=mk[:].unsqueeze(1).broadcast_to([P, 4, P]), op=ALU.mult),
                     reads=[PB[pbk], CONST], writes=[SCB[g]])
            for i, c in enumerate(order):
                S.op("dve", lambda e, i=i, c=c: e.scalar_tensor_tensor(out=Sall[:, i + 1, :], in0=Sall[:, i, :], scalar=dec[d_][:, c:c + 1], in1=dsd[:, i, :],
                                                                   op0=ALU.mult, op1=ALU.add), reads=[STB, DSB, DCB[d_]], writes=[STB])
            S.op("act", lambda e: e.activation(out=Sball[:, 0:n, :].rearrange("p a b -> p (a b)"), in_=Sall[:, 0:n, :].rearrange("p a b -> p (a b)"), func=AF.Copy),
                 reads=[STB], writes=[SBB])
            step()
            for n_, c in enumerate(lat):
                i = order.index(c)
                g, q = divmod(n_, 4)
                pbo = 5 + (n_ % 2)

                def fo(e, c=c, g=g, q=q, pbo=pbo, i=i):
                    e.matmul(bank(pbo, P), lhsT=scm[g][:, q * P:(q + 1) * P], rhs=vt[:, c, :], start=True, stop=False)
                    return e.matmul(bank(pbo, P), lhsT=qdT[d_][:, c * P:(c + 1) * P], rhs=Sball[:, i, :], start=False, stop=True)
                S.op("pe", fo, reads=[SCB[g], VB, QDB[d_], SBB], writes=[PB[pbo]])
                if d_ == 0:
                    S.op("act", lambda e, c=c, pbo=pbo: e.activation(out=o1[:, c, :], in_=bank(pbo, P), func=AF.Copy), reads=[PB[pbo]], writes=[O1B])
                else:
                    S.op("dve", lambda e, c=c, pbo=pbo: e.tensor_tensor(out=ot[:, c, :], in0=bank(pbo, P), in1=o1[:, c, :], op=ALU.add), reads=[PB[pbo], O1B], writes=[OTB])
                step()

        if DBG["hstage"] < 2:
            return
        scan(0, [8, 9, 0, 1, 2, 3, 4, 5, 6, 7], None)
        if DBG["hstage"] < 3:
            return
        if mode == "A":
            S.dma("sp", sout_d[hd], Sall[:, 10, :], STB, reads=[STB], writes=[OUTB])
            return
        if mode == "F":
            AGB = Buf("agin")
            S.dma("sp", agin[hd][:, :], Sall[:, 10, :], STB, reads=[STB], writes=[AGB])
            S._need("pool", S._deps([AGB], []))
            ccs = es.enter_context(nc.semaphore("cc%d" % hd))
            nc.gpsimd.collective_compute("AllGather", ALU.bypass, replica_groups=[[2 * i_, 2 * i_ + 1] for i_ in range(DBG.get("ncores", 8) // 2)],
                                         ins=[agin[hd].ap().opt()], outs=[agout[hd].ap().opt()]).then_inc(ccs, 1)

            def init2():
                nc.sync.wait_ge(ccs, 1)
                S.dma("sp", Sg[:], agout[hd].ap().rearrange("(r p) v -> p r v", p=P), SGX, reads=[], writes=[SGX])
                S.op("dve", lambda e: e.tensor_scalar(out=Sall[:, 0, :], in0=Sg[:, 0, :], scalar1=psel[:, 0:1], scalar2=None, op0=ALU.mult), reads=[SGX, CONST], writes=[STB])
                S.op("dve", lambda e: e.scalar_tensor_tensor(out=Sall[:, 0, :], in0=Sg[:, 1, :], scalar=psel[:, 1:2], in1=Sall[:, 0, :], op0=ALU.mult, op1=ALU.add),
                     reads=[SGX, CONST, STB], writes=[STB])
            if DBG["hstage"] < 3.5:
                init2()
                return
            scan(1, [7, 6, 5, 4, 3, 2, 1, 0], init2)
        else:
            scan(1, [7, 6, 5, 4, 3, 2, 1, 0], lambda: S.dma("sp", Sall[:, 0, :], sin_d[hd], STB, writes=[STB]))
        if DBG["hstage"] < 4:
            return
        S.op("dve", lambda e: e.tensor_tensor(out=o1[:], in0=ot[:], in1=ot[:], op=ALU.mult), reads=[OTB, O1B], writes=[O1B])
        S.op("dve", lambda e: e.tensor_reduce(out=rs[:, :, 0], in_=o1[:], axis=AX.X, op=ALU.add), reads=[O1B], writes=[RSB])
        S.op("dve", lambda e: e.tensor_scalar(out=rs[:, :, 1], in0=rs[:, :, 0], scalar1=1.0 / P, scalar2=EPS, op0=ALU.mult, op1=ALU.add), reads=[RSB], writes=[RSB])
        S.op("act", lambda e: e.activation(out=rs[:, :, 2], in_=rs[:, :, 1], func=AF.Ln), reads=[RSB], writes=[RSB])
        S.op("act", lambda e: e.activation(out=rs[:, :, 3], in_=rs[:, :, 2], func=AF.Exp, scale=-0.5), reads=[RSB], writes=[RSB])
        S.op("dve", lambda e: e.tensor_tensor(out=ot[:], in0=ot[:], in1=rs[:, :, 3:4].broadcast_to([P, 8, P]), op=ALU.mult), reads=[RSB, OTB], writes=[OTB])
        for g in range(2):
            pbk = 5 + g

            def f(e, g=g, pbk=pbk):
                ins = None
                for q in range(4):
                    ins = e.transpose(out=bank(pbk, P, q * P), in_=ot[:, g * 4 + q, :], identity=idf[:])
                return ins
            S.op("pe", f, reads=[OTB, CONST], writes=[PB[pbk]])
            S.op("dve", lambda e, g=g, pbk=pbk: e.scalar_tensor_tensor(out=ogT[:, hd, g * 512:(g + 1) * 512], in0=bank(pbk), scalar=ongT[:, hd:hd + 1],
                                                                   in1=sgt[:].rearrange("p a b -> p (a b)")[:, g * 512:(g + 1) * 512], op0=ALU.mult, op1=ALU.mult),
                 reads=[PB[pbk], CONST, SGT], writes=[OGB])

    def step():
        if PEND[0] is not None:
            use(PEND[1])
            try:
                next(PEND[0])
            except StopIteration:
                PEND[0] = None
            use(CUR[0])

    def exhaust():
        while PEND[0] is not None:
            step()

    PEND[0], PEND[1] = head(0), 0
    exhaust()
    for hd in range(DBG["nheads"]):
        if hd + 1 < DBG["nheads"]:
            PEND[0], PEND[1] = head(hd + 1), (hd + 1) % 2
        head2(hd)
        exhaust()
    S.barrier()
    if stop == "heads":
        if not DBG.get("nodump"):
            S.dma("sp", dbg["ogT"], ogT[:], DBGB, reads=[OGB], writes=[DBGB])
        return finish()

    if full:
        ring.config(4, 512)
        ada_gate2(1, [(0, gp, GPB)])
        S.barrier()
        tl = [(t * P, x1_d[t * P:(t + 1) * P, :], out_d[t * P:(t + 1) * P, :], None, 0, gp, GPB) for t in range(8)]
        S._need("sp", dict(FIN_TICKS))
        epilogue_tiles(1, ogT, OGB, wo1_d, tl, 0, True)
    return finish()


def _pool_consts(s):
    pm = np.zeros((4, NPMB, P, P), np.float32)
    invc = np.zeros((4, P, NOWN + NCTX), np.float32)

    def g_rc(R, C):
        return (R, C) if s == 0 else (31 - R, 63 - C)

    for gi, w in enumerate(WINS):
        lo, hi = w // 2, w // 2 - 1
        r = RAD[gi]

        def inwin(go, gin):
            return go - lo <= gin <= go + hi

        def cnt1(g, n):
            return min(g + hi, n - 1) - max(g - lo, 0) + 1
        for rel in range(-r, r + 1):
            if rel == 0:
                continue
            B = np.zeros((P, P), np.float32)
            for oi in range(P):
                Ro, Co = 12 + oi // 64, oi % 64
                gro, gco = g_rc(Ro, Co)
                for ii in range(P):
                    Ri, Ci = 12 + 2 * rel + ii // 64, ii % 64
                    gri, gci = g_rc(Ri, Ci)
                    if inwin(gro, gri) and inwin(gco, gci):
                        B[ii, oi] = 1.0
            pm[gi, rel + 4] = B
        for j in range(8):
            Dm = np.zeros((P, P), np.float32)
            for oi in range(P):
                Ro, Co = 2 * j + oi // 64, oi % 64
                gro, gco = g_rc(Ro, Co)
                c = cnt1(gro, 32) * cnt1(gco, 64)
                invc[gi, :, j * P + oi] = 1.0 / c
                for ii in range(P):
                    Ri, Ci = 2 * j + ii // 64, ii % 64
                    gri, gci = g_rc(Ri, Ci)
                    if inwin(gro, gri) and inwin(gco, gci):
                        Dm[ii, oi] = 1.0
                Dm[oi, oi] -= c
            pm[gi, 9 + j] = Dm
        for jc in range(2):
            for ic in range(2):
                Cm = np.zeros((P, P), np.float32)
                for oi in range(P):
                    po = jc * P + oi
                    go = po if s == 0 else 255 - po
                    c = cnt1(go, 256)
                    invc[gi, :, NOWN + po] = 1.0 / c
                    for ii in range(P):
                        pi = ic * P + ii
                        gin = pi if s == 0 else 255 - pi
                        if inwin(go, gin):
                            Cm[ii, oi] = 1.0
                    if ic == jc:
                        Cm[oi, oi] -= c
                pm[gi, 17 + jc * 2 + ic] = Cm
    return pm, invc


def _pp(v):
    return np.ascontiguousarray(np.asarray(v, np.float32).reshape(KC, P).T)


_CACHE = {}


def _program():
    if "F" not in _CACHE:
        p1 = build_program("F")
        _CACHE["F"] = build_program("F", plan=p1._ring_plan)
    return _CACHE["F"]


def _host_inputs(x, c, ctx, c_ctx, ada_w, ada_b, pre_g, post_g, ev_w_in, ev_pool_w, ev_pool_scale, ev_conv_w,
                 ev_conv_b, ev_w_out, od_w_in, od_onorm_g, od_w_out, lb_logits):
    f = np.float32
    x = np.asarray(x, f); ctx = np.asarray(ctx, f); c = np.asarray(c, f); c_ctx = np.asarray(c_ctx, f)
    ada_w = np.asarray(ada_w, f); ada_b = np.asarray(ada_b, f)
    adaw = np.ascontiguousarray(ada_w.reshape(2, D, 12, 512).transpose(0, 2, 1, 3))
    adabT = np.zeros((2, P, 64), f)
    for l in range(2):
        a = ada_b[l, :4096].reshape(32, P).T
        adabT[l, :, 0::2] = a
        adabT[l, :, 1::2] = a
    adabg = np.ascontiguousarray(np.broadcast_to(ada_b[:, None, 4096:], (2, P, D)))
    postg = np.ascontiguousarray(np.broadcast_to(np.asarray(post_g, f)[:, None, :], (2, P, D)))
    pregT = np.stack([_pp(pre_g[0]), _pp(pre_g[1])])
    wi = np.asarray(ev_w_in, f)[0]
    blocks = []
    for gi in range(4):
        blocks.append(np.concatenate([wi[:, gi * 256:(gi + 1) * 256], wi[:, 1024 + gi * 256:1024 + (gi + 1) * 256]], 1))
    for cb in range(8):
        blocks.append(np.concatenate([wi[:, 2048 + k * 1024 + cb * P: 2048 + k * 1024 + (cb + 1) * P] for k in range(4)], 1))
    w0blk = np.ascontiguousarray(np.stack(blocks))
    w1 = np.asarray(od_w_in, f)[0]

    def w1blk(s):
        zf, zb = (0, 2048) if s == 0 else (2048, 0)
        bl = []
        for hd in range(16):
            sl = slice(hd * P, (hd + 1) * P)
            bl.append(np.concatenate([w1[:, zf:zf + 2048][:, sl], w1[:, zb:zb + 2048][:, sl], w1[:, 6144:8192][:, sl],
                                      w1[:, 4096:6144][:, sl], w1[:, 8192:10240][:, sl]], 1))
        return np.ascontiguousarray(np.stack(bl))
    w1b = [w1blk(0), w1blk(1)]
    pc = [_pool_consts(0), _pool_consts(1)]
    pscT = np.ascontiguousarray(np.asarray(ev_pool_scale, f)[0].reshape(8, P).T)
    cw = np.asarray(ev_conv_w, f)[0]
    cbT = np.ascontiguousarray(np.asarray(ev_conv_b, f)[0].reshape(8, P).T)
    ongT = _pp(np.asarray(od_onorm_g, f)[0])
    lbl = np.asarray(lb_logits, f)
    idf = np.eye(P, dtype=f)
    m1 = np.triu(np.ones((P, P), f))
    m2 = np.ascontiguousarray(m1.T)
    wo0 = np.ascontiguousarray(np.asarray(ev_w_out, f)[0])
    wo1 = np.ascontiguousarray(np.asarray(od_w_out, f)[0])
    poolw = np.ascontiguousarray(np.asarray(ev_pool_w, f)[0])
    maps = []
    for core in range(8):
        b, s = core // 2, core % 2
        xb = x[b] if s == 0 else x[b, ::-1]
        cxb = ctx[b] if s == 0 else ctx[b, ::-1]
        cs2 = np.stack([_pp(c[b]), _pp(c_ctx)], -1)
        cbc = np.stack([np.broadcast_to(_pp(c[b])[:, :, None], (P, KC, P)), np.broadcast_to(_pp(c_ctx)[:, :, None], (P, KC, P))])
        cwl = cw if s == 0 else cw[::-1]
        cwT = np.ascontiguousarray(cwl.reshape(3, 8, P).transpose(2, 1, 0))
        lb = lbl if s == 0 else lbl[::-1]
        lblT = np.ascontiguousarray(lb.reshape(2, 3, 16, P).transpose(3, 0, 1, 2))
        maps.append({
            "x_loc": np.ascontiguousarray(xb[:NOWN + NHALO]), "ctx_loc": np.ascontiguousarray(cxb),
            "cs2": np.ascontiguousarray(cs2), "cbc": np.ascontiguousarray(cbc), "adaw": adaw, "adabT": adabT, "adabg": adabg,
            "postg": postg, "pregT": pregT, "w0blk": w0blk, "wo0": wo0, "w1blk": w1b[s], "wo1": wo1, "poolw": poolw,
            "pm": pc[s][0], "invc": pc[s][1], "pscT": pscT, "cwT": cwT, "cbT": cbT, "lbl": lblT, "ongT": ongT,
            "idf": idf, "mask1": m1, "mask2": m2,
            "psel": np.ascontiguousarray(np.broadcast_to(np.array([1.0, 0.0] if s == 1 else [0.0, 1.0], f), (P, 2))),
        })
    return maps


def kernel(**inputs):
    ncF = _program()
    maps = _host_inputs(**inputs)
    resB = run_bass_kernel_spmd(ncF, maps, core_ids=list(range(8)))
    out = np.empty((4, 2048, D), np.float32)
    for core in range(8):
        b, s = core // 2, core % 2
        y = np.asarray(resB.results[core]["out"], np.float32)
        if s == 0:
            out[b, :NOWN] = y
        else:
            out[b, NOWN:] = y[::-1]
    return out
```
